# Optimizing a Trainium2 kernel written in Bass

```python
import math
import jax, jax.numpy as jnp
from jax import lax
import numpy as np

D_MODEL = 1024
BATCH = 8
SEQ = 4096
DEPTH = 2

PLE_DIM = 256
D_FF = 2816
RMS_EPS = 1e-6
N_NORMS = 8

A_GROUPS = 4
A_WIDTH = 256
A_GROUP_DIM = A_WIDTH // A_GROUPS
A_CHUNK = 128

B_GROUPS = 4
B_WIDTH = 256
B_GROUP_DIM = B_WIDTH // B_GROUPS
CONV_W = 4
LRU_C = 8.0

C_HEADS = 8
C_KV_GROUPS = 2
C_HPG = C_HEADS // C_KV_GROUPS
HEAD_DIM = 64
C_WIDTH = C_HEADS * HEAD_DIM
KV_W = C_KV_GROUPS * HEAD_DIM
CMP_LEN = 32
CMP_STRIDE = 16
CMP_HIDDEN = 128
SEL_LEN = 64
SEL_TOP = 16
WINDOW = 512
Q_BLOCK = 64
FORCE_SCORE = 1e4
NEG = -1e30

N_BUCKETS = 32
MAX_DISTANCE = 1024

D_MIX = A_WIDTH + B_WIDTH + C_WIDTH
IN_SPLITS = (A_WIDTH, A_WIDTH, B_WIDTH, B_WIDTH, C_WIDTH, KV_W, KV_W, KV_W, KV_W, KV_W, KV_W, C_HEADS, C_HEADS, C_HEADS)
N_IN = 2 * A_WIDTH + 2 * B_WIDTH + C_WIDTH + 6 * KV_W + 3 * C_HEADS

kernel_name = "hymba_style_gmlp_rglru_nsa_hybrid"


def rmsnorm(x, g):
    xf = x.astype(jnp.float32)
    y = xf * lax.rsqrt(jnp.mean(xf * xf, axis=-1, keepdims=True) + RMS_EPS)
    return (y * g.astype(jnp.float32)).astype(x.dtype)


def swiglu(x, wg, wu, wd):
    return (jax.nn.silu(x @ wg) * (x @ wu)) @ wd


def t5_bucket(dist):
    n = jnp.maximum(dist, 0)
    max_exact = N_BUCKETS // 2
    nf = jnp.maximum(n, max_exact).astype(jnp.float32)
    large = max_exact + (jnp.log(nf / max_exact) / math.log(MAX_DISTANCE / max_exact)
                         * (N_BUCKETS - max_exact)).astype(jnp.int32)
    large = jnp.minimum(large, N_BUCKETS - 1)
    return jnp.where(n < max_exact, n, large)


def spatial_gating(u, v, norm_g, w_s, b_s):
    bn, s, _ = u.shape
    nc = s // A_CHUNK
    v = rmsnorm(v, norm_g).reshape(bn, nc, A_CHUNK, A_GROUPS, A_GROUP_DIM)
    w = jnp.where(jnp.tril(jnp.ones((A_CHUNK, A_CHUNK), bool)), w_s, 0)
    mixed = jnp.einsum('gts,bcsgd->bctgd', w, v) + b_s.T[None, None, :, :, None]
    return u * mixed.reshape(bn, s, A_WIDTH)


def rg_lru_block(xb, gate, conv_w, conv_b, wa, ba, wx, bx, lam):
    bn, s, w = xb.shape
    xp = jnp.pad(xb, ((0, 0), (CONV_W - 1, 0), (0, 0)))
    xc = sum(xp[:, k:k + s] * conv_w[k] for k in range(CONV_W)) + conv_b
    xg = xc.reshape(bn, s, B_GROUPS, B_GROUP_DIM)
    r = jax.nn.sigmoid(jnp.einsum('bsgi,gij->bsgj', xg, wa).reshape(bn, s, w) + ba)
    i = jax.nn.sigmoid(jnp.einsum('bsgi,gij->bsgj', xg, wx).reshape(bn, s, w) + bx)
    log_a = -LRU_C * r.astype(jnp.float32) * jax.nn.softplus(-lam.astype(jnp.float32))
    a = jnp.exp(log_a)
    b_in = jnp.sqrt(-jnp.expm1(2.0 * log_a)) * (i * xc).astype(jnp.float32)

    def combine(left, right):
        a1, b1 = left
        a2, b2 = right
        return a1 * a2, a2 * b1 + b2

    _, hs = lax.associative_scan(combine, (a, b_in), axis=1)
    return hs.astype(xb.dtype) * jax.nn.gelu(gate)


def compress(k, pos, w1, b1, w2, b2):
    bn, s, g, d = k.shape
    n_cmp = (s - CMP_LEN) // CMP_STRIDE + 1
    idx = jnp.arange(n_cmp)[:, None] * CMP_STRIDE + jnp.arange(CMP_LEN)[None, :]
    blk = k[:, idx] + pos[None, None, :, None, :]
    blk = jnp.moveaxis(blk, 3, 2).reshape(bn, n_cmp, g, CMP_LEN * d)
    return jax.nn.gelu(blk @ w1 + b1) @ w2 + b2


def nsa(q, kc_raw, vc_raw, ks, vs, kw, vw, gc, gs, gw, rel_bias, cmp_pos, cmp_w1, cmp_b1, cmp_w2, cmp_b2):
    bn, s = q.shape[:2]
    G, R, D = C_KV_GROUPS, C_HPG, HEAD_DIM
    n_cmp = (s - CMP_LEN) // CMP_STRIDE + 1
    n_sel = s // SEL_LEN
    top = min(SEL_TOP, n_sel)
    nq = s // Q_BLOCK
    scale = HEAD_DIM ** -0.5

    kc = compress(kc_raw, cmp_pos[0], cmp_w1[0], cmp_b1[0], cmp_w2[0], cmp_b2[0])
    vc = compress(vc_raw, cmp_pos[1], cmp_w1[1], cmp_b1[1], cmp_w2[1], cmp_b2[1])
    cs = jnp.arange(n_cmp) * CMP_STRIDE
    cmp_end = cs + CMP_LEN - 1
    ss = jnp.arange(n_sel) * SEL_LEN
    overlap = jnp.clip(jnp.minimum(cs[:, None] + CMP_LEN, ss[None] + SEL_LEN)
                       - jnp.maximum(cs[:, None], ss[None]), 0, None).astype(jnp.float32) / CMP_LEN
    ks_blk = ks.reshape(bn, n_sel, SEL_LEN, G, D).transpose(0, 3, 1, 2, 4)
    vs_blk = vs.reshape(bn, n_sel, SEL_LEN, G, D).transpose(0, 3, 1, 2, 4)
    kw_pad = jnp.pad(kw, ((0, 0), (WINDOW, 0), (0, 0), (0, 0)))
    vw_pad = jnp.pad(vw, ((0, 0), (WINDOW, 0), (0, 0), (0, 0)))
    rb_heads = rel_bias.astype(jnp.float32).reshape(N_BUCKETS, G, R)
    rb_grp = rb_heads.transpose(1, 0, 2)
    bi = jnp.arange(bn)[:, None, None, None]
    gi = jnp.arange(G)[None, None, :, None]
    gi5 = jnp.arange(G)[None, None, :, None, None]
    j_sel = jnp.arange(n_sel)

    def block_fn(args):
        c, qc = args
        t = c * Q_BLOCK + jnp.arange(Q_BLOCK)
        d_c = t[:, None] - cmp_end[None]
        ok_c = d_c >= 0
        lg = (jnp.einsum('btgrd,bngd->bgrtn', qc, kc).astype(jnp.float32) * scale
              + rb_heads[t5_bucket(d_c)].transpose(2, 3, 0, 1))
        p_c = jax.nn.softmax(jnp.where(ok_c, lg, NEG), axis=-1) * ok_c
        o_c = jnp.einsum('bgrtn,bngd->btgrd', p_c.astype(vc.dtype), vc)
        imp = jnp.einsum('bgrtn,nj->btgj', p_c, overlap)
        blk_t = (t // SEL_LEN)[None, :, None, None]
        forced = (j_sel == 0) | (j_sel == blk_t) | (j_sel == blk_t - 1)
        score = jnp.where(j_sel <= blk_t, jnp.where(forced, FORCE_SCORE, imp), -1.0)
        top_v, top_i = lax.top_k(score, top)
        kg = ks_blk[bi, gi, top_i]
        vg = vs_blk[bi, gi, top_i]
        tok = top_i[..., None] * SEL_LEN + jnp.arange(SEL_LEN)
        d_s = t[None, :, None, None, None] - tok
        ok_s = (d_s >= 0) & (top_v >= 0.0)[..., None]
        bias_s = jnp.moveaxis(rb_grp[gi5, t5_bucket(d_s)], -1, 3)
        lg = jnp.einsum('btgrd,btgkld->btgrkl', qc, kg).astype(jnp.float32) * scale + bias_s
        lg = jnp.where(ok_s[:, :, :, None], lg, NEG)
        p_s = jax.nn.softmax(lg.reshape(bn, Q_BLOCK, G, R, top * SEL_LEN), axis=-1).reshape(lg.shape)
        o_s = jnp.einsum('btgrkl,btgkld->btgrd', p_s.astype(vg.dtype), vg)
        kwc = lax.dynamic_slice_in_dim(kw_pad, c * Q_BLOCK, Q_BLOCK + WINDOW, axis=1)
        vwc = lax.dynamic_slice_in_dim(vw_pad, c * Q_BLOCK, Q_BLOCK + WINDOW, axis=1)
        spos = c * Q_BLOCK - WINDOW + jnp.arange(Q_BLOCK + WINDOW)
        d_w = t[:, None] - spos[None]
        ok_w = (d_w >= 0) & (d_w < WINDOW) & (spos[None] >= 0)
        lg = (jnp.einsum('btgrd,bsgd->bgrts', qc, kwc).astype(jnp.float32) * scale
              + rb_heads[t5_bucket(d_w)].transpose(2, 3, 0, 1))
        p_w = jax.nn.softmax(jnp.where(ok_w, lg, NEG), axis=-1)
        o_w = jnp.einsum('bgrts,bsgd->btgrd', p_w.astype(vwc.dtype), vwc)
        return o_c, o_s, o_w

    q_blocks = jnp.moveaxis(q.reshape(bn, nq, Q_BLOCK, G, R, D), 1, 0)
    o_c, o_s, o_w = lax.map(block_fn, (jnp.arange(nq), q_blocks))

    def unblock(o):
        return jnp.moveaxis(o, 0, 1).reshape(bn, s, C_HEADS, D)

    out = (jax.nn.sigmoid(gc)[..., None] * unblock(o_c)
           + jax.nn.sigmoid(gs)[..., None] * unblock(o_s)
           + jax.nn.sigmoid(gw)[..., None] * unblock(o_w))
    return out.reshape(bn, s, C_WIDTH)


def setup_inputs(seed: int = 0) -> dict:
    key = jax.random.key(seed)
    ks = jax.random.split(key, 26)
    L = DEPTH

    def nrm(k, shape, scale):
        return jax.random.normal(k, shape, jnp.float32) * scale

    lam_u = jax.random.uniform(ks[17], (L, B_WIDTH), jnp.float32, 0.9, 0.999)
    lam_s = lam_u ** (1.0 / LRU_C)
    return {
        "x": nrm(ks[0], (BATCH, SEQ, D_MODEL), 1.0),
        "p": nrm(ks[1], (L, BATCH, SEQ, PLE_DIM), 1.0),
        "rel_bias": nrm(ks[2], (N_BUCKETS, C_HEADS), 0.2),
        "norm_g": 1.0 + nrm(ks[3], (L, N_NORMS, D_MODEL), 0.05),
        "ffn_w_gate": nrm(ks[4], (L, 2, D_MODEL, D_FF), D_MODEL ** -0.5),
        "ffn_w_up": nrm(ks[5], (L, 2, D_MODEL, D_FF), D_MODEL ** -0.5),
        "ffn_w_down": nrm(ks[6], (L, 2, D_FF, D_MODEL), D_FF ** -0.5),
        "w_in": nrm(ks[7], (L, D_MODEL, N_IN), D_MODEL ** -0.5),
        "w_out": nrm(ks[8], (L, D_MIX, D_MODEL), D_MIX ** -0.5),
        "sgu_norm_g": 1.0 + nrm(ks[9], (L, A_WIDTH), 0.05),
        "sgu_w": nrm(ks[10], (L, A_GROUPS, A_CHUNK, A_CHUNK), A_CHUNK ** -0.5),
        "sgu_b": 1.0 + nrm(ks[11], (L, A_GROUPS, A_CHUNK), 0.1),
        "conv_w": nrm(ks[12], (L, CONV_W, B_WIDTH), CONV_W ** -0.5),
        "conv_b": nrm(ks[13], (L, B_WIDTH), 0.01),
        "lru_wa": nrm(ks[14], (L, B_GROUPS, B_GROUP_DIM, B_GROUP_DIM), B_GROUP_DIM ** -0.5),
        "lru_ba": nrm(ks[15], (L, B_WIDTH), 0.01),
        "lru_wx": nrm(ks[16], (L, B_GROUPS, B_GROUP_DIM, B_GROUP_DIM), B_GROUP_DIM ** -0.5),
        "lru_bx": nrm(ks[18], (L, B_WIDTH), 0.01),
        "lru_lambda": jnp.log(lam_s) - jnp.log1p(-lam_s),
        "cmp_pos": nrm(ks[19], (L, 2, CMP_LEN, HEAD_DIM), 0.02),
        "cmp_w1": nrm(ks[20], (L, 2, CMP_LEN * HEAD_DIM, CMP_HIDDEN), (CMP_LEN * HEAD_DIM) ** -0.5),
        "cmp_b1": nrm(ks[21], (L, 2, CMP_HIDDEN), 0.01),
        "cmp_w2": nrm(ks[22], (L, 2, CMP_HIDDEN, HEAD_DIM), CMP_HIDDEN ** -0.5),
        "cmp_b2": nrm(ks[23], (L, 2, HEAD_DIM), 0.01),
        "ple_w_gate": nrm(ks[24], (L, D_MODEL, D_MODEL), D_MODEL ** -0.5),
        "ple_w_proj": nrm(ks[25], (L, PLE_DIM, D_MODEL), PLE_DIM ** -0.5),
    }


def reference(x, p, rel_bias, norm_g, ffn_w_gate, ffn_w_up, ffn_w_down, w_in, w_out,
              sgu_norm_g, sgu_w, sgu_b, conv_w, conv_b, lru_wa, lru_ba, lru_wx, lru_bx, lru_lambda,
              cmp_pos, cmp_w1, cmp_b1, cmp_w2, cmp_b2, ple_w_gate, ple_w_proj):
    bn, s, _ = x.shape
    split_points = np.cumsum(IN_SPLITS)[:-1].tolist()
    h = x
    for i in range(DEPTH):
        g = norm_g[i]
        f = swiglu(rmsnorm(h, g[0]), ffn_w_gate[i, 0], ffn_w_up[i, 0], ffn_w_down[i, 0])
        h = h + 0.5 * rmsnorm(f, g[1])
        z = rmsnorm(h, g[2]) @ w_in[i]
        (a_u, a_v, b_x, b_gate, c_q, c_kc, c_vc, c_ks, c_vs, c_kw, c_vw,
         c_gc, c_gs, c_gw) = jnp.split(z, split_points, axis=-1)
        y_a = spatial_gating(jax.nn.gelu(a_u), jax.nn.gelu(a_v), sgu_norm_g[i], sgu_w[i], sgu_b[i])
        y_b = rg_lru_block(b_x, b_gate, conv_w[i], conv_b[i], lru_wa[i], lru_ba[i],
                           lru_wx[i], lru_bx[i], lru_lambda[i])
        kv = [t.reshape(bn, s, C_KV_GROUPS, HEAD_DIM) for t in (c_kc, c_vc, c_ks, c_vs, c_kw, c_vw)]
        y_c = nsa(c_q.reshape(bn, s, C_KV_GROUPS, C_HPG, HEAD_DIM), kv[0], kv[1], kv[2], kv[3], kv[4], kv[5],
                  c_gc, c_gs, c_gw, rel_bias, cmp_pos[i], cmp_w1[i], cmp_b1[i], cmp_w2[i], cmp_b2[i])
        mix = jnp.concatenate([y_a, y_b, y_c], axis=-1) @ w_out[i]
        h = h + rmsnorm(mix, g[3])
        f = swiglu(rmsnorm(h, g[4]), ffn_w_gate[i, 1], ffn_w_up[i, 1], ffn_w_down[i, 1])
        h = h + 0.5 * rmsnorm(f, g[5])
        gate = jax.nn.sigmoid(rmsnorm(h, g[6]) @ ple_w_gate[i])
        h = h + rmsnorm(gate * (p[i] @ ple_w_proj[i]), g[7])
    return h
```

```python
import contextlib
import math
import os
import numpy as np
import concourse.bass as bass
import concourse.mybir as mybir
from concourse.bass_utils import run_bass_kernel_spmd

F32 = mybir.dt.float32
BF16 = mybir.dt.bfloat16
AF = mybir.ActivationFunctionType
ALU = mybir.AluOpType

S_LEN = 4096
DM = 1024
DFF = 2816
NIN = 2328
EPS = 1e-6
NEGM = -30000.0
GSW = 1792
SKIP = os.environ.get('MIXSKIP', '')
STOP_AFTER = None


class Buf:
    __slots__ = ("name", "w", "r")

    def __init__(self, name):
        self.name = name
        self.w = {}
        self.r = {}


class Sched:
    def __init__(self, nc, stack, n_dma_sems=48):
        self.nc = nc
        self.ekeys = ["pe", "dve", "act", "pool", "sp"]
        self.prog = {k: [] for k in self.ekeys}
        self.sems = {}
        for k in self.ekeys:
            self.sems[k] = stack.enter_context(nc.semaphore("s_" + k))
        self.cnt = {k: 0 for k in self.ekeys}
        self.dsem_keys = []
        for i in range(n_dma_sems):
            k = "d%d" % i
            self.sems[k] = stack.enter_context(nc.semaphore("s_" + k))
            self.cnt[k] = 0
            self.dsem_keys.append(k)
        self.drr = 0
        self.waited = {k: {} for k in self.ekeys}
        self.sb_off = 16512
        self.uid = 0
        self.skipping = False

    def sb(self, name, shape, dtype, align=64):
        nbytes = int(np.prod(shape[1:])) * mybir.dt.size(dtype)
        off = (self.sb_off + align - 1) // align * align
        self.uid += 1
        t = self.nc.alloc_sbuf_tensor_at("%s_%d" % (name, self.uid), list(shape), dtype, offset=off)
        self.sb_off = off + nbytes
        assert self.sb_off <= 229376, (name, self.sb_off)
        return t

    def mark(self):
        return self.sb_off

    def release(self, m):
        self.sb_off = m

    def _wait(self, ek, ev):
        semkey, val = ev
        if semkey == ek and ek == "pe":
            return
        if self.waited[ek].get(semkey, 0) >= val:
            return
        self.waited[ek][semkey] = val
        sem = self.sems[semkey]
        self.prog[ek].append(lambda e, sem=sem, val=val: e.wait_ge(sem, val))

    def _deps(self, ek, reads, writes):
        deps = {}
        for b in reads:
            for k, v in b.w.items():
                deps[k] = max(deps.get(k, 0), v)
        for b in writes:
            for k, v in b.w.items():
                deps[k] = max(deps.get(k, 0), v)
            for k, v in b.r.items():
                deps[k] = max(deps.get(k, 0), v)
        for k, v in deps.items():
            self._wait(ek, (k, v))

    def _record(self, ev, reads, writes):
        for b in reads:
            b.r[ev[0]] = max(b.r.get(ev[0], 0), ev[1])
        for b in writes:
            b.w[ev[0]] = max(b.w.get(ev[0], 0), ev[1])
            b.r = {}

    def op(self, ek, fn, reads=(), writes=()):
        if self.skipping:
            return
        self._deps(ek, reads, writes)
        self.cnt[ek] += 1
        sem = self.sems[ek]
        self.prog[ek].append(lambda e, fn=fn, sem=sem: fn(e).then_inc(sem, 1))
        self._record((ek, self.cnt[ek]), reads, writes)

    def dma(self, qk, out, in_, reads=(), writes=(), **kw):
        if self.skipping:
            return
        self._deps(qk, reads, writes)
        sk = self.dsem_keys[self.drr % len(self.dsem_keys)]
        self.drr += 1
        if self.cnt[sk] > 0:
            self._wait(qk, (sk, self.cnt[sk]))
        self.cnt[sk] += 16
        sem = self.sems[sk]
        self.prog[qk].append(
            lambda e, out=out, in_=in_, sem=sem, kw=kw: e.dma_start(out=out, in_=in_, **kw).then_inc(sem, 16))
        self._record((sk, self.cnt[sk]), reads, writes)

    def barrier(self):
        for ek in self.ekeys:
            for k, v in self.cnt.items():
                if v > 0 and k != ek:
                    self._wait(ek, (k, v))

    def finish(self):
        for k, v in self.cnt.items():
            if v > 0 and k != "sp":
                self._wait("sp", (k, v))

    def emit(self):
        with self.nc.Block() as block:
            @block.tensor
            def _(e):
                for f in self.prog["pe"]:
                    f(e)

            @block.vector
            def _(e):
                for f in self.prog["dve"]:
                    f(e)

            @block.scalar
            def _(e):
                for f in self.prog["act"]:
                    f(e)

            @block.gpsimd
            def _(e):
                for f in self.prog["pool"]:
                    f(e)

            @block.sync
            def _(e):
                for f in self.prog["sp"]:
                    f(e)


def t5_bucket_np(d):
    n = np.maximum(d, 0)
    nf = np.maximum(n, 16).astype(np.float32)
    large = 16 + (np.log(nf / np.float32(16)) / np.float32(math.log(1024 / 16)) * np.float32(16)).astype(np.int32)
    large = np.minimum(large, 31)
    return np.where(n < 16, n, large)


def t5_bucket_jax_exact(d):
    import jax
    import jax.numpy as jnp
    with jax.default_device(jax.devices("cpu")[0]):
        n = jnp.maximum(jnp.asarray(d, jnp.int32), 0)
        nf = jnp.maximum(n, 16).astype(jnp.float32)
        large = 16 + (jnp.log(nf / 16) / math.log(1024 / 16) * 16).astype(jnp.int32)
        large = jnp.minimum(large, 31)
        return np.asarray(jnp.where(n < 16, n, large))


_CONST = {}


def host_constants():
    if _CONST:
        return _CONST
    c = _CONST
    c["ident"] = np.eye(128, dtype=np.float32)
    c["tril"] = np.tril(np.ones((128, 128), np.float32))
    e = np.zeros((128, 4096), np.float32)
    for k in range(4096):
        e[k // 64, k] = 1.0
        e[64 + k // 64, k] = 1.0
    c["eall"] = e
    ov = np.zeros((256, 64), np.float32)
    for s in range(1, 256):
        n = s - 1
        cs = 16 * n
        for j in range(64):
            o = min(cs + 32, 64 * j + 64) - max(cs, 64 * j)
            if o > 0:
                ov[s, j] = o / 32.0
    c["ovl"] = ov
    try:
        bk = t5_bucket_jax_exact(np.arange(0, 4200))
    except Exception:
        bk = t5_bucket_np(np.arange(0, 4200))
    c["bucket"] = bk
    tmax = np.zeros((128, 127), np.float32)
    tmin = np.full((128, 127), 1e5, np.float32)
    for p in range(128):
        hi = 1 if p >= 64 else 0
        for xx in range(127):
            rel = xx - 63 - hi
            if rel in (0, -1):
                tmax[p, xx] = 1e4
            if rel > 0:
                tmin[p, xx] = -1.0
    c["tmax"] = tmax
    c["tmin"] = tmin
    m = np.zeros((128, 896), np.float32)
    for p in range(128):
        xx = np.arange(896)
        m[p, (xx + 128 - p) >= 512] = NEGM
    c["m512"] = m
    return c


def host_bias_tables(rel_bias):
    c = host_constants()
    bk = c["bucket"]
    rb = np.asarray(rel_bias, np.float32)
    p = np.arange(128)[:, None]
    xx = np.arange(GSW)[None, :]
    d = xx - 384 - p
    ok = d >= 0
    g = rb[bk[np.maximum(d, 0)]]
    g = np.where(ok[:, :, None], g, np.float32(NEGM))
    gs = np.ascontiguousarray(np.transpose(g, (2, 0, 1))).astype(np.float32)
    s = np.arange(256)[:, None]
    t = np.arange(4096)[None, :]
    dc = t - (16 * (s - 1) + 31)
    okc = (dc >= 0) & (s >= 1)
    b = rb[bk[np.maximum(dc, 0)]]
    b = np.where(okc[:, :, None], b, np.float32(NEGM))
    bc = np.ascontiguousarray(np.transpose(b, (2, 0, 1))).astype(np.float32)
    c31 = np.ascontiguousarray(np.tile(rb[31][None, :], (128, 1))).astype(np.float32)
    return gs, bc, c31


def build(stop_after=None):
    nc = bass.Bass("TRN2", target_bir_lowering=False)
    D = {}

    def din(name, shape):
        D[name] = nc.dram_tensor(name, list(shape), F32, kind="ExternalInput").ap()
        return D[name]

    din("x", [S_LEN, DM]); din("pT", [2, 256, S_LEN]); din("ngB", [2, 8, 128, DM])
    din("wg", [2, 2, DM, DFF]); din("wu", [2, 2, DM, DFF]); din("wd", [2, 2, DFF, DM])
    din("w_in", [2, DM, NIN]); din("w_out", [2, DM, DM])
    din("sguB", [2, 128, 256]); din("sgu_w", [2, 4, 128, 128]); din("sgu_bT", [2, 128, 4])
    din("conv_wT", [2, 128, 2, 4]); din("conv_b", [2, 128, 2]); din("lru_wa", [2, 4, 64, 64]); din("lru_ba", [2, 128, 2])
    din("lru_wx", [2, 4, 64, 64]); din("lru_bx", [2, 128, 2]); din("lru_lam", [2, 128, 2])
    din("posT", [2, 64, 2, 32]); din("cmp_w1", [2, 2, 2048, 128]); din("b1T", [2, 128, 2])
    din("cmp_w2", [2, 2, 128, 64]); din("b2kk", [2, 128, 1]); din("b2vB", [2, 128, 64])
    din("wpg", [2, DM, DM]); din("wpp", [2, 256, DM])
    din("ident", [128, 128]); din("tril", [128, 128]); din("eall", [128, 4096]); din("ovl", [256, 64])
    din("gs", [8, 128, GSW]); din("m512", [128, 896]); din("bc", [8, 256, S_LEN])
    din("tmax", [128, 127]); din("tmin", [128, 127]); din("c31", [128, 8])
    y = nc.dram_tensor("y", [S_LEN, DM], F32, kind="ExternalOutput").ap()
    dbg = stop_after is not None and stop_after[1] == "mix"
    if dbg:
        ydA = nc.dram_tensor("ydA", [S_LEN, 256], F32, kind="ExternalOutput").ap()
        ydB = nc.dram_tensor("ydB", [256, S_LEN], F32, kind="ExternalOutput").ap()
        ydC = nc.dram_tensor("ydC", [S_LEN, 512], F32, kind="ExternalOutput").ap()
        ydR = nc.dram_tensor("ydR", [2, 8, 2, 128, 260], F32, kind="ExternalOutput").ap()
    hA = nc.dram_tensor("hA", [S_LEN, DM], F32).ap()
    hB = nc.dram_tensor("hB", [S_LEN, DM], F32).ap()

    with contextlib.ExitStack() as st:
        S = Sched(nc, st)
        PSB = []
        for i in range(7):
            PSB.append((st.enter_context(nc.psum_tensor("psb%d" % i, [128, 512], F32)), Buf("psb%d" % i)))
        pt, b_pt = st.enter_context(nc.psum_tensor("pst", [128, 1024], BF16)), Buf("pst")

        identb = S.sb("identb", [128, 128], BF16); b_const = Buf("const")
        S.dma("pool", identb[:], D["ident"], writes=[b_const])
        ones1 = S.sb("ones1", [128, 1], F32)
        S.op("dve", lambda e: e.memset(ones1[:], 1.0), writes=[b_const])
        glob_mark = S.mark()

        def hbufs(name):
            return [Buf("%s%d" % (name, i)) for i in range(32)]

        HB = {"x": hbufs("x"), "hA": hbufs("hA"), "hB": hbufs("hB"), "y": hbufs("y")}
        HAP = {"x": D["x"], "hA": hA, "hB": hB, "y": y}

        def rstd_from_ss(ss_ap, out_ap, n, reads, writes, tmpbuf):
            S.op("dve", lambda e: e.tensor_scalar(out_ap, ss_ap, 1.0 / n, EPS, ALU.mult, ALU.add), reads=reads, writes=writes)
            S.op("act", lambda e: e.activation(out_ap, out_ap, AF.Sqrt), reads=writes, writes=writes)
            S.op("dve", lambda e: e.reciprocal(out_ap, out_ap), reads=writes, writes=writes)

        def prenorm_xT(src, tok, hin_t, b_hin, gB, b_g, junk, b_junk, small, b_small, xn, b_xn, xT, b_xT, s):
            S.dma("sp", hin_t, HAP[src][tok * 128:(tok + 1) * 128, :], reads=[HB[src][tok]], writes=[b_hin])
            S.op("act", lambda e: e.activation(junk[:], hin_t, AF.Square, accum_out=small[:, 0:1]), reads=[b_hin], writes=[b_junk, b_small])
            rstd_from_ss(small[:, 0:1], small[:, 1:2], float(DM), [b_small], [b_small], None)
            S.op("dve", lambda e: e.scalar_tensor_tensor(out=xn[:], in0=hin_t, scalar=small[:, 1:2], in1=gB, op0=ALU.mult, op1=ALU.mult),
                 reads=[b_hin, b_small, b_g], writes=[b_xn])
            for c in range(8):
                S.op("pe", lambda e, c=c: e.transpose(pt[:, c * 128:(c + 1) * 128], xn[:, c * 128:(c + 1) * 128], identb[:]),
                     reads=[b_xn, b_const], writes=[b_pt])
            S.op("act", lambda e: e.activation(xT[:, :, s * 128:(s + 1) * 128], pt[:].rearrange("p (c n) -> p c n", c=8), AF.Identity),
                 reads=[b_pt], writes=[b_xT])

        def postnorm_residual(banks, src, dst, tok, hin_t, b_hin, gpostB, b_g, junk, b_junk, small, b_small, ftmp, b_ftmp, reload):
            if reload:
                S.dma("sp", hin_t, HAP[src][tok * 128:(tok + 1) * 128, :], reads=[HB[src][tok]], writes=[b_hin])
            for hf in range(2):
                S.op("act", lambda e, hf=hf: e.activation(junk[:, 0:512], banks[hf][0][:, 0:512], AF.Square, accum_out=small[:, 2 + hf:3 + hf]),
                     reads=[banks[hf][1]], writes=[b_junk, b_small])
            S.op("dve", lambda e: e.tensor_tensor(small[:, 4:5], small[:, 2:3], small[:, 3:4], ALU.add), reads=[b_small], writes=[b_small])
            rstd_from_ss(small[:, 4:5], small[:, 5:6], float(DM), [b_small], [b_small], None)
            for hf in range(2):
                S.op("dve", lambda e, hf=hf: e.scalar_tensor_tensor(out=ftmp[:, 0:512], in0=banks[hf][0][:, 0:512], scalar=small[:, 5:6],
                                                                     in1=gpostB[:, hf * 512:(hf + 1) * 512], op0=ALU.mult, op1=ALU.mult),
                     reads=[banks[hf][1], b_small, b_g], writes=[b_ftmp])
                S.op("pool", lambda e, hf=hf: e.tensor_tensor(hin_t[:, hf * 512:(hf + 1) * 512], hin_t[:, hf * 512:(hf + 1) * 512], ftmp[:, 0:512], ALU.add),
                     reads=[b_hin, b_ftmp], writes=[b_hin])
            S.dma("sp", HAP[dst][tok * 128:(tok + 1) * 128, :], hin_t, reads=[b_hin], writes=[HB[dst][tok]])

        def ffn_phase(L, which, src, dst):
            S.barrier()
            S.release(glob_mark)
            wg_sb = S.sb("wg", [128, 8, DFF], BF16); wu_sb = S.sb("wu", [128, 8, DFF], BF16)
            wd_sb = S.sb("wd", [128, 22, DM], BF16)
            b_wg, b_wu, b_wd = Buf("wg"), Buf("wu"), Buf("wd")
            gpre = S.sb("gpre", [128, DM], F32); gpost = S.sb("gpost", [128, DM], F32); b_g = Buf("g")
            hin = S.sb("hin", [128, 4, DM], F32); b_hin = [Buf("hin%d" % i) for i in range(4)]
            xT = S.sb("xT", [128, 8, 512], BF16); b_xT = Buf("xT")
            hT = S.sb("hT", [128, 22, 512], BF16); b_hT = Buf("hT")
            xn = S.sb("xn", [128, DM], BF16); b_xn = Buf("xn")
            junk = S.sb("junk", [128, DM], BF16); b_junk = Buf("junk")
            small = S.sb("small", [128, 8], F32); b_small = Buf("small")
            ftmp = S.sb("ftmp", [128, DM], F32); b_ftmp = Buf("ftmp")
            sgt = [S.sb("sgt%d" % i, [128, 512], BF16) for i in range(2)]; b_sgt = [Buf("sgt0"), Buf("sgt1")]
            ni = 4 * which
            S.dma("sp", gpre[:], D["ngB"][L, ni], writes=[b_g])
            S.dma("sp", gpost[:], D["ngB"][L, ni + 1], writes=[b_g])
            S.op("dve", lambda e: e.tensor_scalar(gpost[:], gpost[:], 0.5, None, ALU.mult), reads=[b_g], writes=[b_g])
            NCB = 4
            CW = DFF // NCB
            cbs = [(0, 768), (768, 1536), (1536, 2304), (2304, 2816)]
            b_wgc = [Buf("wgc%d" % i) for i in range(len(cbs))]
            b_wuc = [Buf("wuc%d" % i) for i in range(len(cbs))]
            b_wdc = [Buf("wdc%d" % i) for i in range(22)]
            for bi_, (c0, c1) in enumerate(cbs):
                for c in range(8):
                    S.dma("pool", wg_sb[:, c, c0:c1], D["wg"][L, which, c * 128:(c + 1) * 128, c0:c1], writes=[b_wgc[bi_]])
                    S.dma("pool", wu_sb[:, c, c0:c1], D["wu"][L, which, c * 128:(c + 1) * 128, c0:c1], writes=[b_wuc[bi_]])
            for c in range(22):
                S.dma("pool", wd_sb[:, c, :], D["wd"][L, which, c * 128:(c + 1) * 128, :], writes=[b_wdc[c]], max_dma_last_dim=4096)

            def cb_of(fc):
                for bi_, (c0, c1) in enumerate(cbs):
                    if c0 <= fc * 128 < c1:
                        return bi_
            pair = 0
            dbank = 0
            for T in range(8):
                for s in range(4):
                    prenorm_xT(src, T * 4 + s, hin[:, s, :], b_hin[s], gpre[:], b_g, junk, b_junk, small, b_small, xn, b_xn, xT, b_xT, s)
                for fc in range(22):
                    pg, bg = PSB[(pair % 2) * 2]
                    pu, bu = PSB[(pair % 2) * 2 + 1]
                    pair += 1
                    for (wsb, bw, ps_, bps) in ((wg_sb, b_wgc[cb_of(fc)], pg, bg), (wu_sb, b_wuc[cb_of(fc)], pu, bu)):
                        for kc in range(8):
                            S.op("pe", lambda e, wsb=wsb, ps_=ps_, kc=kc, fc=fc: e.matmul(ps_[:, 0:512], wsb[:, kc, fc * 128:(fc + 1) * 128], xT[:, kc, :],
                                                                                          start=(kc == 0), stop=(kc == 7)),
                                 reads=[bw, b_xT], writes=[bps])
                    sg_, bsg = sgt[fc % 2], b_sgt[fc % 2]
                    S.op("act", lambda e, sg_=sg_, pg=pg: e.activation(sg_[:], pg[:, 0:512], AF.Silu), reads=[bg], writes=[bsg])
                    S.op("dve", lambda e, sg_=sg_, pu=pu, fc=fc: e.tensor_tensor(hT[:, fc, :], sg_[:], pu[:, 0:512], ALU.mult),
                         reads=[bsg, bu], writes=[b_hT])
                for s in range(4):
                    banks = []
                    for hf in range(2):
                        pb = PSB[4 + dbank % 3]; dbank += 1
                        banks.append(pb)
                        for fc in range(22):
                            S.op("pe", lambda e, pb=pb, fc=fc, s=s, hf=hf: e.matmul(pb[0][:, 0:512], hT[:, fc, s * 128:(s + 1) * 128],
                                                                                    wd_sb[:, fc, hf * 512:(hf + 1) * 512], start=(fc == 0), stop=(fc == 21)),
                                 reads=[b_hT, b_wdc[fc]], writes=[pb[1]])
                    postnorm_residual(banks, src, dst, T * 4 + s, hin[:, s, :], b_hin[s], gpost, b_g, junk, b_junk, small, b_small, ftmp, b_ftmp, False)

        def ple_phase(L, src, dst):
            S.barrier()
            S.release(glob_mark)
            wpg_sb = S.sb("wpg", [128, 8, DM], BF16); wpp_sb = S.sb("wpp", [128, 2, DM], BF16)
            pT_sb = S.sb("pTs", [128, 2, S_LEN], BF16)
            b_w = Buf("plew")
            gpre = S.sb("gpre", [128, DM], F32); gpost = S.sb("gpost", [128, DM], F32); b_g = Buf("g")
            hin = S.sb("hin", [128, 2, DM], F32); b_hin = [Buf("hin0"), Buf("hin1")]
            def two(name, shape, dt):
                return [S.sb(name + str(i), shape, dt) for i in range(2)], [Buf(name + str(i)) for i in range(2)]
            xT2, b_xT2 = two("xT", [128, 8, 128], BF16)
            xn2, b_xn2 = two("xn", [128, DM], BF16)
            junk2, b_junk2 = two("junk", [128, DM], BF16)
            small2, b_small2 = two("small", [128, 8], F32)
            ftmp2, b_ftmp2 = two("ftmp", [128, DM], F32)
            sgf2, b_sgf2 = two("sgf", [128, DM], F32)
            u2, b_u2 = two("u", [128, DM], F32)
            S.dma("sp", gpre[:], D["ngB"][L, 6], writes=[b_g])
            S.dma("sp", gpost[:], D["ngB"][L, 7], writes=[b_g])
            for c in range(8):
                S.dma("pool", wpg_sb[:, c, :], D["wpg"][L, c * 128:(c + 1) * 128, :], writes=[b_w])
            for c in range(2):
                S.dma("pool", wpp_sb[:, c, :], D["wpp"][L, c * 128:(c + 1) * 128, :], writes=[b_w])
                for q in range(4):
                    S.dma("pool", pT_sb[:, c, q * 1024:(q + 1) * 1024], D["pT"][L, c * 128:(c + 1) * 128, q * 1024:(q + 1) * 1024], writes=[b_w])
            rbl = [0]

            def _tile(tok, xT, b_xT, xn, b_xn, junk, b_junk, small, b_small, ftmp, b_ftmp, sgf, b_sgf, u, b_u, hi_, bh):
                prenorm_xT(src, tok, hi_, bh, gpre[:], b_g, junk, b_junk, small, b_small, xn, b_xn, xT, b_xT, 0)
                gb = []
                pb_ = []
                for hf in range(2):
                    g_ = PSB[rbl[0] % 7]; rbl[0] += 1
                    p_ = PSB[rbl[0] % 7]; rbl[0] += 1
                    for kc in range(8):
                        S.op("pe", lambda e, g_=g_, kc=kc, hf=hf: e.matmul(g_[0][:, 0:512], xT[:, kc, :], wpg_sb[:, kc, hf * 512:(hf + 1) * 512],
                                                                           start=(kc == 0), stop=(kc == 7)), reads=[b_xT, b_w], writes=[g_[1]])
                    for c in range(2):
                        S.op("pe", lambda e, p_=p_, c=c, hf=hf, tok=tok: e.matmul(p_[0][:, 0:512], pT_sb[:, c, tok * 128:(tok + 1) * 128],
                                                                                  wpp_sb[:, c, hf * 512:(hf + 1) * 512], start=(c == 0), stop=(c == 1)),
                             reads=[b_w], writes=[p_[1]])
                    S.op("act", lambda e, g_=g_, hf=hf: e.activation(sgf[:, hf * 512:(hf + 1) * 512], g_[0][:, 0:512], AF.Sigmoid), reads=[g_[1]], writes=[b_sgf])
                    S.op("dve", lambda e, p_=p_, hf=hf: e.tensor_tensor(u[:, hf * 512:(hf + 1) * 512], sgf[:, hf * 512:(hf + 1) * 512], p_[0][:, 0:512], ALU.mult),
                         reads=[b_sgf, p_[1]], writes=[b_u])
                S.op("act", lambda e: e.activation(junk[:], u[:], AF.Square, accum_out=small[:, 4:5]), reads=[b_u], writes=[b_junk, b_small])
                rstd_from_ss(small[:, 4:5], small[:, 5:6], float(DM), [b_small], [b_small], None)
                S.op("dve", lambda e: e.scalar_tensor_tensor(out=ftmp[:], in0=u[:], scalar=small[:, 5:6], in1=gpost[:], op0=ALU.mult, op1=ALU.mult),
                     reads=[b_u, b_small, b_g], writes=[b_ftmp])
                S.op("pool", lambda e, hi_=hi_: e.tensor_tensor(hi_, hi_, ftmp[:], ALU.add), reads=[bh, b_ftmp], writes=[bh])
                S.dma("sp", HAP[dst][tok * 128:(tok + 1) * 128, :], hi_, reads=[bh], writes=[HB[dst][tok]])

            for tok in range(32):
                k_ = 0
                _tile(tok, xT2[k_], b_xT2[k_], xn2[k_], b_xn2[k_], junk2[k_], b_junk2[k_], small2[k_], b_small2[k_], ftmp2[k_], b_ftmp2[k_],
                      sgf2[k_], b_sgf2[k_], u2[k_], b_u2[k_], hin[:, tok % 2, :], b_hin[tok % 2])

        def mixer_phase(L, src, dst):
            S.barrier()
            S.release(glob_mark)
            cb = Buf("mixconst")
            win_sb = S.sb("win", [128, 8, NIN], BF16)
            wout_sb = S.sb("wout", [128, 8, DM], BF16)
            w1_sb = S.sb("w1", [128, 64, 128], BF16)
            w2k_pad = S.sb("w2kp", [128, 2, 128], BF16)
            w2v_sb = S.sb("w2v", [128, 64], BF16)
            posT_sb = S.sb("posT", [128, 2, 32], BF16)
            b1c = S.sb("b1c", [128, 2], F32)
            b2k = S.sb("b2k", [128, 1], F32)
            b2vB = S.sb("b2vB", [128, 64], F32)
            ksE = [S.sb("ksE%d" % i, [128, S_LEN], BF16) for i in range(2)]; b_ksT = Buf("ksT")
            kwT = S.sb("kwT", [128, S_LEN], BF16); b_kwT = Buf("kwT")
            vs_aug = S.sb("vsa", [128, 32, 2, 65], BF16); b_vs = Buf("vsa")
            vw_aug = S.sb("vwa", [128, 32, 2, 65], BF16); b_vw = Buf("vwa")
            kcmpT = S.sb("kcmpT", [128, 256], BF16); b_kcmp = Buf("kcmpT")
            cv_aug = S.sb("cva", [128, 2, 2, 129], BF16); b_cv = Buf("cva")
            gs_cur = [S.sb("gsc%d" % i, [128, GSW], BF16) for i in range(2)]; b_gs = [Buf("gs0"), Buf("gs1")]
            identf = S.sb("identf", [128, 128], F32)
            m512_sb = S.sb("m512", [128, 896], F32)
            c31_sb = S.sb("c31", [128, 8], F32)
            tmax_sb = S.sb("tmax", [128, 127], F32); tmin_sb = S.sb("tmin", [128, 127], F32)
            cw_sb = S.sb("cw", [128, 2, 4], F32); cbias = S.sb("cbias", [128, 2], F32)
            bda = S.sb("bda", [128, 2, 128], BF16); bdx = S.sb("bdx", [128, 2, 128], BF16)
            ba_sb = S.sb("ba", [128, 2], F32); bx_sb = S.sb("bx", [128, 2], F32); lamc = S.sb("lamc", [128, 2], F32)
            wsT = S.sb("wsT", [128, 4, 128], BF16)
            sguB = S.sb("sguB", [128, 256], F32); bsT = S.sb("bsT", [128, 4], F32)
            gpre = S.sb("gpre", [128, DM], F32); gpost = S.sb("gpost", [128, DM], F32)
            hin = S.sb("hin", [128, 2, DM], F32)[:, 0:1, :] if False else S.sb("hin", [128, 1, DM], F32); b_hin = [Buf("hin0"), Buf("hin0b")]; b_hin[1] = b_hin[0]
            xT = S.sb("xT", [128, 8, 512], BF16); b_xT = Buf("xT")
            xn = S.sb("xn", [128, DM], BF16); b_xn = Buf("xn")
            junk = xn; b_junk = b_xn
            small = S.sb("small", [128, 8], F32); b_small = Buf("small")
            ftmp = S.sb("ftmp", [128, 512], F32); b_ftmp = Buf("ftmp")
            qz = S.sb("qz", [128, 8, 512], BF16); b_qT = Buf("qz")
            rz = S.sb("rz", [128, 4, 528], BF16); b_roll = Buf("roll")
            xbT = S.sb("xbT", [128, 2, 516], F32); b_xb = Buf("xbT")
            gateT = S.sb("gateT", [128, 2, 512], BF16); b_gate = Buf("gateT")
            carry = S.sb("carry", [128, 2], F32); b_carry = Buf("carry")
            sg = S.sb("sg", [128, 4, 24], F32); b_sg = Buf("sg")
            uv = S.sb("uv", [128, 512], F32); b_uv = Buf("uv")
            avn = S.sb("avn", [128, 256], BF16); b_avn = Buf("avn")
            ytok = S.sb("ytok", [128, 4, 256], BF16); b_ytok = [Buf("ytok%d" % i) for i in range(4)]
            yT = S.sb("yT", [128, 8, 512], BF16); b_yT = Buf("yT")
            ycomb = S.sb("ycomb", [128, 4, 512], F32); b_ycs = [Buf("ycomb%d" % i) for i in range(4)]
            impS = S.sb("imp", [128, 4, 64], F32); b_imp = Buf("imp")
            lg = [S.sb("lg%d" % i, [128, 512], F32) for i in range(2)]; b_lg = [Buf("lg0"), Buf("lg1")]
            PT = [S.sb("PT%d" % i, [128, 512], BF16) for i in range(3)]; b_PT = [Buf("PT%d" % i) for i in range(3)]
            bct = [S.sb("bct0", [128, 512], F32)] * 2; b_bct = [Buf("bct0")] * 2
            nmz = [S.sb("nmz%d" % i, [128, 512], BF16) for i in range(2)]; b_nmT = Buf("nmT")
            nmp = [S.sb("nmp%d" % i, [128, 128], BF16) for i in range(2)]
            fw = [ycomb[:, i, :] for i in range(4)]; b_fw = b_ycs
            xcb = S.sb("xcb", [128, 512], BF16); b_xcb = Buf("xcb")
            hid = S.sb("hid", [128, 4, 64], BF16); b_hid = Buf("hid")
            hidv = S.sb("hidv", [128, 2, 128], BF16); b_hidv = Buf("hidv")
            onesp = S.sb("onesp", [1, 4, 128], BF16); b2v_row = S.sb("b2vr", [1, 64], BF16)
            sc = S.sb("sc", [128, 64], F32); sc2 = S.sb("sc2", [128, 64], F32); m8 = S.sb("m8", [128, 16], F32)
            nm = S.sb("nm", [128, 64], BF16); b_sc = Buf("sc")
            z4 = S.sb("z4", [128, 8], F32); b_z4 = Buf("z4")

            b_win, b_wout, b_cmpw = Buf("win"), Buf("wout"), Buf("cmpw")

            def cdma(q, out, in_, buf=None, **kw):
                S.dma(q, out, in_, writes=[buf if buf is not None else cb], **kw)
            W = D["w_in"][L]
            for c in range(8):
                rows = slice(c * 128, (c + 1) * 128)
                cdma("pool", win_sb[:, c, 0:1024], W[rows, 0:1024], buf=b_win)
                for r in range(4):
                    cdma("pool", win_sb[:, c, 1024 + r * 128:1024 + r * 128 + 64], W[rows, 1024 + r * 64:1024 + r * 64 + 64], buf=b_win)
                    cdma("pool", win_sb[:, c, 1024 + r * 128 + 64:1024 + r * 128 + 128], W[rows, 1024 + (4 + r) * 64:1024 + (4 + r) * 64 + 64], buf=b_win)
                cdma("pool", win_sb[:, c, 1536:NIN], W[rows, 1536:NIN], buf=b_win)
                cdma("pool", wout_sb[:, c, :], D["w_out"][L, rows, :], buf=b_wout)
            for kv in range(2):
                src_w1 = D["cmp_w1"][L, kv].rearrange("(l d) j -> d l j", d=64)
                for half in range(2):
                    for lq in range(4):
                        cdma("pool", w1_sb[half * 64:(half + 1) * 64, kv * 32 + lq * 8:kv * 32 + lq * 8 + 8, :], src_w1[:, lq * 8:(lq + 1) * 8, :], buf=b_cmpw)
            S.op("dve", lambda e: e.memset(w2k_pad[:], 0.0), writes=[b_cmpw])
            for g in range(2):
                cdma("pool", w2k_pad[:, g, g * 64:(g + 1) * 64], D["cmp_w2"][L, 0], buf=b_cmpw)
            cdma("pool", w2v_sb[:], D["cmp_w2"][L, 1], buf=b_cmpw)
            for half in range(2):
                cdma("pool", posT_sb[half * 64:(half + 1) * 64, :, :], D["posT"][L], buf=b_cmpw)
            cdma("sp", b1c[:], D["b1T"][L]); cdma("sp", b2k[:], D["b2kk"][L]); cdma("sp", b2vB[:], D["b2vB"][L], buf=b_cmpw)
            cdma("sp", identf[:], D["ident"])
            cdma("sp", m512_sb[:], D["m512"])
            for q in range(4):
                S.dma("pool", ksE[0][64:128, q * 1024:(q + 1) * 1024], D["eall"][0:64, q * 1024:(q + 1) * 1024], writes=[b_ksT])
                S.dma("pool", ksE[1][0:64, q * 1024:(q + 1) * 1024], D["eall"][0:64, q * 1024:(q + 1) * 1024], writes=[b_ksT])
            cdma("sp", c31_sb[:], D["c31"]); cdma("sp", tmax_sb[:], D["tmax"]); cdma("sp", tmin_sb[:], D["tmin"])
            cdma("sp", cw_sb[:], D["conv_wT"][L])
            cdma("sp", cbias[:], D["conv_b"][L])
            cdma("sp", ba_sb[:], D["lru_ba"][L])
            cdma("sp", bx_sb[:], D["lru_bx"][L])
            cdma("sp", lamc[:], D["lru_lam"][L])
            S.op("dve", lambda e: e.memset(bda[:], 0.0), writes=[cb])
            S.op("dve", lambda e: e.memset(bdx[:], 0.0), writes=[cb])
            for gi in range(4):
                i, a = gi // 2, gi % 2
                cdma("pool", bda[a * 64:(a + 1) * 64, i, a * 64:(a + 1) * 64], D["lru_wa"][L, gi])
                cdma("pool", bdx[a * 64:(a + 1) * 64, i, a * 64:(a + 1) * 64], D["lru_wx"][L, gi])
            cdma("sp", sguB[:], D["sguB"][L]); cdma("sp", bsT[:], D["sgu_bT"][L])
            cdma("sp", gpre[:], D["ngB"][L, 2]); cdma("sp", gpost[:], D["ngB"][L, 3])
            S.op("act", lambda e: e.activation(lamc[:], lamc[:], AF.Exp, scale=-1.0), reads=[cb], writes=[cb])
            S.op("act", lambda e: e.activation(lamc[:], lamc[:], AF.Ln, bias=ones1[:, 0:1]), reads=[cb, b_const], writes=[cb])
            S.op("dve", lambda e: e.tensor_scalar(lamc[:], lamc[:], -8.0, None, ALU.mult), reads=[cb], writes=[cb])
            trl = fw[0]
            S.dma("sp", trl[:, 0:128], D["tril"], writes=[b_fw[0]])
            for g in range(4):
                S.dma("sp", fw[1][:, 0:128], D["sgu_w"][L, g], writes=[b_fw[1]])
                S.op("dve", lambda e: e.tensor_tensor(xn[:, 0:128], fw[1][:, 0:128], trl[:, 0:128], ALU.mult), reads=[b_fw[0], b_fw[1]], writes=[b_xn])
                S.op("pe", lambda e: e.transpose(pt[:, 0:128], xn[:, 0:128], identb[:]), reads=[b_xn, b_const], writes=[b_pt])
                S.op("act", lambda e, g=g: e.activation(wsT[:, g, :], pt[:, 0:128], AF.Identity), reads=[b_pt], writes=[cb])
            pbk = PSB[0]
            for kv in range(2):
                for l in range(32):
                    S.op("pe", lambda e, kv=kv, l=l: e.matmul(pbk[0][:, kv:kv + 1], w1_sb[0:64, kv * 32 + l, :], posT_sb[0:64, kv, l:l + 1],
                                                              start=(kv == 0 and l == 0), stop=(kv == 1 and l == 31), skip_group_check=True),
                         reads=[b_cmpw], writes=[pbk[1]])
            S.op("dve", lambda e: e.tensor_tensor(b1c[:], b1c[:], pbk[0][:, 0:2], ALU.add), reads=[b_cmpw, pbk[1]], writes=[b_cmpw])
            S.op("dve", lambda e: e.memset(vs_aug[:], 1.0), writes=[b_vs])
            S.op("dve", lambda e: e.memset(vw_aug[:], 1.0), writes=[b_vw])
            S.op("dve", lambda e: e.memset(kcmpT[:], 0.0), writes=[b_kcmp])
            S.op("dve", lambda e: e.memset(cv_aug[:], 0.0), writes=[b_cv])
            S.op("dve", lambda e: e.memset(cv_aug[:, :, :, 128:129], 1.0), writes=[b_cv])
            for stt in range(2):
                for g in range(2):
                    S.dma("pool", cv_aug[:, stt, g, 0:64], D["ovl"][stt * 128:(stt + 1) * 128, :], writes=[b_cv])
            S.op("dve", lambda e: e.memset(rz[:], 0.0), writes=[b_roll])
            S.op("dve", lambda e: e.memset(qz[:], 0.0), writes=[b_qT])
            for i_ in range(2):
                S.op("dve", lambda e, i_=i_: e.memset(nmp[i_][:], 0.0), writes=[b_sc])
            S.op("dve", lambda e: e.memset(xbT[:], 0.0), writes=[b_xb])
            S.op("dve", lambda e: e.memset(carry[:], 0.0), writes=[b_carry])
            S.op("dve", lambda e: e.memset(hid[:], 0.0), writes=[b_hid])
            S.op("dve", lambda e: e.memset(onesp[:], 0.0), writes=[cb])
            for a4_ in range(4):
                S.op("dve", lambda e, a4_=a4_: e.memset(onesp[0:1, a4_, 32 * a4_:32 * a4_ + 32], 1.0), reads=[cb], writes=[cb])
            S.dma("pool", b2v_row[:], D["b2vB"][L, 0:1, :], writes=[cb])

            rot = [0]

            def nextbank():
                b = PSB[rot[0] % 3]
                rot[0] += 1
                return b

            ptc = [0]
            lgc = [0]
            gsr = [0]

            for T in range(8):
                t0 = T * 512
                for s in range(4):
                    tok = T * 4 + s
                    prenorm_xT(src, tok, hin[:, 0, :], b_hin[0], gpre[:], cb, junk, b_junk, small, b_small, xn, b_xn, xT, b_xT, s)

                def fm_proj(c0, ncol, evac):
                    pb = nextbank()
                    for kc in range(8):
                        S.op("pe", lambda e, pb=pb, kc=kc: e.matmul(pb[0][0:ncol, 0:512], win_sb[:, kc, c0:c0 + ncol], xT[:, kc, :], start=(kc == 0), stop=(kc == 7)),
                             reads=[b_win, b_xT], writes=[pb[1]])
                    evac(pb)
                for i in range(2):
                    fm_proj(512 + i * 128, 128, lambda pb, i=i: S.op("act", lambda e: e.activation(xbT[:, i, 3:515], pb[0][:, 0:512], AF.Identity), reads=[pb[1]], writes=[b_xb]))
                    fm_proj(768 + i * 128, 128, lambda pb, i=i: S.op("act", lambda e: e.activation(gateT[:, i, :], pb[0][:, 0:512], AF.Gelu_apprx_tanh), reads=[pb[1]], writes=[b_gate]))
                S.op("pool", lambda e: e.memset(qz[64:128, 0:4, :], 0.0), reads=[b_qT], writes=[b_qT])
                S.op("pool", lambda e: e.memset(qz[0:64, 4:8, :], 0.0), reads=[b_qT], writes=[b_qT])
                for r in range(4):
                    def _qev(pb, r=r):
                        S.op("act", lambda e: e.activation(qz[0:64, r, :], pb[0][0:64, 0:512], AF.Identity), reads=[pb[1]], writes=[b_qT])
                        S.op("dve", lambda e: e.tensor_copy(qz[64:128, 4 + r, :], pb[0][64:128, 0:512]), reads=[pb[1]], writes=[b_qT])
                    fm_proj(1024 + r * 128, 128, _qev)
                for kv_ in range(2):
                    def _rev(pb, kv_=kv_):
                        S.op("act", lambda e: e.activation(rz[0:64, kv_ * 2, 16:528], pb[0][0:64, 0:512], AF.Identity), reads=[pb[1]], writes=[b_roll])
                        S.op("dve", lambda e: e.tensor_copy(rz[64:128, kv_ * 2 + 1, 16:528], pb[0][64:128, 0:512]), reads=[pb[1]], writes=[b_roll])
                    fm_proj(1536 + 128 * kv_, 128, _rev)
                def _kev(pb, t0=t0):
                    S.op("dve", lambda e: e.tensor_copy(ksE[0][0:64, t0:t0 + 512], pb[0][0:64, 0:512]), reads=[pb[1]], writes=[b_ksT])
                    S.op("act", lambda e: e.activation(ksE[1][64:128, t0:t0 + 512], pb[0][64:128, 0:512], AF.Identity), reads=[pb[1]], writes=[b_ksT])
                fm_proj(1792, 128, _kev)
                fm_proj(2048, 128, lambda pb, t0=t0: S.op("act", lambda e: e.activation(kwT[:, t0:t0 + 512], pb[0][:, 0:512], AF.Identity), reads=[pb[1]], writes=[b_kwT]))

                S.skipping = 'a' in SKIP
                for s in range(4):
                    tok = T * 4 + s
                    ts_ = slice(s * 128, (s + 1) * 128)
                    pa = nextbank()
                    for kc in range(8):
                        S.op("pe", lambda e, pa=pa, kc=kc, ts_=ts_: e.matmul(pa[0][:, 0:512], xT[:, kc, ts_], win_sb[:, kc, 0:512], start=(kc == 0), stop=(kc == 7)),
                             reads=[b_win, b_xT], writes=[pa[1]])
                    S.op("act", lambda e, pa=pa: e.activation(uv[:], pa[0][:, 0:512], AF.Gelu_apprx_tanh), reads=[pa[1]], writes=[b_uv])
                    S.op("act", lambda e: e.activation(junk[:, 0:256], uv[:, 256:512], AF.Square, accum_out=small[:, 6:7]), reads=[b_uv], writes=[b_junk, b_small])
                    rstd_from_ss(small[:, 6:7], small[:, 7:8], 256.0, [b_small], [b_small], None)
                    S.op("dve", lambda e: e.scalar_tensor_tensor(out=avn[:], in0=uv[:, 256:512], scalar=small[:, 7:8], in1=sguB[:], op0=ALU.mult, op1=ALU.mult),
                         reads=[b_uv, b_small, cb], writes=[b_avn])
                    pm = nextbank()
                    for g in range(4):
                        S.op("pe", lambda e, pm=pm, g=g: e.matmul(pm[0][:, g * 64:(g + 1) * 64], wsT[:, g, :], avn[:, g * 64:(g + 1) * 64], start=True, stop=True,
                                                                  skip_group_check=True),
                             reads=[cb, b_avn], writes=[pm[1]])
                    for g in range(4):
                        S.op("dve", lambda e, pm=pm, g=g, s=s: e.scalar_tensor_tensor(out=ytok[:, s, g * 64:(g + 1) * 64], in0=pm[0][:, g * 64:(g + 1) * 64],
                                                                                      scalar=bsT[:, g:g + 1], in1=uv[:, g * 64:(g + 1) * 64], op0=ALU.add, op1=ALU.mult),
                             reads=[pm[1], cb, b_uv], writes=[b_ytok[s]])
                    pv = nextbank()
                    for kc in range(8):
                        S.op("pe", lambda e, pv=pv, kc=kc, ts_=ts_: e.matmul(pv[0][:, 0:128], xT[:, kc, ts_], win_sb[:, kc, 1920:2048], start=(kc == 0), stop=(kc == 7),
                                                                            skip_group_check=True),
                             reads=[b_win, b_xT], writes=[pv[1]])
                    for kc in range(8):
                        S.op("pe", lambda e, pv=pv, kc=kc, ts_=ts_: e.matmul(pv[0][:, 128:280], xT[:, kc, ts_], win_sb[:, kc, 2176:2328], start=False, stop=(kc == 7),
                                                                            skip_group_check=True),
                             reads=[b_win, b_xT], writes=[pv[1]])
                    for g_ in range(2):
                        S.op("act", lambda e, pv=pv, tok=tok, g_=g_: e.activation(vs_aug[:, tok, g_, 0:64], pv[0][:, g_ * 64:(g_ + 1) * 64], AF.Identity),
                             reads=[pv[1]], writes=[b_vs])
                        S.op("act", lambda e, pv=pv, tok=tok, g_=g_: e.activation(vw_aug[:, tok, g_, 0:64], pv[0][:, 128 + g_ * 64:128 + (g_ + 1) * 64], AF.Identity),
                             reads=[pv[1]], writes=[b_vw])
                    S.op("act", lambda e, pv=pv, s=s: e.activation(sg[:, s, :], pv[0][:, 256:280], AF.Sigmoid), reads=[pv[1]], writes=[b_sg])

                S.skipping = 'b' in SKIP
                for i in range(2):
                    xc, ig, aa, bb = fw[0], fw[1], fw[2], fw[3]
                    S.op("dve", lambda e, i=i: e.tensor_scalar(xc[:], xbT[:, i, 0:512], cw_sb[:, i, 0:1], cbias[:, i:i + 1], ALU.mult, ALU.add),
                         reads=[b_xb, cb], writes=[b_fw[0]])
                    for k in range(1, 4):
                        S.op("dve", lambda e, i=i, k=k: e.scalar_tensor_tensor(out=xc[:], in0=xbT[:, i, k:k + 512], scalar=cw_sb[:, i, k:k + 1], in1=xc[:],
                                                                               op0=ALU.mult, op1=ALU.add), reads=[b_xb, cb, b_fw[0]], writes=[b_fw[0]])
                    S.op("dve", lambda e, i=i: e.tensor_copy(xbT[:, i, 0:3], xbT[:, i, 512:515]), reads=[b_xb], writes=[b_xb])
                    S.op("act", lambda e: e.activation(xcb[:], xc[:], AF.Identity), reads=[b_fw[0]], writes=[b_xcb])
                    pr = nextbank()
                    S.op("pe", lambda e, pr=pr, i=i: e.matmul(pr[0][:, 0:512], bda[:, i, :], xcb[:], start=True, stop=True), reads=[cb, b_xcb], writes=[pr[1]])
                    pi = nextbank()
                    S.op("pe", lambda e, pi=pi, i=i: e.matmul(pi[0][:, 0:512], bdx[:, i, :], xcb[:], start=True, stop=True), reads=[cb, b_xcb], writes=[pi[1]])
                    S.op("act", lambda e, pr=pr, i=i: e.activation(aa[:], pr[0][:, 0:512], AF.Sigmoid, bias=ba_sb[:, i:i + 1]), reads=[pr[1], cb], writes=[b_fw[2]])
                    S.op("act", lambda e, pi=pi, i=i: e.activation(ig[:], pi[0][:, 0:512], AF.Sigmoid, bias=bx_sb[:, i:i + 1]), reads=[pi[1], cb], writes=[b_fw[1]])
                    S.op("act", lambda e, i=i: e.activation(aa[:], aa[:], AF.Exp, scale=lamc[:, i:i + 1]), reads=[b_fw[2], cb], writes=[b_fw[2]])
                    S.op("dve", lambda e: e.tensor_tensor(bb[:], aa[:], aa[:], ALU.mult), reads=[b_fw[2]], writes=[b_fw[3]])
                    S.op("dve", lambda e: e.tensor_scalar(bb[:], bb[:], -1.0, 1.0, ALU.mult, ALU.add), reads=[b_fw[3]], writes=[b_fw[3]])
                    S.op("act", lambda e: e.activation(bb[:], bb[:], AF.Sqrt), reads=[b_fw[3]], writes=[b_fw[3]])
                    S.op("dve", lambda e: e.tensor_tensor(ig[:], ig[:], xc[:], ALU.mult), reads=[b_fw[1], b_fw[0]], writes=[b_fw[1]])
                    S.op("dve", lambda e: e.tensor_tensor(bb[:], bb[:], ig[:], ALU.mult), reads=[b_fw[3], b_fw[1]], writes=[b_fw[3]])
                    S.op("dve", lambda e, i=i: e.tensor_tensor_scan(xc[:], aa[:], bb[:], carry[:, i:i + 1], ALU.mult, ALU.add),
                         reads=[b_fw[2], b_fw[3], b_carry], writes=[b_fw[0]])
                    S.op("dve", lambda e, i=i: e.tensor_copy(carry[:, i:i + 1], xc[:, 511:512]), reads=[b_fw[0]], writes=[b_carry])
                    S.op("dve", lambda e, i=i: e.tensor_tensor(yT[:, 2 + i, :], xc[:], gateT[:, i, :], ALU.mult), reads=[b_fw[0], b_gate], writes=[b_yT])

                S.skipping = 'c' in SKIP
                phs = [nextbank(), nextbank()]
                for g in range(2):
                    ph = phs[g]
                    for kv, roll in ((0, None), (1, None)):
                        col = kv * 32
                        for l in range(32):
                            S.op("pe", lambda e, ph=ph, kv=kv, g=g, l=l, roll=roll, col=col: e.matmul(
                                ph[0][:, col:col + 32], w1_sb[:, kv * 32 + l, :], rz[:, kv * 2 + g, l:l + 16 * 31 + 1:16],
                                start=(kv == 0 and l == 0), stop=(kv == 1 and l == 31), skip_group_check=True), reads=[b_cmpw, b_roll], writes=[ph[1]])
                    for kv in range(2):
                        S.op("act", lambda e, ph=ph, kv=kv, g=g: e.activation(hid[:, kv * 2 + g, 32:64], ph[0][:, kv * 32:kv * 32 + 32],
                                                                              AF.Gelu_apprx_tanh, bias=b1c[:, kv:kv + 1]), reads=[ph[1], b_cmpw], writes=[b_hid])
                S.op("dve", lambda e: e.tensor_copy(rz[:, :, 0:16], rz[:, :, 512:528]), reads=[b_roll], writes=[b_roll])
                pk = nextbank()
                for g in range(2):
                    S.op("pe", lambda e, pk=pk, g=g: e.matmul(pk[0][:, 0:32], w2k_pad[:, g, :], hid[:, g, 32:64], start=(g == 0), stop=(g == 1)),
                         reads=[b_cmpw, b_hid], writes=[pk[1]])
                S.op("act", lambda e, pk=pk, T=T: e.activation(kcmpT[:, 32 * T:32 * T + 32], pk[0][:, 0:32], AF.Identity, bias=b2k[:, 0:1]),
                     reads=[pk[1], b_cmpw], writes=[b_kcmp])
                a4 = T % 4
                stt = T // 4
                S.op("dve", lambda e: e.memset(hidv[:], 0.0), reads=[b_hidv], writes=[b_hidv])
                S.op("dve", lambda e, a4=a4: e.tensor_copy(hidv[:, :, 32 * a4:32 * a4 + 32], hid[:, 2:4, 32:64]), reads=[b_hid, b_hidv], writes=[b_hidv])
                pvv = nextbank()
                for g in range(2):
                    S.op("pe", lambda e, pvv=pvv, g=g: e.matmul(pvv[0][:, g * 64:(g + 1) * 64], hidv[:, g, :], w2v_sb[:], start=(g == 0), stop=False,
                                                                  skip_group_check=True), reads=[b_cmpw, b_hidv], writes=[pvv[1]])
                    S.op("pe", lambda e, pvv=pvv, g=g, a4=a4: e.matmul(pvv[0][:, g * 64:(g + 1) * 64], onesp[0:1, a4, :], b2v_row[0:1, :], start=False, stop=(g == 1),
                                                                      skip_group_check=True), reads=[cb], writes=[pvv[1]])
                for g in range(2):
                    S.op("dve", lambda e, pvv=pvv, g=g, stt=stt: e.tensor_tensor(cv_aug[:, stt, g, 64:128], cv_aug[:, stt, g, 64:128], pvv[0][:, g * 64:(g + 1) * 64], ALU.add),
                         reads=[pvv[1], b_cv], writes=[b_cv])

                S.skipping = 'n' in SKIP
                nslot = 32 * (T + 1)
                stiles = [(0, min(nslot, 128))] + ([(1, nslot - 128)] if nslot > 128 else [])
                for g in range(2):
                    base = 64 * g
                    bs_ = slice(base, base + 64)
                    for r in range(4):
                        h = 4 * g + r
                        ets = []
                        for (stt_, M) in stiles:
                            pb = nextbank()
                            S.op("pe", lambda e, pb=pb, stt_=stt_, M=M, r=r, g=g: e.matmul(pb[0][0:M, 0:512], kcmpT[:, stt_ * 128:stt_ * 128 + M], qz[:, 4 * g + r, :],
                                                                                             start=True, stop=True), reads=[b_kcmp, b_qT], writes=[pb[1]])
                            bi = lgc[0] % 2; lgc[0] += 1
                            S.dma("sp", bct[bi][0:M, :], D["bc"][h, stt_ * 128:stt_ * 128 + M, t0:t0 + 512], writes=[b_bct[bi]])
                            S.op("dve", lambda e, pb=pb, bi=bi, M=M: e.scalar_tensor_tensor(out=lg[bi][0:M, :], in0=pb[0][0:M, 0:512], scalar=0.125, in1=bct[bi][0:M, :],
                                                                                            op0=ALU.mult, op1=ALU.add), reads=[pb[1], b_bct[bi]], writes=[b_lg[bi]])
                            pi_ = ptc[0] % 3; ptc[0] += 1
                            S.op("act", lambda e, bi=bi, pi_=pi_, M=M: e.activation(PT[pi_][0:M, :], lg[bi][0:M, :], AF.Exp), reads=[b_lg[bi]], writes=[b_PT[pi_]])
                            ets.append((pi_, stt_, M))
                        cbk = [PSB[5], PSB[6]]
                        for s in range(4):
                            bk = cbk[s // 2]
                            co = (s % 2) * 129
                            for j, (pi_, stt_, M) in enumerate(ets):
                                S.op("pe", lambda e, bk=bk, co=co, pi_=pi_, stt_=stt_, M=M, s=s, j=j, g=g, ets=ets: e.matmul(
                                    bk[0][:, co:co + 129], PT[pi_][0:M, s * 128:(s + 1) * 128], cv_aug[0:M, stt_, g, :],
                                    start=(s % 2 == 0 and j == 0), stop=(s % 2 == 1 and j == len(ets) - 1), skip_group_check=True),
                                    reads=[b_PT[pi_], b_cv], writes=[bk[1]])
                        for s in range(4):
                            bk = cbk[s // 2]
                            co = (s % 2) * 129
                            S.op("dve", lambda e, bk=bk, co=co: e.tensor_scalar(z4[:, 0:1], bk[0][:, co + 128:co + 129], 1e-30, None, ALU.max), reads=[bk[1]], writes=[b_z4])
                            S.op("dve", lambda e: e.reciprocal(z4[:, 0:1], z4[:, 0:1]), reads=[b_z4], writes=[b_z4])
                            if r == 0:
                                S.op("dve", lambda e, bk=bk, co=co, s=s: e.tensor_scalar(impS[:, s, :], bk[0][:, co:co + 64], z4[:, 0:1], None, ALU.mult),
                                     reads=[bk[1], b_z4], writes=[b_imp])
                            else:
                                S.op("dve", lambda e, bk=bk, co=co, s=s: e.scalar_tensor_tensor(out=impS[:, s, :], in0=bk[0][:, co:co + 64], scalar=z4[:, 0:1], in1=impS[:, s, :],
                                                                                                op0=ALU.mult, op1=ALU.add), reads=[bk[1], b_z4, b_imp], writes=[b_imp])
                            S.op("dve", lambda e, bk=bk, co=co, s=s, h=h: e.tensor_scalar(ycomb[:, s, h * 64:(h + 1) * 64], bk[0][:, co + 64:co + 128], z4[:, 0:1], sg[:, s, h:h + 1],
                                                                                          ALU.mult, ALU.mult), reads=[bk[1], b_z4, b_sg], writes=[b_ycs[s]])
                    for s in range(4):
                        itile = T * 4 + s
                        off = 63 - 2 * itile
                        S.op("dve", lambda e, s=s, off=off: e.tensor_tensor(sc[:], impS[:, s, :], tmax_sb[:, off:off + 64], ALU.max), reads=[b_imp, cb], writes=[b_sc])
                        S.op("dve", lambda e, off=off: e.tensor_tensor(sc[:], sc[:], tmin_sb[:, off:off + 64], ALU.min), reads=[b_sc, cb], writes=[b_sc])
                        S.op("dve", lambda e: e.memset(sc[:, 0:1], 1e4), reads=[b_sc], writes=[b_sc])
                        S.op("dve", lambda e: e.max(out=m8[:, 0:8], in_=sc[:]), reads=[b_sc], writes=[b_sc])
                        S.op("dve", lambda e: e.match_replace(out=sc2[:], in_to_replace=m8[:, 0:8], in_values=sc[:], imm_value=-3.0e38), reads=[b_sc], writes=[b_sc])
                        S.op("dve", lambda e: e.max(out=m8[:, 8:16], in_=sc2[:]), reads=[b_sc], writes=[b_sc])
                        S.op("dve", lambda e, g=g: e.tensor_scalar(nmp[g][:, 64 * (1 - g):64 * (1 - g) + 64], sc[:], m8[:, 15:16], NEGM, ALU.is_lt, ALU.mult), reads=[b_sc], writes=[b_sc])
                        S.op("pe", lambda e, g=g: e.transpose(pt[:, 0:128], nmp[g][:], identb[:]), reads=[b_sc, b_const], writes=[b_pt])
                        S.op("act", lambda e, g=g, s=s: e.activation(nmz[g][:, s * 128:(s + 1) * 128], pt[:, 0:128], AF.Identity), reads=[b_pt], writes=[b_nmT])
                    wjobs, sjobs = [], []

                    def mk_hook(r_, g=g):
                        def mask_hook():
                            oh = slice(64, 128) if g == 0 else slice(0, 64)
                            S.op("pool", lambda e: e.tensor_copy(qz[oh, 4 * g + r_, :], nmz[g][oh, :]), reads=[b_nmT, b_qT], writes=[b_qT])
                        return mask_hook
                    for r in range(4):
                        h = 4 * g + r
                        selb, winb = PSB[3], PSB[4]
                        st_ = {}

                        def mk_sel(kt, gsc, b_gsc, h=h, g=g, selb=selb, first=False, load=False, hook=None, post=None):
                            Dd = t0 - 128 * kt
                            f0 = max(0, -Dd)
                            J = {}

                            def A():
                                if hook is not None:
                                    hook()
                                if load:
                                    S.dma("pool", gsc[:], D["gs"][h], writes=[b_gsc])
                                pb = nextbank()
                                J["pb"] = pb
                                S.op("pe", lambda e: e.matmul(pb[0][:, f0:512], ksE[g][:, kt * 128:(kt + 1) * 128], qz[:, h, f0:512], start=True, stop=True),
                                     reads=[b_ksT, b_qT], writes=[pb[1]])

                            def B():
                                pb = J["pb"]
                                pi_ = ptc[0] % 3; ptc[0] += 1
                                J["pi"] = pi_
                                if Dd <= 896:
                                    bi = lgc[0] % 2; lgc[0] += 1
                                    x0 = Dd + 384
                                    S.op("dve", lambda e: e.scalar_tensor_tensor(out=lg[bi][:, f0:512], in0=pb[0][:, f0:512], scalar=0.125,
                                                                                 in1=gsc[:, x0 + f0:x0 + 512], op0=ALU.mult, op1=ALU.add),
                                         reads=[pb[1], b_gsc], writes=[b_lg[bi]])
                                    S.op("act", lambda e: e.activation(PT[pi_][:, f0:512], lg[bi][:, f0:512], AF.Exp), reads=[b_lg[bi]], writes=[b_PT[pi_]])
                                else:
                                    S.op("act", lambda e: e.activation(PT[pi_][:, 0:512], pb[0][:, 0:512], AF.Exp, bias=c31_sb[:, h:h + 1], scale=0.125),
                                         reads=[pb[1], cb], writes=[b_PT[pi_]])

                            def C():
                                pi_ = J["pi"]
                                S.op("pe", lambda e: e.matmul(selb[0][0:65, f0:512], vs_aug[:, kt, g, :], PT[pi_][:, f0:512],
                                                              start=first, stop=False, skip_group_check=True),
                                     reads=[b_PT[pi_], b_vs], writes=[selb[1]])
                            return (A, B, C, post)

                        def mk_win(kt, gsc, b_gsc, h=h, g=g, winb=winb, first=False, post=None, load=False):
                            Dd = t0 - 128 * kt
                            lo = max(0, -Dd)
                            hi = min(512, 639 - Dd)
                            J = {}

                            def A():
                                if load:
                                    S.dma("pool", gsc[:], D["gs"][h], writes=[b_gsc])
                                pb = nextbank()
                                J["pb"] = pb
                                S.op("pe", lambda e: e.matmul(pb[0][:, lo:hi], kwT[:, kt * 128:(kt + 1) * 128], qz[:, h, lo:hi], start=True, stop=True),
                                     reads=[b_kwT, b_qT], writes=[pb[1]])

                            def B():
                                pb = J["pb"]
                                bi = lgc[0] % 2; lgc[0] += 1
                                x0 = Dd + 384
                                S.op("dve", lambda e: e.scalar_tensor_tensor(out=lg[bi][:, lo:hi], in0=pb[0][:, lo:hi], scalar=0.125,
                                                                             in1=gsc[:, x0 + lo:x0 + hi], op0=ALU.mult, op1=ALU.add),
                                     reads=[pb[1], b_gsc], writes=[b_lg[bi]])
                                if Dd >= 128:
                                    S.op("dve", lambda e: e.tensor_tensor(lg[bi][:, lo:hi], lg[bi][:, lo:hi], m512_sb[:, Dd - 128 + lo:Dd - 128 + hi], ALU.add),
                                         reads=[b_lg[bi], cb], writes=[b_lg[bi]])
                                pi_ = ptc[0] % 3; ptc[0] += 1
                                J["pi"] = pi_
                                S.op("act", lambda e: e.activation(PT[pi_][:, lo:hi], lg[bi][:, lo:hi], AF.Exp), reads=[b_lg[bi]], writes=[b_PT[pi_]])

                            def C():
                                pi_ = J["pi"]
                                S.op("pe", lambda e: e.matmul(winb[0][0:65, lo:hi], vw_aug[:, kt, g, :], PT[pi_][:, lo:hi],
                                                              start=first, stop=False, skip_group_check=True),
                                     reads=[b_PT[pi_], b_vw], writes=[winb[1]])
                            return (A, B, C, post)

                        def mk_post(which_, h=h, selb=selb, winb=winb):
                            def post():
                                acc, goff = ((selb, 8), (winb, 16))[which_]
                                S.op("act", lambda e: e.activation(bct[0][0:65, :], acc[0][0:65, 0:512], AF.Identity), reads=[acc[1]], writes=[b_bct[0]])
                                bk = nextbank()
                                for s in range(4):
                                    S.op("pe", lambda e, s=s: e.transpose(bk[0][:, s * 65:(s + 1) * 65], bct[0][0:65, s * 128:(s + 1) * 128], identf[0:65, 0:65]),
                                         reads=[b_bct[0], cb], writes=[bk[1]])
                                if dbg and T < 2:
                                    S.op("act", lambda e: e.activation(lg[0][:, 0:260], bk[0][:, 0:260], AF.Identity), reads=[bk[1]], writes=[b_lg[0]])
                                    S.dma("sp", ydR[T, h, which_], lg[0][:, 0:260], reads=[b_lg[0]])
                                for s in range(4):
                                    S.op("dve", lambda e, s=s: e.tensor_scalar(z4[:, 0:1], bk[0][:, s * 65 + 64:s * 65 + 65], 1e-30, None, ALU.max), reads=[bk[1]], writes=[b_z4])
                                    S.op("dve", lambda e: e.reciprocal(z4[:, 0:1], z4[:, 0:1]), reads=[b_z4], writes=[b_z4])
                                    S.op("dve", lambda e, s=s: e.tensor_tensor(z4[:, 1:2], z4[:, 0:1], sg[:, s, goff + h:goff + h + 1], ALU.mult), reads=[b_z4, b_sg], writes=[b_z4])
                                    S.op("dve", lambda e, s=s: e.scalar_tensor_tensor(out=ycomb[:, s, h * 64:(h + 1) * 64], in0=bk[0][:, s * 65:s * 65 + 64], scalar=z4[:, 1:2],
                                                                                      in1=ycomb[:, s, h * 64:(h + 1) * 64], op0=ALU.mult, op1=ALU.add),
                                         reads=[bk[1], b_z4, b_ycs[s]], writes=[b_ycs[s]])
                            return post

                        kts = [4 * T] + [k for k in range(max(0, 4 * T - 4), 4 * T + 4) if k != 4 * T]
                        gi_ = gsr[0] % 2; gsr[0] += 1
                        for j_, kt in enumerate(kts):
                            wjobs.append(mk_win(kt, gs_cur[gi_], b_gs[gi_], first=(j_ == 0), load=(j_ == 0), post=(mk_post(1) if j_ == len(kts) - 1 else None)))
                        for kt in range(0, 4 * T + 4):
                            wjobs.append(mk_sel(kt, gs_cur[gi_], b_gs[gi_], first=(kt == 0), load=False,
                                                hook=(mk_hook(r) if kt == 0 else None), post=(mk_post(0) if kt == 4 * T + 3 else None)))
                    jobs = wjobs
                    nj = len(jobs)
                    for i_ in range(nj + 2):
                        if i_ < nj:
                            jobs[i_][0]()
                        if 0 <= i_ - 1 < nj:
                            jobs[i_ - 1][1]()
                        if 0 <= i_ - 2 < nj:
                            jobs[i_ - 2][2]()
                            if jobs[i_ - 2][3] is not None:
                                jobs[i_ - 2][3]()

                S.skipping = False
                if dbg and L == stop_after[0]:
                    for s in range(4):
                        tok = T * 4 + s
                        S.dma("pool", ydA[tok * 128:(tok + 1) * 128, :], ytok[:, s, :], reads=[b_ytok[s]])
                        S.dma("sp", ydC[tok * 128:(tok + 1) * 128, :], ycomb[:, s, :], reads=[b_ycs[s]])
                    for i in range(2):
                        S.dma("pool", ydB[i * 128:(i + 1) * 128, t0:t0 + 512], yT[:, 2 + i, :], reads=[b_yT])
                for s in range(4):
                    for c in (0, 1):
                        S.op("pe", lambda e, s=s, c=c: e.transpose(pt[:, c * 128:(c + 1) * 128], ytok[:, s, c * 128:(c + 1) * 128], identb[:]), reads=[b_ytok[s], b_const], writes=[b_pt])
                    S.op("act", lambda e, s=s: e.activation(yT[:, 0:2, s * 128:(s + 1) * 128], pt[:, 0:256].rearrange("p (c n) -> p c n", c=2), AF.Identity), reads=[b_pt], writes=[b_yT])
                    pf = nextbank()
                    for c in range(4):
                        S.op("pe", lambda e, s=s, c=c, pf=pf: e.transpose(pf[0][:, c * 128:(c + 1) * 128], ycomb[:, s, c * 128:(c + 1) * 128], identf[:]), reads=[b_ycs[s], cb], writes=[pf[1]])
                    S.op("act", lambda e, s=s, pf=pf: e.activation(yT[:, 4:8, s * 128:(s + 1) * 128], pf[0][:, 0:512].rearrange("p (c n) -> p c n", c=4), AF.Identity), reads=[pf[1]], writes=[b_yT])
                for s in range(4):
                    tok = T * 4 + s
                    banks = [PSB[5], PSB[6]]
                    for hf in range(2):
                        for c in range(8):
                            S.op("pe", lambda e, hf=hf, c=c, s=s: e.matmul(banks[hf][0][:, 0:512], yT[:, c, s * 128:(s + 1) * 128], wout_sb[:, c, hf * 512:(hf + 1) * 512],
                                                                           start=(c == 0), stop=(c == 7)), reads=[b_yT, b_wout], writes=[banks[hf][1]])
                    postnorm_residual(banks, src, dst, tok, hin[:, 0, :], b_hin[0], gpost, cb, junk, b_junk, small, b_small, ftmp, b_ftmp, True)

        seq = []
        cur = "x"
        for L in range(2):
            for ph in ("ffn1", "mix", "ffn2", "ple"):
                seq.append((L, ph))
        if stop_after is not None:
            seq = seq[:seq.index(stop_after) + 1]
        for i, (L, ph) in enumerate(seq):
            last = (i == len(seq) - 1)
            dst = "y" if last else ("hA" if cur != "hA" else "hB")
            if ph == "ffn1":
                ffn_phase(L, 0, cur, dst)
            elif ph == "ffn2":
                ffn_phase(L, 1, cur, dst)
            elif ph == "mix":
                mixer_phase(L, cur, dst)
            else:
                ple_phase(L, cur, dst)
            cur = dst
        S.finish()
        S.emit()
    return nc


_PROG = {}
LAST_RES = None


def prep_inputs(inputs, b):
    f = lambda a: np.ascontiguousarray(np.asarray(a, dtype=np.float32))
    c = host_constants()
    gs, bc, c31 = host_bias_tables(inputs["rel_bias"])
    m = {}
    m["x"] = f(inputs["x"][b])
    m["pT"] = f(np.transpose(np.asarray(inputs["p"])[:, b], (0, 2, 1)))
    m["ngB"] = f(np.broadcast_to(np.asarray(inputs["norm_g"])[:, :, None, :], (2, 8, 128, DM)))
    m["wg"] = f(inputs["ffn_w_gate"]); m["wu"] = f(inputs["ffn_w_up"]); m["wd"] = f(inputs["ffn_w_down"])
    m["w_in"] = f(inputs["w_in"]); m["w_out"] = f(inputs["w_out"])
    m["sguB"] = f(np.broadcast_to(np.asarray(inputs["sgu_norm_g"])[:, None, :], (2, 128, 256)))
    m["sgu_w"] = f(inputs["sgu_w"])
    m["sgu_bT"] = f(np.transpose(np.asarray(inputs["sgu_b"]), (0, 2, 1)))
    v2 = lambda a: f(np.transpose(np.asarray(a, np.float32).reshape(2, 2, 128), (0, 2, 1)))
    m["conv_wT"] = f(np.transpose(np.asarray(inputs["conv_w"], np.float32).reshape(2, 4, 2, 128), (0, 3, 2, 1)))
    m["conv_b"] = v2(inputs["conv_b"]); m["lru_wa"] = f(inputs["lru_wa"]); m["lru_ba"] = v2(inputs["lru_ba"])
    m["lru_wx"] = f(inputs["lru_wx"]); m["lru_bx"] = v2(inputs["lru_bx"]); m["lru_lam"] = v2(inputs["lru_lambda"])
    m["posT"] = f(np.transpose(np.asarray(inputs["cmp_pos"]), (0, 3, 1, 2)))
    m["cmp_w1"] = f(inputs["cmp_w1"])
    m["b1T"] = f(np.transpose(np.asarray(inputs["cmp_b1"]), (0, 2, 1)))
    m["cmp_w2"] = f(inputs["cmp_w2"])
    b2 = np.asarray(inputs["cmp_b2"], np.float32)
    m["b2kk"] = f(np.concatenate([b2[:, 0], b2[:, 0]], axis=1)[:, :, None])
    m["b2vB"] = f(np.broadcast_to(b2[:, 1][:, None, :], (2, 128, 64)))
    m["wpg"] = f(inputs["ple_w_gate"]); m["wpp"] = f(inputs["ple_w_proj"])
    for k in ("ident", "tril", "eall", "ovl", "m512", "tmax", "tmin"):
        m[k] = c[k]
    m["gs"] = gs; m["bc"] = bc; m["c31"] = c31
    return m


def kernel(**inputs):
    key = STOP_AFTER
    if key not in _PROG:
        _PROG[key] = build(STOP_AFTER)
    nc = _PROG[key]
    shared = prep_inputs(inputs, 0)
    in_maps = []
    for b in range(8):
        m = dict(shared)
        m["x"] = np.ascontiguousarray(np.asarray(inputs["x"][b], dtype=np.float32))
        m["pT"] = np.ascontiguousarray(np.transpose(np.asarray(inputs["p"])[:, b], (0, 2, 1)).astype(np.float32))
        in_maps.append(m)
    ncore = int(os.environ.get("KCORES", "8"))
    res = run_bass_kernel_spmd(nc, in_maps[:ncore], core_ids=list(range(ncore)))
    global LAST_RES
    LAST_RES = res.results
    outs = [np.asarray(r["y"], dtype=np.float32) for r in res.results]
    while len(outs) < 8:
        outs.append(np.zeros_like(outs[0]))
    return np.stack(outs, axis=0)
```

```python
import contextlib
import math
import os
import numpy as np
import concourse.bass as bass
import concourse.mybir as mybir
from concourse.bass_utils import run_bass_kernel_spmd

F32 = mybir.dt.float32
BF16 = mybir.dt.bfloat16
AF = mybir.ActivationFunctionType
ALU = mybir.AluOpType

S_LEN = 4096
DM = 1024
DFF = 2816
NIN = 2328
EPS = 1e-6
NEGM = -30000.0
GSW = 1792
SKIP = os.environ.get('MIXSKIP', '')
STOP_AFTER = None


class Buf:
    __slots__ = ("name", "w", "r")

    def __init__(self, name):
        self.name = name
        self.w = {}
        self.r = {}


class Sched:
    def __init__(self, nc, stack, n_dma_sems=48):
        self.nc = nc
        self.ekeys = ["pe", "dve", "act", "pool", "sp"]
        self.prog = {k: [] for k in self.ekeys}
        self.sems = {}
        for k in self.ekeys:
            self.sems[k] = stack.enter_context(nc.semaphore("s_" + k))
        self.cnt = {k: 0 for k in self.ekeys}
        self.dsem_keys = []
        for i in range(n_dma_sems):
            k = "d%d" % i
            self.sems[k] = stack.enter_context(nc.semaphore("s_" + k))
            self.cnt[k] = 0
            self.dsem_keys.append(k)
        self.drr = 0
        self.waited = {k: {} for k in self.ekeys}
        self.sb_off = 16512
        self.uid = 0
        self.skipping = False

    def sb(self, name, shape, dtype, align=64):
        nbytes = int(np.prod(shape[1:])) * mybir.dt.size(dtype)
        off = (self.sb_off + align - 1) // align * align
        self.uid += 1
        t = self.nc.alloc_sbuf_tensor_at("%s_%d" % (name, self.uid), list(shape), dtype, offset=off)
        self.sb_off = off + nbytes
        assert self.sb_off <= 229376, (name, self.sb_off)
        return t

    def mark(self):
        return self.sb_off

    def release(self, m):
        self.sb_off = m

    def _wait(self, ek, ev):
        semkey, val = ev
        if semkey == ek and ek == "pe":
            return
        if self.waited[ek].get(semkey, 0) >= val:
            return
        self.waited[ek][semkey] = val
        sem = self.sems[semkey]
        self.prog[ek].append(lambda e, sem=sem, val=val: e.wait_ge(sem, val))

    def _deps(self, ek, reads, writes):
        deps = {}
        for b in reads:
            for k, v in b.w.items():
                deps[k] = max(deps.get(k, 0), v)
        for b in writes:
            for k, v in b.w.items():
                deps[k] = max(deps.get(k, 0), v)
            for k, v in b.r.items():
                deps[k] = max(deps.get(k, 0), v)
        for k, v in deps.items():
            self._wait(ek, (k, v))

    def _record(self, ev, reads, writes):
        for b in reads:
            b.r[ev[0]] = max(b.r.get(ev[0], 0), ev[1])
        for b in writes:
            b.w[ev[0]] = max(b.w.get(ev[0], 0), ev[1])
            b.r = {}

    def op(self, ek, fn, reads=(), writes=()):
        if self.skipping:
            return
        self._deps(ek, reads, writes)
        self.cnt[ek] += 1
        sem = self.sems[ek]
        self.prog[ek].append(lambda e, fn=fn, sem=sem: fn(e).then_inc(sem, 1))
        self._record((ek, self.cnt[ek]), reads, writes)

    def dma(self, qk, out, in_, reads=(), writes=(), **kw):
        if self.skipping:
            return
        self._deps(qk, reads, writes)
        sk = self.dsem_keys[self.drr % len(self.dsem_keys)]
        self.drr += 1
        if self.cnt[sk] > 0:
            self._wait(qk, (sk, self.cnt[sk]))
        self.cnt[sk] += 16
        sem = self.sems[sk]
        self.prog[qk].append(
            lambda e, out=out, in_=in_, sem=sem, kw=kw: e.dma_start(out=out, in_=in_, **kw).then_inc(sem, 16))
        self._record((sk, self.cnt[sk]), reads, writes)

    def barrier(self):
        for ek in self.ekeys:
            for k, v in self.cnt.items():
                if v > 0 and k != ek:
                    self._wait(ek, (k, v))

    def finish(self):
        for k, v in self.cnt.items():
            if v > 0 and k != "sp":
                self._wait("sp", (k, v))

    def emit(self):
        with self.nc.Block() as block:
            @block.tensor
            def _(e):
                for f in self.prog["pe"]:
                    f(e)

            @block.vector
            def _(e):
                for f in self.prog["dve"]:
                    f(e)

            @block.scalar
            def _(e):
                for f in self.prog["act"]:
                    f(e)

            @block.gpsimd
            def _(e):
                for f in self.prog["pool"]:
                    f(e)

            @block.sync
            def _(e):
                for f in self.prog["sp"]:
                    f(e)


def t5_bucket_np(d):
    n = np.maximum(d, 0)
    nf = np.maximum(n, 16).astype(np.float32)
    large = 16 + (np.log(nf / np.float32(16)) / np.float32(math.log(1024 / 16)) * np.float32(16)).astype(np.int32)
    large = np.minimum(large, 31)
    return np.where(n < 16, n, large)


def t5_bucket_jax_exact(d):
    import jax
    import jax.numpy as jnp
    with jax.default_device(jax.devices("cpu")[0]):
        n = jnp.maximum(jnp.asarray(d, jnp.int32), 0)
        nf = jnp.maximum(n, 16).astype(jnp.float32)
        large = 16 + (jnp.log(nf / 16) / math.log(1024 / 16) * 16).astype(jnp.int32)
        large = jnp.minimum(large, 31)
        return np.asarray(jnp.where(n < 16, n, large))


_CONST = {}


def host_constants():
    if _CONST:
        return _CONST
    c = _CONST
    c["ident"] = np.eye(128, dtype=np.float32)
    c["tril"] = np.tril(np.ones((128, 128), np.float32))
    e = np.zeros((128, 4096), np.float32)
    for k in range(4096):
        e[k // 64, k] = 1.0
        e[64 + k // 64, k] = 1.0
    c["eall"] = e
    ov = np.zeros((256, 64), np.float32)
    for s in range(1, 256):
        n = s - 1
        cs = 16 * n
        for j in range(64):
            o = min(cs + 32, 64 * j + 64) - max(cs, 64 * j)
            if o > 0:
                ov[s, j] = o / 32.0
    c["ovl"] = ov
    try:
        bk = t5_bucket_jax_exact(np.arange(0, 4200))
    except Exception:
        bk = t5_bucket_np(np.arange(0, 4200))
    c["bucket"] = bk
    tmax = np.zeros((128, 127), np.float32)
    tmin = np.full((128, 127), 1e5, np.float32)
    for p in range(128):
        hi = 1 if p >= 64 else 0
        for xx in range(127):
            rel = xx - 63 - hi
            if rel in (0, -1):
                tmax[p, xx] = 1e4
            if rel > 0:
                tmin[p, xx] = -1.0
    c["tmax"] = tmax
    c["tmin"] = tmin
    m = np.zeros((128, 896), np.float32)
    for p in range(128):
        xx = np.arange(896)
        m[p, (xx + 128 - p) >= 512] = NEGM
    c["m512"] = m
    return c


def host_bias_tables(rel_bias):
    c = host_constants()
    bk = c["bucket"]
    rb = np.asarray(rel_bias, np.float32)
    p = np.arange(128)[:, None]
    xx = np.arange(GSW)[None, :]
    d = xx - 384 - p
    ok = d >= 0
    g = rb[bk[np.maximum(d, 0)]]
    g = np.where(ok[:, :, None], g, np.float32(NEGM))
    gs = np.ascontiguousarray(np.transpose(g, (2, 0, 1))).astype(np.float32)
    s = np.arange(256)[:, None]
    t = np.arange(4096)[None, :]
    dc = t - (16 * (s - 1) + 31)
    okc = (dc >= 0) & (s >= 1)
    b = rb[bk[np.maximum(dc, 0)]]
    b = np.where(okc[:, :, None], b, np.float32(NEGM))
    bc = np.ascontiguousarray(np.transpose(b, (2, 0, 1))).astype(np.float32)
    c31 = np.ascontiguousarray(np.tile(rb[31][None, :], (128, 1))).astype(np.float32)
    return gs, bc, c31


def build(stop_after=None):
    nc = bass.Bass("TRN2", target_bir_lowering=False)
    D = {}

    def din(name, shape):
        D[name] = nc.dram_tensor(name, list(shape), F32, kind="ExternalInput").ap()
        return D[name]

    din("x", [S_LEN, DM]); din("pT", [2, 256, S_LEN]); din("ngB", [2, 8, 128, DM])
    din("wg", [2, 2, DM, DFF]); din("wu", [2, 2, DM, DFF]); din("wd", [2, 2, DFF, DM])
    din("w_in", [2, DM, NIN]); din("w_out", [2, DM, DM])
    din("sguB", [2, 128, 256]); din("sgu_w", [2, 4, 128, 128]); din("sgu_bT", [2, 128, 4])
    din("conv_wT", [2, 128, 2, 4]); din("conv_b", [2, 128, 2]); din("lru_wa", [2, 4, 64, 64]); din("lru_ba", [2, 128, 2])
    din("lru_wx", [2, 4, 64, 64]); din("lru_bx", [2, 128, 2]); din("lru_lam", [2, 128, 2])
    din("posT", [2, 64, 2, 32]); din("cmp_w1", [2, 2, 2048, 128]); din("b1T", [2, 128, 2])
    din("cmp_w2", [2, 2, 128, 64]); din("b2kk", [2, 128, 1]); din("b2vB", [2, 128, 64])
    din("wpg", [2, DM, DM]); din("wpp", [2, 256, DM])
    din("ident", [128, 128]); din("tril", [128, 128]); din("eall", [128, 4096]); din("ovl", [256, 64])
    din("gs", [8, 128, GSW]); din("m512", [128, 896]); din("bc", [8, 256, S_LEN])
    din("tmax", [128, 127]); din("tmin", [128, 127]); din("c31", [128, 8])
    y = nc.dram_tensor("y", [S_LEN, DM], F32, kind="ExternalOutput").ap()
    dbg = stop_after is not None and stop_after[1] == "mix"
    if dbg:
        ydA = nc.dram_tensor("ydA", [S_LEN, 256], F32, kind="ExternalOutput").ap()
        ydB = nc.dram_tensor("ydB", [256, S_LEN], F32, kind="ExternalOutput").ap()
        ydC = nc.dram_tensor("ydC", [S_LEN, 512], F32, kind="ExternalOutput").ap()
        ydR = nc.dram_tensor("ydR", [2, 8, 2, 128, 260], F32, kind="ExternalOutput").ap()
    hA = nc.dram_tensor("hA", [S_LEN, DM], F32).ap()
    hB = nc.dram_tensor("hB", [S_LEN, DM], F32).ap()

    with contextlib.ExitStack() as st:
        S = Sched(nc, st)
        PSB = []
        for i in range(7):
            PSB.append((st.enter_context(nc.psum_tensor("psb%d" % i, [128, 512], F32)), Buf("psb%d" % i)))
        pt, b_pt = st.enter_context(nc.psum_tensor("pst", [128, 1024], BF16)), Buf("pst")

        identb = S.sb("identb", [128, 128], BF16); b_const = Buf("const")
        S.dma("pool", identb[:], D["ident"], writes=[b_const])
        ones1 = S.sb("ones1", [128, 1], F32)
        S.op("dve", lambda e: e.memset(ones1[:], 1.0), writes=[b_const])
        glob_mark = S.mark()

        def hbufs(name):
            return [Buf("%s%d" % (name, i)) for i in range(32)]

        HB = {"x": hbufs("x"), "hA": hbufs("hA"), "hB": hbufs("hB"), "y": hbufs("y")}
        HAP = {"x": D["x"], "hA": hA, "hB": hB, "y": y}

        def rstd_from_ss(ss_ap, out_ap, n, reads, writes, tmpbuf):
            S.op("dve", lambda e: e.tensor_scalar(out_ap, ss_ap, 1.0 / n, EPS, ALU.mult, ALU.add), reads=reads, writes=writes)
            S.op("act", lambda e: e.activation(out_ap, out_ap, AF.Sqrt), reads=writes, writes=writes)
            S.op("dve", lambda e: e.reciprocal(out_ap, out_ap), reads=writes, writes=writes)

        def prenorm_xT(src, tok, hin_t, b_hin, gB, b_g, junk, b_junk, small, b_small, xn, b_xn, xT, b_xT, s):
            S.dma("sp", hin_t, HAP[src][tok * 128:(tok + 1) * 128, :], reads=[HB[src][tok]], writes=[b_hin])
            S.op("act", lambda e: e.activation(junk[:], hin_t, AF.Square, accum_out=small[:, 0:1]), reads=[b_hin], writes=[b_junk, b_small])
            rstd_from_ss(small[:, 0:1], small[:, 1:2], float(DM), [b_small], [b_small], None)
            S.op("dve", lambda e: e.scalar_tensor_tensor(out=xn[:], in0=hin_t, scalar=small[:, 1:2], in1=gB, op0=ALU.mult, op1=ALU.mult),
                 reads=[b_hin, b_small, b_g], writes=[b_xn])
            for c in range(8):
                S.op("pe", lambda e, c=c: e.transpose(pt[:, c * 128:(c + 1) * 128], xn[:, c * 128:(c + 1) * 128], identb[:]),
                     reads=[b_xn, b_const], writes=[b_pt])
            S.op("act", lambda e: e.activation(xT[:, :, s * 128:(s + 1) * 128], pt[:].rearrange("p (c n) -> p c n", c=8), AF.Identity),
                 reads=[b_pt], writes=[b_xT])

        def postnorm_residual(banks, src, dst, tok, hin_t, b_hin, gpostB, b_g, junk, b_junk, small, b_small, ftmp, b_ftmp, reload):
            if reload:
                S.dma("sp", hin_t, HAP[src][tok * 128:(tok + 1) * 128, :], reads=[HB[src][tok]], writes=[b_hin])
            for hf in range(2):
                S.op("act", lambda e, hf=hf: e.activation(junk[:, 0:512], banks[hf][0][:, 0:512], AF.Square, accum_out=small[:, 2 + hf:3 + hf]),
                     reads=[banks[hf][1]], writes=[b_junk, b_small])
            S.op("dve", lambda e: e.tensor_tensor(small[:, 4:5], small[:, 2:3], small[:, 3:4], ALU.add), reads=[b_small], writes=[b_small])
            rstd_from_ss(small[:, 4:5], small[:, 5:6], float(DM), [b_small], [b_small], None)
            for hf in range(2):
                S.op("dve", lambda e, hf=hf: e.scalar_tensor_tensor(out=ftmp[:, 0:512], in0=banks[hf][0][:, 0:512], scalar=small[:, 5:6],
                                                                     in1=gpostB[:, hf * 512:(hf + 1) * 512], op0=ALU.mult, op1=ALU.mult),
                     reads=[banks[hf][1], b_small, b_g], writes=[b_ftmp])
                S.op("pool", lambda e, hf=hf: e.tensor_tensor(hin_t[:, hf * 512:(hf + 1) * 512], hin_t[:, hf * 512:(hf + 1) * 512], ftmp[:, 0:512], ALU.add),
                     reads=[b_hin, b_ftmp], writes=[b_hin])
            S.dma("sp", HAP[dst][tok * 128:(tok + 1) * 128, :], hin_t, reads=[b_hin], writes=[HB[dst][tok]])

        def ffn_phase(L, which, src, dst):
            S.barrier()
            S.release(glob_mark)
            wg_sb = S.sb("wg", [128, 8, DFF], BF16); wu_sb = S.sb("wu", [128, 8, DFF], BF16)
            wd_sb = S.sb("wd", [128, 22, DM], BF16)
            b_wg, b_wu, b_wd = Buf("wg"), Buf("wu"), Buf("wd")
            gpre = S.sb("gpre", [128, DM], F32); gpost = S.sb("gpost", [128, DM], F32); b_g = Buf("g")
            hin = S.sb("hin", [128, 4, DM], F32); b_hin = [Buf("hin%d" % i) for i in range(4)]
            xT = S.sb("xT", [128, 8, 512], BF16); b_xT = Buf("xT")
            hT = S.sb("hT", [128, 22, 512], BF16); b_hT = Buf("hT")
            xn = S.sb("xn", [128, DM], BF16); b_xn = Buf("xn")
            junk = S.sb("junk", [128, DM], BF16); b_junk = Buf("junk")
            small = S.sb("small", [128, 8], F32); b_small = Buf("small")
            ftmp = S.sb("ftmp", [128, DM], F32); b_ftmp = Buf("ftmp")
            sgt = [S.sb("sgt%d" % i, [128, 512], BF16) for i in range(2)]; b_sgt = [Buf("sgt0"), Buf("sgt1")]
            ni = 4 * which
            S.dma("sp", gpre[:], D["ngB"][L, ni], writes=[b_g])
            S.dma("sp", gpost[:], D["ngB"][L, ni + 1], writes=[b_g])
            S.op("dve", lambda e: e.tensor_scalar(gpost[:], gpost[:], 0.5, None, ALU.mult), reads=[b_g], writes=[b_g])
            NCB = 4
            CW = DFF // NCB
            cbs = [(0, 768), (768, 1536), (1536, 2304), (2304, 2816)]
            b_wgc = [Buf("wgc%d" % i) for i in range(len(cbs))]
            b_wuc = [Buf("wuc%d" % i) for i in range(len(cbs))]
            b_wdc = [Buf("wdc%d" % i) for i in range(22)]
            for bi_, (c0, c1) in enumerate(cbs):
                for c in range(8):
                    S.dma("pool", wg_sb[:, c, c0:c1], D["wg"][L, which, c * 128:(c + 1) * 128, c0:c1], writes=[b_wgc[bi_]])
                    S.dma("pool", wu_sb[:, c, c0:c1], D["wu"][L, which, c * 128:(c + 1) * 128, c0:c1], writes=[b_wuc[bi_]])
            for c in range(22):
                S.dma("pool", wd_sb[:, c, :], D["wd"][L, which, c * 128:(c + 1) * 128, :], writes=[b_wdc[c]], max_dma_last_dim=4096)

            def cb_of(fc):
                for bi_, (c0, c1) in enumerate(cbs):
                    if c0 <= fc * 128 < c1:
                        return bi_
            pair = 0
            dbank = 0
            for T in range(8):
                for s in range(4):
                    prenorm_xT(src, T * 4 + s, hin[:, s, :], b_hin[s], gpre[:], b_g, junk, b_junk, small, b_small, xn, b_xn, xT, b_xT, s)
                for fc in range(22):
                    pg, bg = PSB[(pair % 2) * 2]
                    pu, bu = PSB[(pair % 2) * 2 + 1]
                    pair += 1
                    for (wsb, bw, ps_, bps) in ((wg_sb, b_wgc[cb_of(fc)], pg, bg), (wu_sb, b_wuc[cb_of(fc)], pu, bu)):
                        for kc in range(8):
                            S.op("pe", lambda e, wsb=wsb, ps_=ps_, kc=kc, fc=fc: e.matmul(ps_[:, 0:512], wsb[:, kc, fc * 128:(fc + 1) * 128], xT[:, kc, :],
                                                                                          start=(kc == 0), stop=(kc == 7)),
                                 reads=[bw, b_xT], writes=[bps])
                    sg_, bsg = sgt[fc % 2], b_sgt[fc % 2]
                    S.op("act", lambda e, sg_=sg_, pg=pg: e.activation(sg_[:], pg[:, 0:512], AF.Silu), reads=[bg], writes=[bsg])
                    S.op("dve", lambda e, sg_=sg_, pu=pu, fc=fc: e.tensor_tensor(hT[:, fc, :], sg_[:], pu[:, 0:512], ALU.mult),
                         reads=[bsg, bu], writes=[b_hT])
                for s in range(4):
                    banks = []
                    for hf in range(2):
                        pb = PSB[4 + dbank % 3]; dbank += 1
                        banks.append(pb)
                        for fc in range(22):
                            S.op("pe", lambda e, pb=pb, fc=fc, s=s, hf=hf: e.matmul(pb[0][:, 0:512], hT[:, fc, s * 128:(s + 1) * 128],
                                                                                    wd_sb[:, fc, hf * 512:(hf + 1) * 512], start=(fc == 0), stop=(fc == 21)),
                                 reads=[b_hT, b_wdc[fc]], writes=[pb[1]])
                    postnorm_residual(banks, src, dst, T * 4 + s, hin[:, s, :], b_hin[s], gpost, b_g, junk, b_junk, small, b_small, ftmp, b_ftmp, False)

        def ple_phase(L, src, dst):
            S.barrier()
            S.release(glob_mark)
            wpg_sb = S.sb("wpg", [128, 8, DM], BF16); wpp_sb = S.sb("wpp", [128, 2, DM], BF16)
            pT_sb = S.sb("pTs", [128, 2, S_LEN], BF16)
            b_w = Buf("plew")
            gpre = S.sb("gpre", [128, DM], F32); gpost = S.sb("gpost", [128, DM], F32); b_g = Buf("g")
            hin = S.sb("hin", [128, 2, DM], F32); b_hin = [Buf("hin0"), Buf("hin1")]
            def two(name, shape, dt):
                return [S.sb(name + str(i), shape, dt) for i in range(2)], [Buf(name + str(i)) for i in range(2)]
            xT2, b_xT2 = two("xT", [128, 8, 128], BF16)
            xn2, b_xn2 = two("xn", [128, DM], BF16)
            junk2, b_junk2 = two("junk", [128, DM], BF16)
            small2, b_small2 = two("small", [128, 8], F32)
            ftmp2, b_ftmp2 = two("ftmp", [128, DM], F32)
            sgf2, b_sgf2 = two("sgf", [128, DM], F32)
            u2, b_u2 = two("u", [128, DM], F32)
            S.dma("sp", gpre[:], D["ngB"][L, 6], writes=[b_g])
            S.dma("sp", gpost[:], D["ngB"][L, 7], writes=[b_g])
            for c in range(8):
                S.dma("pool", wpg_sb[:, c, :], D["wpg"][L, c * 128:(c + 1) * 128, :], writes=[b_w])
            for c in range(2):
                S.dma("pool", wpp_sb[:, c, :], D["wpp"][L, c * 128:(c + 1) * 128, :], writes=[b_w])
                for q in range(4):
                    S.dma("pool", pT_sb[:, c, q * 1024:(q + 1) * 1024], D["pT"][L, c * 128:(c + 1) * 128, q * 1024:(q + 1) * 1024], writes=[b_w])
            rbl = [0]

            def _tile(tok, xT, b_xT, xn, b_xn, junk, b_junk, small, b_small, ftmp, b_ftmp, sgf, b_sgf, u, b_u, hi_, bh):
                prenorm_xT(src, tok, hi_, bh, gpre[:], b_g, junk, b_junk, small, b_small, xn, b_xn, xT, b_xT, 0)
                gb = []
                pb_ = []
                for hf in range(2):
                    g_ = PSB[rbl[0] % 7]; rbl[0] += 1
                    p_ = PSB[rbl[0] % 7]; rbl[0] += 1
                    for kc in range(8):
                        S.op("pe", lambda e, g_=g_, kc=kc, hf=hf: e.matmul(g_[0][:, 0:512], xT[:, kc, :], wpg_sb[:, kc, hf * 512:(hf + 1) * 512],
                                                                           start=(kc == 0), stop=(kc == 7)), reads=[b_xT, b_w], writes=[g_[1]])
                    for c in range(2):
                        S.op("pe", lambda e, p_=p_, c=c, hf=hf, tok=tok: e.matmul(p_[0][:, 0:512], pT_sb[:, c, tok * 128:(tok + 1) * 128],
                                                                                  wpp_sb[:, c, hf * 512:(hf + 1) * 512], start=(c == 0), stop=(c == 1)),
                             reads=[b_w], writes=[p_[1]])
                    S.op("act", lambda e, g_=g_, hf=hf: e.activation(sgf[:, hf * 512:(hf + 1) * 512], g_[0][:, 0:512], AF.Sigmoid), reads=[g_[1]], writes=[b_sgf])
                    S.op("dve", lambda e, p_=p_, hf=hf: e.tensor_tensor(u[:, hf * 512:(hf + 1) * 512], sgf[:, hf * 512:(hf + 1) * 512], p_[0][:, 0:512], ALU.mult),
                         reads=[b_sgf, p_[1]], writes=[b_u])
                S.op("act", lambda e: e.activation(junk[:], u[:], AF.Square, accum_out=small[:, 4:5]), reads=[b_u], writes=[b_junk, b_small])
                rstd_from_ss(small[:, 4:5], small[:, 5:6], float(DM), [b_small], [b_small], None)
                S.op("dve", lambda e: e.scalar_tensor_tensor(out=ftmp[:], in0=u[:], scalar=small[:, 5:6], in1=gpost[:], op0=ALU.mult, op1=ALU.mult),
                     reads=[b_u, b_small, b_g], writes=[b_ftmp])
                S.op("pool", lambda e, hi_=hi_: e.tensor_tensor(hi_, hi_, ftmp[:], ALU.add), reads=[bh, b_ftmp], writes=[bh])
                S.dma("sp", HAP[dst][tok * 128:(tok + 1) * 128, :], hi_, reads=[bh], writes=[HB[dst][tok]])

            for tok in range(32):
                k_ = 0
                _tile(tok, xT2[k_], b_xT2[k_], xn2[k_], b_xn2[k_], junk2[k_], b_junk2[k_], small2[k_], b_small2[k_], ftmp2[k_], b_ftmp2[k_],
                      sgf2[k_], b_sgf2[k_], u2[k_], b_u2[k_], hin[:, tok % 2, :], b_hin[tok % 2])

        def mixer_phase(L, src, dst):
            S.barrier()
            S.release(glob_mark)
            cb = Buf("mixconst")
            win_sb = S.sb("win", [128, 8, NIN], BF16)
            wout_sb = S.sb("wout", [128, 8, DM], BF16)
            w1_sb = S.sb("w1", [128, 64, 128], BF16)
            w2k_pad = S.sb("w2kp", [128, 2, 128], BF16)
            w2v_sb = S.sb("w2v", [128, 64], BF16)
            posT_sb = S.sb("posT", [128, 2, 32], BF16)
            b1c = S.sb("b1c", [128, 2], F32)
            b2k = S.sb("b2k", [128, 1], F32)
            b2vB = S.sb("b2vB", [128, 64], F32)
            ksE = [S.sb("ksE%d" % i, [128, S_LEN], BF16) for i in range(2)]; b_ksT = Buf("ksT")
            kwT = S.sb("kwT", [128, S_LEN], BF16); b_kwT = Buf("kwT")
            vs_aug = S.sb("vsa", [128, 32, 2, 65], BF16); b_vs = Buf("vsa")
            vw_aug = S.sb("vwa", [128, 32, 2, 65], BF16); b_vw = Buf("vwa")
            kcmpT = S.sb("kcmpT", [128, 256], BF16); b_kcmp = Buf("kcmpT")
            cv_aug = S.sb("cva", [128, 2, 2, 129], BF16); b_cv = Buf("cva")
            gs_cur = [S.sb("gsc%d" % i, [128, GSW], BF16) for i in range(2)]; b_gs = [Buf("gs0"), Buf("gs1")]
            identf = S.sb("identf", [128, 128], F32)
            m512_sb = S.sb("m512", [128, 896], F32)
            c31_sb = S.sb("c31", [128, 8], F32)
            tmax_sb = S.sb("tmax", [128, 127], F32); tmin_sb = S.sb("tmin", [128, 127], F32)
            cw_sb = S.sb("cw", [128, 2, 4], F32); cbias = S.sb("cbias", [128, 2], F32)
            bda = S.sb("bda", [128, 2, 128], BF16); bdx = S.sb("bdx", [128, 2, 128], BF16)
            ba_sb = S.sb("ba", [128, 2], F32); bx_sb = S.sb("bx", [128, 2], F32); lamc = S.sb("lamc", [128, 2], F32)
            wsT = S.sb("wsT", [128, 4, 128], BF16)
            sguB = S.sb("sguB", [128, 256], F32); bsT = S.sb("bsT", [128, 4], F32)
            gpre = S.sb("gpre", [128, DM], F32); gpost = S.sb("gpost", [128, DM], F32)
            hin = S.sb("hin", [128, 2, DM], F32)[:, 0:1, :] if False else S.sb("hin", [128, 1, DM], F32); b_hin = [Buf("hin0"), Buf("hin0b")]; b_hin[1] = b_hin[0]
            xT = S.sb("xT", [128, 8, 512], BF16); b_xT = Buf("xT")
            xn = S.sb("xn", [128, DM], BF16); b_xn = Buf("xn")
            junk = xn; b_junk = b_xn
            small = S.sb("small", [128, 8], F32); b_small = Buf("small")
            ftmp = S.sb("ftmp", [128, 512], F32); b_ftmp = Buf("ftmp")
            qz = S.sb("qz", [128, 8, 512], BF16); b_qT = Buf("qz")
            rz = S.sb("rz", [128, 4, 528], BF16); b_roll = Buf("roll")
            xbT = S.sb("xbT", [128, 2, 516], F32); b_xb = Buf("xbT")
            gateT = S.sb("gateT", [128, 2, 512], BF16); b_gate = Buf("gateT")
            carry = S.sb("carry", [128, 2], F32); b_carry = Buf("carry")
            sg = S.sb("sg", [128, 4, 24], F32); b_sg = Buf("sg")
            uv = S.sb("uv", [128, 512], F32); b_uv = Buf("uv")
            avn = S.sb("avn", [128, 256], BF16); b_avn = Buf("avn")
            ytok = S.sb("ytok", [128, 4, 256], BF16); b_ytok = [Buf("ytok%d" % i) for i in range(4)]
            yT = S.sb("yT", [128, 8, 512], BF16); b_yT = Buf("yT")
            ycomb = S.sb("ycomb", [128, 4, 512], F32); b_ycs = [Buf("ycomb%d" % i) for i in range(4)]
            impS = S.sb("imp", [128, 4, 64], F32); b_imp = Buf("imp")
            lg = [S.sb("lg%d" % i, [128, 512], F32) for i in range(2)]; b_lg = [Buf("lg0"), Buf("lg1")]
            PT = [S.sb("PT%d" % i, [128, 512], BF16) for i in range(3)]; b_PT = [Buf("PT%d" % i) for i in range(3)]
            bct = [S.sb("bct0", [128, 512], F32)] * 2; b_bct = [Buf("bct0")] * 2
            nmz = [S.sb("nmz%d" % i, [128, 512], BF16) for i in range(2)]; b_nmT = Buf("nmT")
            nmp = [S.sb("nmp%d" % i, [128, 128], BF16) for i in range(2)]
            fw = [ycomb[:, i, :] for i in range(4)]; b_fw = b_ycs
            xcb = S.sb("xcb", [128, 512], BF16); b_xcb = Buf("xcb")
            hid = S.sb("hid", [128, 4, 64], BF16); b_hid = Buf("hid")
            hidv = S.sb("hidv", [128, 2, 128], BF16); b_hidv = Buf("hidv")
            onesp = S.sb("onesp", [1, 4, 128], BF16); b2v_row = S.sb("b2vr", [1, 64], BF16)
            sc = S.sb("sc", [128, 64], F32); sc2 = S.sb("sc2", [128, 64], F32); m8 = S.sb("m8", [128, 16], F32)
            nm = S.sb("nm", [128, 64], BF16); b_sc = Buf("sc")
            z4 = S.sb("z4", [128, 8], F32); b_z4 = Buf("z4")

            b_win, b_wout, b_cmpw = Buf("win"), Buf("wout"), Buf("cmpw")

            def cdma(q, out, in_, buf=None, **kw):
                S.dma(q, out, in_, writes=[buf if buf is not None else cb], **kw)
            W = D["w_in"][L]
            for c in range(8):
                rows = slice(c * 128, (c + 1) * 128)
                cdma("pool", win_sb[:, c, 0:1024], W[rows, 0:1024], buf=b_win)
                for r in range(4):
                    cdma("pool", win_sb[:, c, 1024 + r * 128:1024 + r * 128 + 64], W[rows, 1024 + r * 64:1024 + r * 64 + 64], buf=b_win)
                    cdma("pool", win_sb[:, c, 1024 + r * 128 + 64:1024 + r * 128 + 128], W[rows, 1024 + (4 + r) * 64:1024 + (4 + r) * 64 + 64], buf=b_win)
                cdma("pool", win_sb[:, c, 1536:NIN], W[rows, 1536:NIN], buf=b_win)
                cdma("pool", wout_sb[:, c, :], D["w_out"][L, rows, :], buf=b_wout)
            for kv in range(2):
                src_w1 = D["cmp_w1"][L, kv].rearrange("(l d) j -> d l j", d=64)
                for half in range(2):
                    for lq in range(4):
                        cdma("pool", w1_sb[half * 64:(half + 1) * 64, kv * 32 + lq * 8:kv * 32 + lq * 8 + 8, :], src_w1[:, lq * 8:(lq + 1) * 8, :], buf=b_cmpw)
            S.op("dve", lambda e: e.memset(w2k_pad[:], 0.0), writes=[b_cmpw])
            for g in range(2):
                cdma("pool", w2k_pad[:, g, g * 64:(g + 1) * 64], D["cmp_w2"][L, 0], buf=b_cmpw)
            cdma("pool", w2v_sb[:], D["cmp_w2"][L, 1], buf=b_cmpw)
            for half in range(2):
                cdma("pool", posT_sb[half * 64:(half + 1) * 64, :, :], D["posT"][L], buf=b_cmpw)
            cdma("sp", b1c[:], D["b1T"][L]); cdma("sp", b2k[:], D["b2kk"][L]); cdma("sp", b2vB[:], D["b2vB"][L], buf=b_cmpw)
            cdma("sp", identf[:], D["ident"])
            cdma("sp", m512_sb[:], D["m512"])
            for q in range(4):
                S.dma("pool", ksE[0][64:128, q * 1024:(q + 1) * 1024], D["eall"][0:64, q * 1024:(q + 1) * 1024], writes=[b_ksT])
                S.dma("pool", ksE[1][0:64, q * 1024:(q + 1) * 1024], D["eall"][0:64, q * 1024:(q + 1) * 1024], writes=[b_ksT])
            cdma("sp", c31_sb[:], D["c31"]); cdma("sp", tmax_sb[:], D["tmax"]); cdma("sp", tmin_sb[:], D["tmin"])
            cdma("sp", cw_sb[:], D["conv_wT"][L])
            cdma("sp", cbias[:], D["conv_b"][L])
            cdma("sp", ba_sb[:], D["lru_ba"][L])
            cdma("sp", bx_sb[:], D["lru_bx"][L])
            cdma("sp", lamc[:], D["lru_lam"][L])
            S.op("dve", lambda e: e.memset(bda[:], 0.0), writes=[cb])
            S.op("dve", lambda e: e.memset(bdx[:], 0.0), writes=[cb])
            for gi in range(4):
                i, a = gi // 2, gi % 2
                cdma("pool", bda[a * 64:(a + 1) * 64, i, a * 64:(a + 1) * 64], D["lru_wa"][L, gi])
                cdma("pool", bdx[a * 64:(a + 1) * 64, i, a * 64:(a + 1) * 64], D["lru_wx"][L, gi])
            cdma("sp", sguB[:], D["sguB"][L]); cdma("sp", bsT[:], D["sgu_bT"][L])
            cdma("sp", gpre[:], D["ngB"][L, 2]); cdma("sp", gpost[:], D["ngB"][L, 3])
            S.op("act", lambda e: e.activation(lamc[:], lamc[:], AF.Exp, scale=-1.0), reads=[cb], writes=[cb])
            S.op("act", lambda e: e.activation(lamc[:], lamc[:], AF.Ln, bias=ones1[:, 0:1]), reads=[cb, b_const], writes=[cb])
            S.op("dve", lambda e: e.tensor_scalar(lamc[:], lamc[:], -8.0, None, ALU.mult), reads=[cb], writes=[cb])
            trl = fw[0]
            S.dma("sp", trl[:, 0:128], D["tril"], writes=[b_fw[0]])
            for g in range(4):
                S.dma("sp", fw[1][:, 0:128], D["sgu_w"][L, g], writes=[b_fw[1]])
                S.op("dve", lambda e: e.tensor_tensor(xn[:, 0:128], fw[1][:, 0:128], trl[:, 0:128], ALU.mult), reads=[b_fw[0], b_fw[1]], writes=[b_xn])
                S.op("pe", lambda e: e.transpose(pt[:, 0:128], xn[:, 0:128], identb[:]), reads=[b_xn, b_const], writes=[b_pt])
                S.op("act", lambda e, g=g: e.activation(wsT[:, g, :], pt[:, 0:128], AF.Identity), reads=[b_pt], writes=[cb])
            pbk = PSB[0]
            for kv in range(2):
                for l in range(32):
                    S.op("pe", lambda e, kv=kv, l=l: e.matmul(pbk[0][:, kv:kv + 1], w1_sb[0:64, kv * 32 + l, :], posT_sb[0:64, kv, l:l + 1],
                                                              start=(kv == 0 and l == 0), stop=(kv == 1 and l == 31), skip_group_check=True),
                         reads=[b_cmpw], writes=[pbk[1]])
            S.op("dve", lambda e: e.tensor_tensor(b1c[:], b1c[:], pbk[0][:, 0:2], ALU.add), reads=[b_cmpw, pbk[1]], writes=[b_cmpw])
            S.op("dve", lambda e: e.memset(vs_aug[:], 1.0), writes=[b_vs])
            S.op("dve", lambda e: e.memset(vw_aug[:], 1.0), writes=[b_vw])
            S.op("dve", lambda e: e.memset(kcmpT[:], 0.0), writes=[b_kcmp])
            S.op("dve", lambda e: e.memset(cv_aug[:], 0.0), writes=[b_cv])
            S.op("dve", lambda e: e.memset(cv_aug[:, :, :, 128:129], 1.0), writes=[b_cv])
            for stt in range(2):
                for g in range(2):
                    S.dma("pool", cv_aug[:, stt, g, 0:64], D["ovl"][stt * 128:(stt + 1) * 128, :], writes=[b_cv])
            S.op("dve", lambda e: e.memset(rz[:], 0.0), writes=[b_roll])
            S.op("dve", lambda e: e.memset(qz[:], 0.0), writes=[b_qT])
            for i_ in range(2):
                S.op("dve", lambda e, i_=i_: e.memset(nmp[i_][:], 0.0), writes=[b_sc])
            S.op("dve", lambda e: e.memset(xbT[:], 0.0), writes=[b_xb])
            S.op("dve", lambda e: e.memset(carry[:], 0.0), writes=[b_carry])
            S.op("dve", lambda e: e.memset(hid[:], 0.0), writes=[b_hid])
            S.op("dve", lambda e: e.memset(onesp[:], 0.0), writes=[cb])
            for a4_ in range(4):
                S.op("dve", lambda e, a4_=a4_: e.memset(onesp[0:1, a4_, 32 * a4_:32 * a4_ + 32], 1.0), reads=[cb], writes=[cb])
            S.dma("pool", b2v_row[:], D["b2vB"][L, 0:1, :], writes=[cb])

            rot = [0]

            def nextbank():
                b = PSB[rot[0] % 3]
                rot[0] += 1
                return b

            ptc = [0]
            lgc = [0]
            gsr = [0]

            for T in range(8):
                t0 = T * 512
                for s in range(4):
                    tok = T * 4 + s
                    prenorm_xT(src, tok, hin[:, 0, :], b_hin[0], gpre[:], cb, junk, b_junk, small, b_small, xn, b_xn, xT, b_xT, s)

                def fm_proj(c0, ncol, evac):
                    pb = nextbank()
                    for kc in range(8):
                        S.op("pe", lambda e, pb=pb, kc=kc: e.matmul(pb[0][0:ncol, 0:512], win_sb[:, kc, c0:c0 + ncol], xT[:, kc, :], start=(kc == 0), stop=(kc == 7)),
                             reads=[b_win, b_xT], writes=[pb[1]])
                    evac(pb)
                for i in range(2):
                    fm_proj(512 + i * 128, 128, lambda pb, i=i: S.op("act", lambda e: e.activation(xbT[:, i, 3:515], pb[0][:, 0:512], AF.Identity), reads=[pb[1]], writes=[b_xb]))
                    fm_proj(768 + i * 128, 128, lambda pb, i=i: S.op("act", lambda e: e.activation(gateT[:, i, :], pb[0][:, 0:512], AF.Gelu_apprx_tanh), reads=[pb[1]], writes=[b_gate]))
                S.op("pool", lambda e: e.memset(qz[64:128, 0:4, :], 0.0), reads=[b_qT], writes=[b_qT])
                S.op("pool", lambda e: e.memset(qz[0:64, 4:8, :], 0.0), reads=[b_qT], writes=[b_qT])
                for r in range(4):
                    def _qev(pb, r=r):
                        S.op("act", lambda e: e.activation(qz[0:64, r, :], pb[0][0:64, 0:512], AF.Identity), reads=[pb[1]], writes=[b_qT])
                        S.op("dve", lambda e: e.tensor_copy(qz[64:128, 4 + r, :], pb[0][64:128, 0:512]), reads=[pb[1]], writes=[b_qT])
                    fm_proj(1024 + r * 128, 128, _qev)
                for kv_ in range(2):
                    def _rev(pb, kv_=kv_):
                        S.op("act", lambda e: e.activation(rz[0:64, kv_ * 2, 16:528], pb[0][0:64, 0:512], AF.Identity), reads=[pb[1]], writes=[b_roll])
                        S.op("dve", lambda e: e.tensor_copy(rz[64:128, kv_ * 2 + 1, 16:528], pb[0][64:128, 0:512]), reads=[pb[1]], writes=[b_roll])
                    fm_proj(1536 + 128 * kv_, 128, _rev)
                def _kev(pb, t0=t0):
                    S.op("dve", lambda e: e.tensor_copy(ksE[0][0:64, t0:t0 + 512], pb[0][0:64, 0:512]), reads=[pb[1]], writes=[b_ksT])
                    S.op("act", lambda e: e.activation(ksE[1][64:128, t0:t0 + 512], pb[0][64:128, 0:512], AF.Identity), reads=[pb[1]], writes=[b_ksT])
                fm_proj(1792, 128, _kev)
                fm_proj(2048, 128, lambda pb, t0=t0: S.op("act", lambda e: e.activation(kwT[:, t0:t0 + 512], pb[0][:, 0:512], AF.Identity), reads=[pb[1]], writes=[b_kwT]))

                S.skipping = 'a' in SKIP
                for s in range(4):
                    tok = T * 4 + s
                    ts_ = slice(s * 128, (s + 1) * 128)
                    pa = nextbank()
                    for kc in range(8):
                        S.op("pe", lambda e, pa=pa, kc=kc, ts_=ts_: e.matmul(pa[0][:, 0:512], xT[:, kc, ts_], win_sb[:, kc, 0:512], start=(kc == 0), stop=(kc == 7)),
                             reads=[b_win, b_xT], writes=[pa[1]])
                    S.op("act", lambda e, pa=pa: e.activation(uv[:], pa[0][:, 0:512], AF.Gelu_apprx_tanh), reads=[pa[1]], writes=[b_uv])
                    S.op("act", lambda e: e.activation(junk[:, 0:256], uv[:, 256:512], AF.Square, accum_out=small[:, 6:7]), reads=[b_uv], writes=[b_junk, b_small])
                    rstd_from_ss(small[:, 6:7], small[:, 7:8], 256.0, [b_small], [b_small], None)
                    S.op("dve", lambda e: e.scalar_tensor_tensor(out=avn[:], in0=uv[:, 256:512], scalar=small[:, 7:8], in1=sguB[:], op0=ALU.mult, op1=ALU.mult),
                         reads=[b_uv, b_small, cb], writes=[b_avn])
                    pm = nextbank()
                    for g in range(4):
                        S.op("pe", lambda e, pm=pm, g=g: e.matmul(pm[0][:, g * 64:(g + 1) * 64], wsT[:, g, :], avn[:, g * 64:(g + 1) * 64], start=True, stop=True,
                                                                  skip_group_check=True),
                             reads=[cb, b_avn], writes=[pm[1]])
                    for g in range(4):
                        S.op("dve", lambda e, pm=pm, g=g, s=s: e.scalar_tensor_tensor(out=ytok[:, s, g * 64:(g + 1) * 64], in0=pm[0][:, g * 64:(g + 1) * 64],
                                                                                      scalar=bsT[:, g:g + 1], in1=uv[:, g * 64:(g + 1) * 64], op0=ALU.add, op1=ALU.mult),
                             reads=[pm[1], cb, b_uv], writes=[b_ytok[s]])
                    pv = nextbank()
                    for kc in range(8):
                        S.op("pe", lambda e, pv=pv, kc=kc, ts_=ts_: e.matmul(pv[0][:, 0:128], xT[:, kc, ts_], win_sb[:, kc, 1920:2048], start=(kc == 0), stop=(kc == 7),
                                                                            skip_group_check=True),
                             reads=[b_win, b_xT], writes=[pv[1]])
                    for kc in range(8):
                        S.op("pe", lambda e, pv=pv, kc=kc, ts_=ts_: e.matmul(pv[0][:, 128:280], xT[:, kc, ts_], win_sb[:, kc, 2176:2328], start=False, stop=(kc == 7),
                                                                            skip_group_check=True),
                             reads=[b_win, b_xT], writes=[pv[1]])
                    for g_ in range(2):
                        S.op("act", lambda e, pv=pv, tok=tok, g_=g_: e.activation(vs_aug[:, tok, g_, 0:64], pv[0][:, g_ * 64:(g_ + 1) * 64], AF.Identity),
                             reads=[pv[1]], writes=[b_vs])
                        S.op("act", lambda e, pv=pv, tok=tok, g_=g_: e.activation(vw_aug[:, tok, g_, 0:64], pv[0][:, 128 + g_ * 64:128 + (g_ + 1) * 64], AF.Identity),
                             reads=[pv[1]], writes=[b_vw])
                    S.op("act", lambda e, pv=pv, s=s: e.activation(sg[:, s, :], pv[0][:, 256:280], AF.Sigmoid), reads=[pv[1]], writes=[b_sg])

                S.skipping = 'b' in SKIP
                for i in range(2):
                    xc, ig, aa, bb = fw[0], fw[1], fw[2], fw[3]
                    S.op("dve", lambda e, i=i: e.tensor_scalar(xc[:], xbT[:, i, 0:512], cw_sb[:, i, 0:1], cbias[:, i:i + 1], ALU.mult, ALU.add),
                         reads=[b_xb, cb], writes=[b_fw[0]])
                    for k in range(1, 4):
                        S.op("dve", lambda e, i=i, k=k: e.scalar_tensor_tensor(out=xc[:], in0=xbT[:, i, k:k + 512], scalar=cw_sb[:, i, k:k + 1], in1=xc[:],
                                                                               op0=ALU.mult, op1=ALU.add), reads=[b_xb, cb, b_fw[0]], writes=[b_fw[0]])
                    S.op("dve", lambda e, i=i: e.tensor_copy(xbT[:, i, 0:3], xbT[:, i, 512:515]), reads=[b_xb], writes=[b_xb])
                    S.op("act", lambda e: e.activation(xcb[:], xc[:], AF.Identity), reads=[b_fw[0]], writes=[b_xcb])
                    pr = nextbank()
                    S.op("pe", lambda e, pr=pr, i=i: e.matmul(pr[0][:, 0:512], bda[:, i, :], xcb[:], start=True, stop=True), reads=[cb, b_xcb], writes=[pr[1]])
                    pi = nextbank()
                    S.op("pe", lambda e, pi=pi, i=i: e.matmul(pi[0][:, 0:512], bdx[:, i, :], xcb[:], start=True, stop=True), reads=[cb, b_xcb], writes=[pi[1]])
                    S.op("act", lambda e, pr=pr, i=i: e.activation(aa[:], pr[0][:, 0:512], AF.Sigmoid, bias=ba_sb[:, i:i + 1]), reads=[pr[1], cb], writes=[b_fw[2]])
                    S.op("act", lambda e, pi=pi, i=i: e.activation(ig[:], pi[0][:, 0:512], AF.Sigmoid, bias=bx_sb[:, i:i + 1]), reads=[pi[1], cb], writes=[b_fw[1]])
                    S.op("act", lambda e, i=i: e.activation(aa[:], aa[:], AF.Exp, scale=lamc[:, i:i + 1]), reads=[b_fw[2], cb], writes=[b_fw[2]])
                    S.op("dve", lambda e: e.tensor_tensor(bb[:], aa[:], aa[:], ALU.mult), reads=[b_fw[2]], writes=[b_fw[3]])
                    S.op("dve", lambda e: e.tensor_scalar(bb[:], bb[:], -1.0, 1.0, ALU.mult, ALU.add), reads=[b_fw[3]], writes=[b_fw[3]])
                    S.op("act", lambda e: e.activation(bb[:], bb[:], AF.Sqrt), reads=[b_fw[3]], writes=[b_fw[3]])
                    S.op("dve", lambda e: e.tensor_tensor(ig[:], ig[:], xc[:], ALU.mult), reads=[b_fw[1], b_fw[0]], writes=[b_fw[1]])
                    S.op("dve", lambda e: e.tensor_tensor(bb[:], bb[:], ig[:], ALU.mult), reads=[b_fw[3], b_fw[1]], writes=[b_fw[3]])
                    S.op("dve", lambda e, i=i: e.tensor_tensor_scan(xc[:], aa[:], bb[:], carry[:, i:i + 1], ALU.mult, ALU.add),
                         reads=[b_fw[2], b_fw[3], b_carry], writes=[b_fw[0]])
                    S.op("dve", lambda e, i=i: e.tensor_copy(carry[:, i:i + 1], xc[:, 511:512]), reads=[b_fw[0]], writes=[b_carry])
                    S.op("dve", lambda e, i=i: e.tensor_tensor(yT[:, 2 + i, :], xc[:], gateT[:, i, :], ALU.mult), reads=[b_fw[0], b_gate], writes=[b_yT])

                S.skipping = 'c' in SKIP
                phs = [nextbank(), nextbank()]
                for g in range(2):
                    ph = phs[g]
                    for kv, roll in ((0, None), (1, None)):
                        col = kv * 32
                        for l in range(32):
                            S.op("pe", lambda e, ph=ph, kv=kv, g=g, l=l, roll=roll, col=col: e.matmul(
                                ph[0][:, col:col + 32], w1_sb[:, kv * 32 + l, :], rz[:, kv * 2 + g, l:l + 16 * 31 + 1:16],
                                start=(kv == 0 and l == 0), stop=(kv == 1 and l == 31), skip_group_check=True), reads=[b_cmpw, b_roll], writes=[ph[1]])
                    for kv in range(2):
                        S.op("act", lambda e, ph=ph, kv=kv, g=g: e.activation(hid[:, kv * 2 + g, 32:64], ph[0][:, kv * 32:kv * 32 + 32],
                                                                              AF.Gelu_apprx_tanh, bias=b1c[:, kv:kv + 1]), reads=[ph[1], b_cmpw], writes=[b_hid])
                S.op("dve", lambda e: e.tensor_copy(rz[:, :, 0:16], rz[:, :, 512:528]), reads=[b_roll], writes=[b_roll])
                pk = nextbank()
                for g in range(2):
                    S.op("pe", lambda e, pk=pk, g=g: e.matmul(pk[0][:, 0:32], w2k_pad[:, g, :], hid[:, g, 32:64], start=(g == 0), stop=(g == 1)),
                         reads=[b_cmpw, b_hid], writes=[pk[1]])
                S.op("act", lambda e, pk=pk, T=T: e.activation(kcmpT[:, 32 * T:32 * T + 32], pk[0][:, 0:32], AF.Identity, bias=b2k[:, 0:1]),
                     reads=[pk[1], b_cmpw], writes=[b_kcmp])
                a4 = T % 4
                stt = T // 4
                S.op("dve", lambda e: e.memset(hidv[:], 0.0), reads=[b_hidv], writes=[b_hidv])
                S.op("dve", lambda e, a4=a4: e.tensor_copy(hidv[:, :, 32 * a4:32 * a4 + 32], hid[:, 2:4, 32:64]), reads=[b_hid, b_hidv], writes=[b_hidv])
                pvv = nextbank()
                for g in range(2):
                    S.op("pe", lambda e, pvv=pvv, g=g: e.matmul(pvv[0][:, g * 64:(g + 1) * 64], hidv[:, g, :], w2v_sb[:], start=(g == 0), stop=False,
                                                                  skip_group_check=True), reads=[b_cmpw, b_hidv], writes=[pvv[1]])
                    S.op("pe", lambda e, pvv=pvv, g=g, a4=a4: e.matmul(pvv[0][:, g * 64:(g + 1) * 64], onesp[0:1, a4, :], b2v_row[0:1, :], start=False, stop=(g == 1),
                                                                      skip_group_check=True), reads=[cb], writes=[pvv[1]])
                for g in range(2):
                    S.op("dve", lambda e, pvv=pvv, g=g, stt=stt: e.tensor_tensor(cv_aug[:, stt, g, 64:128], cv_aug[:, stt, g, 64:128], pvv[0][:, g * 64:(g + 1) * 64], ALU.add),
                         reads=[pvv[1], b_cv], writes=[b_cv])

                S.skipping = 'n' in SKIP
                nslot = 32 * (T + 1)
                stiles = [(0, min(nslot, 128))] + ([(1, nslot - 128)] if nslot > 128 else [])
                for g in range(2):
                    base = 64 * g
                    bs_ = slice(base, base + 64)
                    for r in range(4):
                        h = 4 * g + r
                        ets = []
                        for (stt_, M) in stiles:
                            pb = nextbank()
                            S.op("pe", lambda e, pb=pb, stt_=stt_, M=M, r=r, g=g: e.matmul(pb[0][0:M, 0:512], kcmpT[:, stt_ * 128:stt_ * 128 + M], qz[:, 4 * g + r, :],
                                                                                             start=True, stop=True), reads=[b_kcmp, b_qT], writes=[pb[1]])
                            bi = lgc[0] % 2; lgc[0] += 1
                            S.dma("sp", bct[bi][0:M, :], D["bc"][h, stt_ * 128:stt_ * 128 + M, t0:t0 + 512], writes=[b_bct[bi]])
                            S.op("dve", lambda e, pb=pb, bi=bi, M=M: e.scalar_tensor_tensor(out=lg[bi][0:M, :], in0=pb[0][0:M, 0:512], scalar=0.125, in1=bct[bi][0:M, :],
                                                                                            op0=ALU.mult, op1=ALU.add), reads=[pb[1], b_bct[bi]], writes=[b_lg[bi]])
                            pi_ = ptc[0] % 3; ptc[0] += 1
                            S.op("act", lambda e, bi=bi, pi_=pi_, M=M: e.activation(PT[pi_][0:M, :], lg[bi][0:M, :], AF.Exp), reads=[b_lg[bi]], writes=[b_PT[pi_]])
                            ets.append((pi_, stt_, M))
                        cbk = [PSB[5], PSB[6]]
                        for s in range(4):
                            bk = cbk[s // 2]
                            co = (s % 2) * 129
                            for j, (pi_, stt_, M) in enumerate(ets):
                                S.op("pe", lambda e, bk=bk, co=co, pi_=pi_, stt_=stt_, M=M, s=s, j=j, g=g, ets=ets: e.matmul(
                                    bk[0][:, co:co + 129], PT[pi_][0:M, s * 128:(s + 1) * 128], cv_aug[0:M, stt_, g, :],
                                    start=(s % 2 == 0 and j == 0), stop=(s % 2 == 1 and j == len(ets) - 1), skip_group_check=True),
                                    reads=[b_PT[pi_], b_cv], writes=[bk[1]])
                        for s in range(4):
                            bk = cbk[s // 2]
                            co = (s % 2) * 129
                            S.op("dve", lambda e, bk=bk, co=co: e.tensor_scalar(z4[:, 0:1], bk[0][:, co + 128:co + 129], 1e-30, None, ALU.max), reads=[bk[1]], writes=[b_z4])
                            S.op("dve", lambda e: e.reciprocal(z4[:, 0:1], z4[:, 0:1]), reads=[b_z4], writes=[b_z4])
                            if r == 0:
                                S.op("dve", lambda e, bk=bk, co=co, s=s: e.tensor_scalar(impS[:, s, :], bk[0][:, co:co + 64], z4[:, 0:1], None, ALU.mult),
                                     reads=[bk[1], b_z4], writes=[b_imp])
                            else:
                                S.op("dve", lambda e, bk=bk, co=co, s=s: e.scalar_tensor_tensor(out=impS[:, s, :], in0=bk[0][:, co:co + 64], scalar=z4[:, 0:1], in1=impS[:, s, :],
                                                                                                op0=ALU.mult, op1=ALU.add), reads=[bk[1], b_z4, b_imp], writes=[b_imp])
                            S.op("dve", lambda e, bk=bk, co=co, s=s, h=h: e.tensor_scalar(ycomb[:, s, h * 64:(h + 1) * 64], bk[0][:, co + 64:co + 128], z4[:, 0:1], sg[:, s, h:h + 1],
                                                                                          ALU.mult, ALU.mult), reads=[bk[1], b_z4, b_sg], writes=[b_ycs[s]])
                    for s in range(4):
                        itile = T * 4 + s
                        off = 63 - 2 * itile
                        S.op("dve", lambda e, s=s, off=off: e.tensor_tensor(sc[:], impS[:, s, :], tmax_sb[:, off:off + 64], ALU.max), reads=[b_imp, cb], writes=[b_sc])
                        S.op("dve", lambda e, off=off: e.tensor_tensor(sc[:], sc[:], tmin_sb[:, off:off + 64], ALU.min), reads=[b_sc, cb], writes=[b_sc])
                        S.op("dve", lambda e: e.memset(sc[:, 0:1], 1e4), reads=[b_sc], writes=[b_sc])
                        S.op("dve", lambda e: e.max(out=m8[:, 0:8], in_=sc[:]), reads=[b_sc], writes=[b_sc])
                        S.op("dve", lambda e: e.match_replace(out=sc2[:], in_to_replace=m8[:, 0:8], in_values=sc[:], imm_value=-3.0e38), reads=[b_sc], writes=[b_sc])
                        S.op("dve", lambda e: e.max(out=m8[:, 8:16], in_=sc2[:]), reads=[b_sc], writes=[b_sc])
                        S.op("dve", lambda e, g=g: e.tensor_scalar(nmp[g][:, 64 * (1 - g):64 * (1 - g) + 64], sc[:], m8[:, 15:16], NEGM, ALU.is_lt, ALU.mult), reads=[b_sc], writes=[b_sc])
                        S.op("pe", lambda e, g=g: e.transpose(pt[:, 0:128], nmp[g][:], identb[:]), reads=[b_sc, b_const], writes=[b_pt])
                        S.op("act", lambda e, g=g, s=s: e.activation(nmz[g][:, s * 128:(s + 1) * 128], pt[:, 0:128], AF.Identity), reads=[b_pt], writes=[b_nmT])
                    wjobs, sjobs = [], []

                    def mk_hook(r_, g=g):
                        def mask_hook():
                            oh = slice(64, 128) if g == 0 else slice(0, 64)
                            S.op("pool", lambda e: e.tensor_copy(qz[oh, 4 * g + r_, :], nmz[g][oh, :]), reads=[b_nmT, b_qT], writes=[b_qT])
                        return mask_hook
                    for r in range(4):
                        h = 4 * g + r
                        selb, winb = PSB[3], PSB[4]
                        st_ = {}

                        def mk_sel(kt, gsc, b_gsc, h=h, g=g, selb=selb, first=False, load=False, hook=None, post=None):
                            Dd = t0 - 128 * kt
                            f0 = max(0, -Dd)
                            J = {}

                            def A():
                                if hook is not None:
                                    hook()
                                if load:
                                    S.dma("pool", gsc[:], D["gs"][h], writes=[b_gsc])
                                pb = nextbank()
                                J["pb"] = pb
                                S.op("pe", lambda e: e.matmul(pb[0][:, f0:512], ksE[g][:, kt * 128:(kt + 1) * 128], qz[:, h, f0:512], start=True, stop=True),
                                     reads=[b_ksT, b_qT], writes=[pb[1]])

                            def B():
                                pb = J["pb"]
                                pi_ = ptc[0] % 3; ptc[0] += 1
                                J["pi"] = pi_
                                if Dd <= 896:
                                    bi = lgc[0] % 2; lgc[0] += 1
                                    x0 = Dd + 384
                                    S.op("dve", lambda e: e.scalar_tensor_tensor(out=lg[bi][:, f0:512], in0=pb[0][:, f0:512], scalar=0.125,
                                                                                 in1=gsc[:, x0 + f0:x0 + 512], op0=ALU.mult, op1=ALU.add),
                                         reads=[pb[1], b_gsc], writes=[b_lg[bi]])
                                    S.op("act", lambda e: e.activation(PT[pi_][:, f0:512], lg[bi][:, f0:512], AF.Exp), reads=[b_lg[bi]], writes=[b_PT[pi_]])
                                else:
                                    S.op("act", lambda e: e.activation(PT[pi_][:, 0:512], pb[0][:, 0:512], AF.Exp, bias=c31_sb[:, h:h + 1], scale=0.125),
                                         reads=[pb[1], cb], writes=[b_PT[pi_]])

                            def C():
                                pi_ = J["pi"]
                                for s in range(f0 // 128, 4):
                                    S.op("pe", lambda e, s=s: e.matmul(selb[0][:, s * 65:(s + 1) * 65], PT[pi_][:, s * 128:(s + 1) * 128], vs_aug[:, kt, g, :],
                                                                        start=(first and s == 0), stop=False, skip_group_check=True),
                                         reads=[b_PT[pi_], b_vs], writes=[selb[1]])
                            return (A, B, C, post)

                        def mk_win(kt, gsc, b_gsc, h=h, g=g, winb=winb, first=False, post=None, load=False):
                            Dd = t0 - 128 * kt
                            lo = max(0, -Dd)
                            hi = min(512, 639 - Dd)
                            J = {}

                            def A():
                                if load:
                                    S.dma("pool", gsc[:], D["gs"][h], writes=[b_gsc])
                                pb = nextbank()
                                J["pb"] = pb
                                S.op("pe", lambda e: e.matmul(pb[0][:, lo:hi], kwT[:, kt * 128:(kt + 1) * 128], qz[:, h, lo:hi], start=True, stop=True),
                                     reads=[b_kwT, b_qT], writes=[pb[1]])

                            def B():
                                pb = J["pb"]
                                bi = lgc[0] % 2; lgc[0] += 1
                                x0 = Dd + 384
                                S.op("dve", lambda e: e.scalar_tensor_tensor(out=lg[bi][:, lo:hi], in0=pb[0][:, lo:hi], scalar=0.125,
                                                                             in1=gsc[:, x0 + lo:x0 + hi], op0=ALU.mult, op1=ALU.add),
                                     reads=[pb[1], b_gsc], writes=[b_lg[bi]])
                                if Dd >= 128:
                                    S.op("dve", lambda e: e.tensor_tensor(lg[bi][:, lo:hi], lg[bi][:, lo:hi], m512_sb[:, Dd - 128 + lo:Dd - 128 + hi], ALU.add),
                                         reads=[b_lg[bi], cb], writes=[b_lg[bi]])
                                pi_ = ptc[0] % 3; ptc[0] += 1
                                J["pi"] = pi_
                                S.op("act", lambda e: e.activation(PT[pi_][:, lo:hi], lg[bi][:, lo:hi], AF.Exp), reads=[b_lg[bi]], writes=[b_PT[pi_]])

                            def C():
                                pi_ = J["pi"]
                                fw_ = first
                                for s in range(4):
                                    a_ = max(lo, 128 * s)
                                    b__ = min(hi, 128 * s + 128)
                                    if b__ <= a_:
                                        continue
                                    S.op("pe", lambda e, s=s, a_=a_, b__=b__, fw_=fw_: e.matmul(winb[0][a_ - 128 * s:b__ - 128 * s, s * 65:(s + 1) * 65], PT[pi_][:, a_:b__],
                                                                                               vw_aug[:, kt, g, :], start=fw_, stop=False, skip_group_check=True),
                                         reads=[b_PT[pi_], b_vw], writes=[winb[1]])
                                    fw_ = False
                            return (A, B, C, post)

                        def mk_post(which_, h=h, selb=selb, winb=winb):
                            def post():
                                if dbg and T < 2:
                                    for bi_, bk in ((0, selb), (1, winb)) if False else [((0, selb), (1, winb))[which_]]:
                                        S.op("act", lambda e, bk=bk: e.activation(lg[0][:, 0:260], bk[0][:, 0:260], AF.Identity), reads=[bk[1]], writes=[b_lg[0]])
                                        S.dma("sp", ydR[T, h, bi_], lg[0][:, 0:260], reads=[b_lg[0]])
                                for (bk, goff) in [((selb, 8), (winb, 16))[which_]]:
                                    zv = bk[0][:, 0:260].rearrange("p (s c) -> p s c", s=4)[:, :, 64:65]
                                    z3 = z4[:, 0:4].rearrange("p (s c) -> p s c", c=1)
                                    f3 = z4[:, 4:8].rearrange("p (s c) -> p s c", c=1)
                                    S.op("dve", lambda e, zv=zv, z3=z3: e.tensor_scalar(z3, zv, 1e-30, None, ALU.max), reads=[bk[1]], writes=[b_z4])
                                    S.op("dve", lambda e: e.reciprocal(z4[:, 0:4], z4[:, 0:4]), reads=[b_z4], writes=[b_z4])
                                    S.op("dve", lambda e, z3=z3, f3=f3, goff=goff: e.tensor_tensor(f3, z3, sg[:, :, goff + h:goff + h + 1], ALU.mult), reads=[b_z4, b_sg], writes=[b_z4])
                                    for s in range(4):
                                        S.op("dve", lambda e, bk=bk, s=s: e.scalar_tensor_tensor(out=ycomb[:, s, h * 64:(h + 1) * 64], in0=bk[0][:, s * 65:s * 65 + 64], scalar=z4[:, 4 + s:5 + s],
                                                                                               in1=ycomb[:, s, h * 64:(h + 1) * 64], op0=ALU.mult, op1=ALU.add),
                                             reads=[bk[1], b_z4, b_ycs[s]], writes=[b_ycs[s]])
                            return post

                        kts = [4 * T] + [k for k in range(max(0, 4 * T - 4), 4 * T + 4) if k != 4 * T]
                        gi_ = gsr[0] % 2; gsr[0] += 1
                        for j_, kt in enumerate(kts):
                            wjobs.append(mk_win(kt, gs_cur[gi_], b_gs[gi_], first=(j_ == 0), load=(j_ == 0), post=(mk_post(1) if j_ == len(kts) - 1 else None)))
                        for kt in range(0, 4 * T + 4):
                            wjobs.append(mk_sel(kt, gs_cur[gi_], b_gs[gi_], first=(kt == 0), load=False,
                                                hook=(mk_hook(r) if kt == 0 else None), post=(mk_post(0) if kt == 4 * T + 3 else None)))
                    jobs = wjobs
                    nj = len(jobs)
                    for i_ in range(nj + 2):
                        if i_ < nj:
                            jobs[i_][0]()
                        if 0 <= i_ - 1 < nj:
                            jobs[i_ - 1][1]()
                        if 0 <= i_ - 2 < nj:
                            jobs[i_ - 2][2]()
                            if jobs[i_ - 2][3] is not None:
                                jobs[i_ - 2][3]()

                S.skipping = False
                if dbg and L == stop_after[0]:
                    for s in range(4):
                        tok = T * 4 + s
                        S.dma("pool", ydA[tok * 128:(tok + 1) * 128, :], ytok[:, s, :], reads=[b_ytok[s]])
                        S.dma("sp", ydC[tok * 128:(tok + 1) * 128, :], ycomb[:, s, :], reads=[b_ycs[s]])
                    for i in range(2):
                        S.dma("pool", ydB[i * 128:(i + 1) * 128, t0:t0 + 512], yT[:, 2 + i, :], reads=[b_yT])
                for s in range(4):
                    for c in (0, 1):
                        S.op("pe", lambda e, s=s, c=c: e.transpose(pt[:, c * 128:(c + 1) * 128], ytok[:, s, c * 128:(c + 1) * 128], identb[:]), reads=[b_ytok[s], b_const], writes=[b_pt])
                    S.op("act", lambda e, s=s: e.activation(yT[:, 0:2, s * 128:(s + 1) * 128], pt[:, 0:256].rearrange("p (c n) -> p c n", c=2), AF.Identity), reads=[b_pt], writes=[b_yT])
                    pf = nextbank()
                    for c in range(4):
                        S.op("pe", lambda e, s=s, c=c, pf=pf: e.transpose(pf[0][:, c * 128:(c + 1) * 128], ycomb[:, s, c * 128:(c + 1) * 128], identf[:]), reads=[b_ycs[s], cb], writes=[pf[1]])
                    S.op("act", lambda e, s=s, pf=pf: e.activation(yT[:, 4:8, s * 128:(s + 1) * 128], pf[0][:, 0:512].rearrange("p (c n) -> p c n", c=4), AF.Identity), reads=[pf[1]], writes=[b_yT])
                for s in range(4):
                    tok = T * 4 + s
                    banks = [PSB[5], PSB[6]]
                    for hf in range(2):
                        for c in range(8):
                            S.op("pe", lambda e, hf=hf, c=c, s=s: e.matmul(banks[hf][0][:, 0:512], yT[:, c, s * 128:(s + 1) * 128], wout_sb[:, c, hf * 512:(hf + 1) * 512],
                                                                           start=(c == 0), stop=(c == 7)), reads=[b_yT, b_wout], writes=[banks[hf][1]])
                    postnorm_residual(banks, src, dst, tok, hin[:, 0, :], b_hin[0], gpost, cb, junk, b_junk, small, b_small, ftmp, b_ftmp, True)

        seq = []
        cur = "x"
        for L in range(2):
            for ph in ("ffn1", "mix", "ffn2", "ple"):
                seq.append((L, ph))
        if stop_after is not None:
            seq = seq[:seq.index(stop_after) + 1]
        for i, (L, ph) in enumerate(seq):
            last = (i == len(seq) - 1)
            dst = "y" if last else ("hA" if cur != "hA" else "hB")
            if ph == "ffn1":
                ffn_phase(L, 0, cur, dst)
            elif ph == "ffn2":
                ffn_phase(L, 1, cur, dst)
            elif ph == "mix":
                mixer_phase(L, cur, dst)
            else:
                ple_phase(L, cur, dst)
            cur = dst
        S.finish()
        S.emit()
    return nc


_PROG = {}
LAST_RES = None


def prep_inputs(inputs, b):
    f = lambda a: np.ascontiguousarray(np.asarray(a, dtype=np.float32))
    c = host_constants()
    gs, bc, c31 = host_bias_tables(inputs["rel_bias"])
    m = {}
    m["x"] = f(inputs["x"][b])
    m["pT"] = f(np.transpose(np.asarray(inputs["p"])[:, b], (0, 2, 1)))
    m["ngB"] = f(np.broadcast_to(np.asarray(inputs["norm_g"])[:, :, None, :], (2, 8, 128, DM)))
    m["wg"] = f(inputs["ffn_w_gate"]); m["wu"] = f(inputs["ffn_w_up"]); m["wd"] = f(inputs["ffn_w_down"])
    m["w_in"] = f(inputs["w_in"]); m["w_out"] = f(inputs["w_out"])
    m["sguB"] = f(np.broadcast_to(np.asarray(inputs["sgu_norm_g"])[:, None, :], (2, 128, 256)))
    m["sgu_w"] = f(inputs["sgu_w"])
    m["sgu_bT"] = f(np.transpose(np.asarray(inputs["sgu_b"]), (0, 2, 1)))
    v2 = lambda a: f(np.transpose(np.asarray(a, np.float32).reshape(2, 2, 128), (0, 2, 1)))
    m["conv_wT"] = f(np.transpose(np.asarray(inputs["conv_w"], np.float32).reshape(2, 4, 2, 128), (0, 3, 2, 1)))
    m["conv_b"] = v2(inputs["conv_b"]); m["lru_wa"] = f(inputs["lru_wa"]); m["lru_ba"] = v2(inputs["lru_ba"])
    m["lru_wx"] = f(inputs["lru_wx"]); m["lru_bx"] = v2(inputs["lru_bx"]); m["lru_lam"] = v2(inputs["lru_lambda"])
    m["posT"] = f(np.transpose(np.asarray(inputs["cmp_pos"]), (0, 3, 1, 2)))
    m["cmp_w1"] = f(inputs["cmp_w1"])
    m["b1T"] = f(np.transpose(np.asarray(inputs["cmp_b1"]), (0, 2, 1)))
    m["cmp_w2"] = f(inputs["cmp_w2"])
    b2 = np.asarray(inputs["cmp_b2"], np.float32)
    m["b2kk"] = f(np.concatenate([b2[:, 0], b2[:, 0]], axis=1)[:, :, None])
    m["b2vB"] = f(np.broadcast_to(b2[:, 1][:, None, :], (2, 128, 64)))
    m["wpg"] = f(inputs["ple_w_gate"]); m["wpp"] = f(inputs["ple_w_proj"])
    for k in ("ident", "tril", "eall", "ovl", "m512", "tmax", "tmin"):
        m[k] = c[k]
    m["gs"] = gs; m["bc"] = bc; m["c31"] = c31
    return m


def kernel(**inputs):
    key = STOP_AFTER
    if key not in _PROG:
        _PROG[key] = build(STOP_AFTER)
    nc = _PROG[key]
    shared = prep_inputs(inputs, 0)
    in_maps = []
    for b in range(8):
        m = dict(shared)
        m["x"] = np.ascontiguousarray(np.asarray(inputs["x"][b], dtype=np.float32))
        m["pT"] = np.ascontiguousarray(np.transpose(np.asarray(inputs["p"])[:, b], (0, 2, 1)).astype(np.float32))
        in_maps.append(m)
    ncore = int(os.environ.get("KCORES", "8"))
    res = run_bass_kernel_spmd(nc, in_maps[:ncore], core_ids=list(range(ncore)))
    global LAST_RES
    LAST_RES = res.results
    outs = [np.asarray(r["y"], dtype=np.float32) for r in res.results]
    while len(outs) < 8:
        outs.append(np.zeros_like(outs[0]))
    return np.stack(outs, axis=0)
```

```python
import contextlib
import math
import os
import numpy as np
import concourse.bass as bass
import concourse.mybir as mybir
from concourse.bass_utils import run_bass_kernel_spmd

F32 = mybir.dt.float32
BF16 = mybir.dt.bfloat16
AF = mybir.ActivationFunctionType
ALU = mybir.AluOpType

S_LEN = 4096
DM = 1024
DFF = 2816
NIN = 2328
EPS = 1e-6
NEGM = -30000.0
GSW = 1792
SKIP = os.environ.get('MIXSKIP', '')
STOP_AFTER = None


class Buf:
    __slots__ = ("name", "w", "r")

    def __init__(self, name):
        self.name = name
        self.w = {}
        self.r = {}


class Sched:
    def __init__(self, nc, stack, n_dma_sems=48):
        self.nc = nc
        self.ekeys = ["pe", "dve", "act", "pool", "sp"]
        self.prog = {k: [] for k in self.ekeys}
        self.sems = {}
        for k in self.ekeys:
            self.sems[k] = stack.enter_context(nc.semaphore("s_" + k))
        self.cnt = {k: 0 for k in self.ekeys}
        self.dsem_keys = []
        for i in range(n_dma_sems):
            k = "d%d" % i
            self.sems[k] = stack.enter_context(nc.semaphore("s_" + k))
            self.cnt[k] = 0
            self.dsem_keys.append(k)
        self.drr = 0
        self.waited = {k: {} for k in self.ekeys}
        self.sb_off = 16512
        self.uid = 0
        self.skipping = False

    def sb(self, name, shape, dtype, align=64):
        nbytes = int(np.prod(shape[1:])) * mybir.dt.size(dtype)
        off = (self.sb_off + align - 1) // align * align
        self.uid += 1
        t = self.nc.alloc_sbuf_tensor_at("%s_%d" % (name, self.uid), list(shape), dtype, offset=off)
        self.sb_off = off + nbytes
        assert self.sb_off <= 229376, (name, self.sb_off)
        return t

    def mark(self):
        return self.sb_off

    def release(self, m):
        self.sb_off = m

    def _wait(self, ek, ev):
        semkey, val = ev
        if semkey == ek and ek == "pe":
            return
        if self.waited[ek].get(semkey, 0) >= val:
            return
        self.waited[ek][semkey] = val
        sem = self.sems[semkey]
        self.prog[ek].append(lambda e, sem=sem, val=val: e.wait_ge(sem, val))

    def _deps(self, ek, reads, writes):
        deps = {}
        for b in reads:
            for k, v in b.w.items():
                deps[k] = max(deps.get(k, 0), v)
        for b in writes:
            for k, v in b.w.items():
                deps[k] = max(deps.get(k, 0), v)
            for k, v in b.r.items():
                deps[k] = max(deps.get(k, 0), v)
        for k, v in deps.items():
            self._wait(ek, (k, v))

    def _record(self, ev, reads, writes):
        for b in reads:
            b.r[ev[0]] = max(b.r.get(ev[0], 0), ev[1])
        for b in writes:
            b.w[ev[0]] = max(b.w.get(ev[0], 0), ev[1])
            b.r = {}

    def op(self, ek, fn, reads=(), writes=()):
        if self.skipping:
            return
        self._deps(ek, reads, writes)
        self.cnt[ek] += 1
        sem = self.sems[ek]
        self.prog[ek].append(lambda e, fn=fn, sem=sem: fn(e).then_inc(sem, 1))
        self._record((ek, self.cnt[ek]), reads, writes)

    def dma(self, qk, out, in_, reads=(), writes=(), **kw):
        if self.skipping:
            return
        self._deps(qk, reads, writes)
        sk = self.dsem_keys[self.drr % len(self.dsem_keys)]
        self.drr += 1
        if self.cnt[sk] > 0:
            self._wait(qk, (sk, self.cnt[sk]))
        self.cnt[sk] += 16
        sem = self.sems[sk]
        self.prog[qk].append(
            lambda e, out=out, in_=in_, sem=sem, kw=kw: e.dma_start(out=out, in_=in_, **kw).then_inc(sem, 16))
        self._record((sk, self.cnt[sk]), reads, writes)

    def barrier(self):
        for ek in self.ekeys:
            for k, v in self.cnt.items():
                if v > 0 and k != ek:
                    self._wait(ek, (k, v))

    def finish(self):
        for k, v in self.cnt.items():
            if v > 0 and k != "sp":
                self._wait("sp", (k, v))

    def emit(self):
        with self.nc.Block() as block:
            @block.tensor
            def _(e):
                for f in self.prog["pe"]:
                    f(e)

            @block.vector
            def _(e):
                for f in self.prog["dve"]:
                    f(e)

            @block.scalar
            def _(e):
                for f in self.prog["act"]:
                    f(e)

            @block.gpsimd
            def _(e):
                for f in self.prog["pool"]:
                    f(e)

            @block.sync
            def _(e):
                for f in self.prog["sp"]:
                    f(e)


def t5_bucket_np(d):
    n = np.maximum(d, 0)
    nf = np.maximum(n, 16).astype(np.float32)
    large = 16 + (np.log(nf / np.float32(16)) / np.float32(math.log(1024 / 16)) * np.float32(16)).astype(np.int32)
    large = np.minimum(large, 31)
    return np.where(n < 16, n, large)


def t5_bucket_jax_exact(d):
    import jax
    import jax.numpy as jnp
    with jax.default_device(jax.devices("cpu")[0]):
        n = jnp.maximum(jnp.asarray(d, jnp.int32), 0)
        nf = jnp.maximum(n, 16).astype(jnp.float32)
        large = 16 + (jnp.log(nf / 16) / math.log(1024 / 16) * 16).astype(jnp.int32)
        large = jnp.minimum(large, 31)
        return np.asarray(jnp.where(n < 16, n, large))


_CONST = {}


def host_constants():
    if _CONST:
        return _CONST
    c = _CONST
    c["ident"] = np.eye(128, dtype=np.float32)
    c["tril"] = np.tril(np.ones((128, 128), np.float32))
    e = np.zeros((128, 4096), np.float32)
    for k in range(4096):
        e[k // 64, k] = 1.0
        e[64 + k // 64, k] = 1.0
    c["eall"] = e
    ov = np.zeros((256, 64), np.float32)
    for s in range(1, 256):
        n = s - 1
        cs = 16 * n
        for j in range(64):
            o = min(cs + 32, 64 * j + 64) - max(cs, 64 * j)
            if o > 0:
                ov[s, j] = o / 32.0
    c["ovl"] = ov
    try:
        bk = t5_bucket_jax_exact(np.arange(0, 4200))
    except Exception:
        bk = t5_bucket_np(np.arange(0, 4200))
    c["bucket"] = bk
    tmax = np.zeros((128, 127), np.float32)
    tmin = np.full((128, 127), 1e5, np.float32)
    for p in range(128):
        hi = 1 if p >= 64 else 0
        for xx in range(127):
            rel = xx - 63 - hi
            if rel in (0, -1):
                tmax[p, xx] = 1e4
            if rel > 0:
                tmin[p, xx] = -1.0
    c["tmax"] = tmax
    c["tmin"] = tmin
    m = np.zeros((128, 896), np.float32)
    for p in range(128):
        xx = np.arange(896)
        m[p, (xx + 128 - p) >= 512] = NEGM
    c["m512"] = m
    return c


def host_bias_tables(rel_bias):
    c = host_constants()
    bk = c["bucket"]
    rb = np.asarray(rel_bias, np.float32)
    p = np.arange(128)[:, None]
    xx = np.arange(GSW)[None, :]
    d = xx - 384 - p
    ok = d >= 0
    g = rb[bk[np.maximum(d, 0)]]
    g = np.where(ok[:, :, None], g, np.float32(NEGM))
    gs = np.ascontiguousarray(np.transpose(g, (2, 0, 1))).astype(np.float32)
    s = np.arange(256)[:, None]
    t = np.arange(4096)[None, :]
    dc = t - (16 * (s - 1) + 31)
    okc = (dc >= 0) & (s >= 1)
    b = rb[bk[np.maximum(dc, 0)]]
    b = np.where(okc[:, :, None], b, np.float32(NEGM))
    bc = np.ascontiguousarray(np.transpose(b, (2, 0, 1))).astype(np.float32)
    c31 = np.ascontiguousarray(np.tile(rb[31][None, :], (128, 1))).astype(np.float32)
    return gs, bc, c31


def build(stop_after=None):
    nc = bass.Bass("TRN2", target_bir_lowering=False)
    D = {}

    def din(name, shape):
        D[name] = nc.dram_tensor(name, list(shape), F32, kind="ExternalInput").ap()
        return D[name]

    din("x", [S_LEN, DM]); din("pT", [2, 256, S_LEN]); din("ngB", [2, 8, 128, DM])
    din("wg", [2, 2, DM, DFF]); din("wu", [2, 2, DM, DFF]); din("wd", [2, 2, DFF, DM])
    din("w_in", [2, DM, NIN]); din("w_out", [2, DM, DM])
    din("sguB", [2, 128, 256]); din("sgu_w", [2, 4, 128, 128]); din("sgu_bT", [2, 128, 4])
    din("conv_wT", [2, 128, 2, 4]); din("conv_b", [2, 128, 2]); din("lru_wa", [2, 4, 64, 64]); din("lru_ba", [2, 128, 2])
    din("lru_wx", [2, 4, 64, 64]); din("lru_bx", [2, 128, 2]); din("lru_lam", [2, 128, 2])
    din("posT", [2, 64, 2, 32]); din("cmp_w1", [2, 2, 2048, 128]); din("b1T", [2, 128, 2])
    din("cmp_w2", [2, 2, 128, 64]); din("b2kk", [2, 128, 1]); din("b2vB", [2, 128, 64])
    din("wpg", [2, DM, DM]); din("wpp", [2, 256, DM])
    din("ident", [128, 128]); din("tril", [128, 128]); din("eall", [128, 4096]); din("ovl", [256, 64])
    din("gs", [8, 128, GSW]); din("m512", [128, 896]); din("bc", [8, 256, S_LEN])
    din("tmax", [128, 127]); din("tmin", [128, 127]); din("c31", [128, 8])
    y = nc.dram_tensor("y", [S_LEN, DM], F32, kind="ExternalOutput").ap()
    dbg = stop_after is not None and stop_after[1] == "mix"
    if dbg:
        ydA = nc.dram_tensor("ydA", [S_LEN, 256], F32, kind="ExternalOutput").ap()
        ydB = nc.dram_tensor("ydB", [256, S_LEN], F32, kind="ExternalOutput").ap()
        ydC = nc.dram_tensor("ydC", [S_LEN, 512], F32, kind="ExternalOutput").ap()
        ydR = nc.dram_tensor("ydR", [2, 8, 2, 128, 260], F32, kind="ExternalOutput").ap()
    hA = nc.dram_tensor("hA", [S_LEN, DM], F32).ap()
    hB = nc.dram_tensor("hB", [S_LEN, DM], F32).ap()

    with contextlib.ExitStack() as st:
        S = Sched(nc, st)
        PSB = []
        for i in range(7):
            PSB.append((st.enter_context(nc.psum_tensor("psb%d" % i, [128, 512], F32)), Buf("psb%d" % i)))
        pt, b_pt = st.enter_context(nc.psum_tensor("pst", [128, 1024], BF16)), Buf("pst")

        identb = S.sb("identb", [128, 128], BF16); b_const = Buf("const")
        S.dma("pool", identb[:], D["ident"], writes=[b_const])
        ones1 = S.sb("ones1", [128, 1], F32)
        S.op("dve", lambda e: e.memset(ones1[:], 1.0), writes=[b_const])
        epsb = S.sb("epsb", [128, 1], F32)
        S.op("dve", lambda e: e.memset(epsb[:], EPS), writes=[b_const])
        glob_mark = S.mark()

        def hbufs(name):
            return [Buf("%s%d" % (name, i)) for i in range(32)]

        HB = {"x": hbufs("x"), "hA": hbufs("hA"), "hB": hbufs("hB"), "y": hbufs("y")}
        HAP = {"x": D["x"], "hA": hA, "hB": hB, "y": y}

        def rstd_from_ss(ss_ap, out_ap, n, reads, writes, tmpbuf):
            S.op("act", lambda e: e.activation(out_ap, ss_ap, AF.Sqrt, bias=epsb[:, 0:1], scale=1.0 / n), reads=list(reads) + [b_const], writes=writes)
            S.op("dve", lambda e: e.reciprocal(out_ap, out_ap), reads=writes, writes=writes)

        def prenorm_xT(src, tok, hin_t, b_hin, gB, b_g, junk, b_junk, small, b_small, xn, b_xn, xT, b_xT, s):
            S.dma("sp", hin_t, HAP[src][tok * 128:(tok + 1) * 128, :], reads=[HB[src][tok]], writes=[b_hin])
            S.op("act", lambda e: e.activation(junk[:], hin_t, AF.Square, accum_out=small[:, 0:1]), reads=[b_hin], writes=[b_junk, b_small])
            rstd_from_ss(small[:, 0:1], small[:, 1:2], float(DM), [b_small], [b_small], None)
            S.op("dve", lambda e: e.scalar_tensor_tensor(out=xn[:], in0=hin_t, scalar=small[:, 1:2], in1=gB, op0=ALU.mult, op1=ALU.mult),
                 reads=[b_hin, b_small, b_g], writes=[b_xn])
            for c in range(8):
                S.op("pe", lambda e, c=c: e.transpose(pt[:, c * 128:(c + 1) * 128], xn[:, c * 128:(c + 1) * 128], identb[:]),
                     reads=[b_xn, b_const], writes=[b_pt])
            S.op("act", lambda e: e.activation(xT[:, :, s * 128:(s + 1) * 128], pt[:].rearrange("p (c n) -> p c n", c=8), AF.Identity),
                 reads=[b_pt], writes=[b_xT])

        def postnorm_residual(banks, src, dst, tok, hin_t, b_hin, gpostB, b_g, junk, b_junk, small, b_small, ftmp, b_ftmp, reload):
            if reload:
                S.dma("sp", hin_t, HAP[src][tok * 128:(tok + 1) * 128, :], reads=[HB[src][tok]], writes=[b_hin])
            for hf in range(2):
                S.op("act", lambda e, hf=hf: e.activation(junk[:, 0:512], banks[hf][0][:, 0:512], AF.Square, accum_out=small[:, 2 + hf:3 + hf]),
                     reads=[banks[hf][1]], writes=[b_junk, b_small])
            S.op("dve", lambda e: e.tensor_tensor(small[:, 4:5], small[:, 2:3], small[:, 3:4], ALU.add), reads=[b_small], writes=[b_small])
            rstd_from_ss(small[:, 4:5], small[:, 5:6], float(DM), [b_small], [b_small], None)
            for hf in range(2):
                S.op("dve", lambda e, hf=hf: e.scalar_tensor_tensor(out=ftmp[:, 0:512], in0=banks[hf][0][:, 0:512], scalar=small[:, 5:6],
                                                                     in1=gpostB[:, hf * 512:(hf + 1) * 512], op0=ALU.mult, op1=ALU.mult),
                     reads=[banks[hf][1], b_small, b_g], writes=[b_ftmp])
                S.op("pool", lambda e, hf=hf: e.tensor_tensor(hin_t[:, hf * 512:(hf + 1) * 512], hin_t[:, hf * 512:(hf + 1) * 512], ftmp[:, 0:512], ALU.add),
                     reads=[b_hin, b_ftmp], writes=[b_hin])
            S.dma("sp", HAP[dst][tok * 128:(tok + 1) * 128, :], hin_t, reads=[b_hin], writes=[HB[dst][tok]])

        def ffn_phase(L, which, src, dst):
            S.barrier()
            S.release(glob_mark)
            wg_sb = S.sb("wg", [128, 8, DFF], BF16); wu_sb = S.sb("wu", [128, 8, DFF], BF16)
            wd_sb = S.sb("wd", [128, 22, DM], BF16)
            b_wg, b_wu, b_wd = Buf("wg"), Buf("wu"), Buf("wd")
            gpre = S.sb("gpre", [128, DM], F32); gpost = S.sb("gpost", [128, DM], F32); b_g = Buf("g")
            hin = S.sb("hin", [128, 4, DM], F32); b_hin = [Buf("hin%d" % i) for i in range(4)]
            xT = S.sb("xT", [128, 8, 512], BF16); b_xT = Buf("xT")
            hT = S.sb("hT", [128, 22, 512], BF16); b_hT = Buf("hT")
            xn = S.sb("xn", [128, DM], BF16); b_xn = Buf("xn")
            junk = S.sb("junk", [128, DM], BF16); b_junk = Buf("junk")
            small = S.sb("small", [128, 8], F32); b_small = Buf("small")
            ftmp = S.sb("ftmp", [128, DM], F32); b_ftmp = Buf("ftmp")
            sgt = [S.sb("sgt%d" % i, [128, 512], BF16) for i in range(2)]; b_sgt = [Buf("sgt0"), Buf("sgt1")]
            ni = 4 * which
            S.dma("sp", gpre[:], D["ngB"][L, ni], writes=[b_g])
            S.dma("sp", gpost[:], D["ngB"][L, ni + 1], writes=[b_g])
            S.op("dve", lambda e: e.tensor_scalar(gpost[:], gpost[:], 0.5, None, ALU.mult), reads=[b_g], writes=[b_g])
            NCB = 4
            CW = DFF // NCB
            cbs = [(0, 768), (768, 1536), (1536, 2304), (2304, 2816)]
            b_wgc = [Buf("wgc%d" % i) for i in range(len(cbs))]
            b_wuc = [Buf("wuc%d" % i) for i in range(len(cbs))]
            b_wdc = [Buf("wdc%d" % i) for i in range(22)]
            for bi_, (c0, c1) in enumerate(cbs):
                for c in range(8):
                    S.dma("pool", wg_sb[:, c, c0:c1], D["wg"][L, which, c * 128:(c + 1) * 128, c0:c1], writes=[b_wgc[bi_]])
                    S.dma("pool", wu_sb[:, c, c0:c1], D["wu"][L, which, c * 128:(c + 1) * 128, c0:c1], writes=[b_wuc[bi_]])
            for c in range(22):
                S.dma("pool", wd_sb[:, c, :], D["wd"][L, which, c * 128:(c + 1) * 128, :], writes=[b_wdc[c]], max_dma_last_dim=4096)

            def cb_of(fc):
                for bi_, (c0, c1) in enumerate(cbs):
                    if c0 <= fc * 128 < c1:
                        return bi_
            pair = 0
            dbank = 0
            for T in range(8):
                for s in range(4):
                    prenorm_xT(src, T * 4 + s, hin[:, s, :], b_hin[s], gpre[:], b_g, junk, b_junk, small, b_small, xn, b_xn, xT, b_xT, s)
                for fc in range(22):
                    pg, bg = PSB[(pair % 2) * 2]
                    pu, bu = PSB[(pair % 2) * 2 + 1]
                    pair += 1
                    for (wsb, bw, ps_, bps) in ((wg_sb, b_wgc[cb_of(fc)], pg, bg), (wu_sb, b_wuc[cb_of(fc)], pu, bu)):
                        for kc in range(8):
                            S.op("pe", lambda e, wsb=wsb, ps_=ps_, kc=kc, fc=fc: e.matmul(ps_[:, 0:512], wsb[:, kc, fc * 128:(fc + 1) * 128], xT[:, kc, :],
                                                                                          start=(kc == 0), stop=(kc == 7)),
                                 reads=[bw, b_xT], writes=[bps])
                    sg_, bsg = sgt[fc % 2], b_sgt[fc % 2]
                    S.op("act", lambda e, sg_=sg_, pg=pg: e.activation(sg_[:], pg[:, 0:512], AF.Silu), reads=[bg], writes=[bsg])
                    S.op("dve", lambda e, sg_=sg_, pu=pu, fc=fc: e.tensor_tensor(hT[:, fc, :], sg_[:], pu[:, 0:512], ALU.mult),
                         reads=[bsg, bu], writes=[b_hT])
                for s in range(4):
                    banks = []
                    for hf in range(2):
                        pb = PSB[4 + dbank % 3]; dbank += 1
                        banks.append(pb)
                        for fc in range(22):
                            S.op("pe", lambda e, pb=pb, fc=fc, s=s, hf=hf: e.matmul(pb[0][:, 0:512], hT[:, fc, s * 128:(s + 1) * 128],
                                                                                    wd_sb[:, fc, hf * 512:(hf + 1) * 512], start=(fc == 0), stop=(fc == 21)),
                                 reads=[b_hT, b_wdc[fc]], writes=[pb[1]])
                    postnorm_residual(banks, src, dst, T * 4 + s, hin[:, s, :], b_hin[s], gpost, b_g, junk, b_junk, small, b_small, ftmp, b_ftmp, False)

        def ple_phase(L, src, dst):
            S.barrier()
            S.release(glob_mark)
            wpg_sb = S.sb("wpg", [128, 8, DM], BF16); wpp_sb = S.sb("wpp", [128, 2, DM], BF16)
            pT_sb = S.sb("pTs", [128, 2, S_LEN], BF16)
            b_w = Buf("plew")
            gpre = S.sb("gpre", [128, DM], F32); gpost = S.sb("gpost", [128, DM], F32); b_g = Buf("g")
            hin = S.sb("hin", [128, 2, DM], F32); b_hin = [Buf("hin0"), Buf("hin1")]
            def two(name, shape, dt):
                return [S.sb(name + str(i), shape, dt) for i in range(2)], [Buf(name + str(i)) for i in range(2)]
            xT2, b_xT2 = two("xT", [128, 8, 128], BF16)
            xn2, b_xn2 = two("xn", [128, DM], BF16)
            junk2, b_junk2 = two("junk", [128, DM], BF16)
            small2, b_small2 = two("small", [128, 8], F32)
            ftmp2, b_ftmp2 = two("ftmp", [128, DM], F32)
            sgf2, b_sgf2 = two("sgf", [128, DM], F32)
            u2, b_u2 = two("u", [128, DM], F32)
            S.dma("sp", gpre[:], D["ngB"][L, 6], writes=[b_g])
            S.dma("sp", gpost[:], D["ngB"][L, 7], writes=[b_g])
            for c in range(8):
                S.dma("pool", wpg_sb[:, c, :], D["wpg"][L, c * 128:(c + 1) * 128, :], writes=[b_w])
            for c in range(2):
                S.dma("pool", wpp_sb[:, c, :], D["wpp"][L, c * 128:(c + 1) * 128, :], writes=[b_w])
                for q in range(4):
                    S.dma("pool", pT_sb[:, c, q * 1024:(q + 1) * 1024], D["pT"][L, c * 128:(c + 1) * 128, q * 1024:(q + 1) * 1024], writes=[b_w])
            rbl = [0]

            def _tile(tok, xT, b_xT, xn, b_xn, junk, b_junk, small, b_small, ftmp, b_ftmp, sgf, b_sgf, u, b_u, hi_, bh):
                prenorm_xT(src, tok, hi_, bh, gpre[:], b_g, junk, b_junk, small, b_small, xn, b_xn, xT, b_xT, 0)
                gb = []
                pb_ = []
                for hf in range(2):
                    g_ = PSB[rbl[0] % 7]; rbl[0] += 1
                    p_ = PSB[rbl[0] % 7]; rbl[0] += 1
                    for kc in range(8):
                        S.op("pe", lambda e, g_=g_, kc=kc, hf=hf: e.matmul(g_[0][:, 0:512], xT[:, kc, :], wpg_sb[:, kc, hf * 512:(hf + 1) * 512],
                                                                           start=(kc == 0), stop=(kc == 7)), reads=[b_xT, b_w], writes=[g_[1]])
                    for c in range(2):
                        S.op("pe", lambda e, p_=p_, c=c, hf=hf, tok=tok: e.matmul(p_[0][:, 0:512], pT_sb[:, c, tok * 128:(tok + 1) * 128],
                                                                                  wpp_sb[:, c, hf * 512:(hf + 1) * 512], start=(c == 0), stop=(c == 1)),
                             reads=[b_w], writes=[p_[1]])
                    S.op("act", lambda e, g_=g_, hf=hf: e.activation(sgf[:, hf * 512:(hf + 1) * 512], g_[0][:, 0:512], AF.Sigmoid), reads=[g_[1]], writes=[b_sgf])
                    S.op("dve", lambda e, p_=p_, hf=hf: e.tensor_tensor(u[:, hf * 512:(hf + 1) * 512], sgf[:, hf * 512:(hf + 1) * 512], p_[0][:, 0:512], ALU.mult),
                         reads=[b_sgf, p_[1]], writes=[b_u])
                S.op("act", lambda e: e.activation(junk[:], u[:], AF.Square, accum_out=small[:, 4:5]), reads=[b_u], writes=[b_junk, b_small])
                rstd_from_ss(small[:, 4:5], small[:, 5:6], float(DM), [b_small], [b_small], None)
                S.op("dve", lambda e: e.scalar_tensor_tensor(out=ftmp[:], in0=u[:], scalar=small[:, 5:6], in1=gpost[:], op0=ALU.mult, op1=ALU.mult),
                     reads=[b_u, b_small, b_g], writes=[b_ftmp])
                S.op("pool", lambda e, hi_=hi_: e.tensor_tensor(hi_, hi_, ftmp[:], ALU.add), reads=[bh, b_ftmp], writes=[bh])
                S.dma("sp", HAP[dst][tok * 128:(tok + 1) * 128, :], hi_, reads=[bh], writes=[HB[dst][tok]])

            for tok in range(32):
                k_ = 0
                _tile(tok, xT2[k_], b_xT2[k_], xn2[k_], b_xn2[k_], junk2[k_], b_junk2[k_], small2[k_], b_small2[k_], ftmp2[k_], b_ftmp2[k_],
                      sgf2[k_], b_sgf2[k_], u2[k_], b_u2[k_], hin[:, tok % 2, :], b_hin[tok % 2])

        def mixer_phase(L, src, dst):
            S.barrier()
            S.release(glob_mark)
            cb = Buf("mixconst")
            win_sb = S.sb("win", [128, 8, NIN], BF16)
            wout_sb = S.sb("wout", [128, 8, DM], BF16)
            w1_sb = S.sb("w1", [128, 64, 128], BF16)
            w2k_pad = S.sb("w2kp", [128, 2, 128], BF16)
            w2v_sb = S.sb("w2v", [128, 64], BF16)
            posT_sb = S.sb("posT", [128, 2, 32], BF16)
            b1c = S.sb("b1c", [128, 2], F32)
            b2k = S.sb("b2k", [128, 1], F32)
            b2vB = S.sb("b2vB", [128, 64], F32)
            ksE = [S.sb("ksE%d" % i, [128, S_LEN], BF16) for i in range(2)]; b_ksT = Buf("ksT")
            kwT = S.sb("kwT", [128, S_LEN], BF16); b_kwT = Buf("kwT")
            vs_aug = S.sb("vsa", [128, 32, 2, 65], BF16); b_vs = Buf("vsa")
            vw_aug = S.sb("vwa", [128, 32, 2, 65], BF16); b_vw = Buf("vwa")
            kcmpT = S.sb("kcmpT", [128, 256], BF16); b_kcmp = Buf("kcmpT")
            cv_aug = S.sb("cva", [128, 2, 2, 129], BF16); b_cv = Buf("cva")
            gs_cur = [S.sb("gsc%d" % i, [128, GSW], BF16) for i in range(2)]; b_gs = [Buf("gs0"), Buf("gs1")]
            identf = S.sb("identf", [128, 128], F32)
            m512_sb = S.sb("m512", [128, 896], F32)
            c31_sb = S.sb("c31", [128, 8], F32)
            tmax_sb = S.sb("tmax", [128, 127], F32); tmin_sb = S.sb("tmin", [128, 127], F32)
            cw_sb = S.sb("cw", [128, 2, 4], F32); cbias = S.sb("cbias", [128, 2], F32)
            bda = S.sb("bda", [128, 2, 128], BF16); bdx = S.sb("bdx", [128, 2, 128], BF16)
            ba_sb = S.sb("ba", [128, 2], F32); bx_sb = S.sb("bx", [128, 2], F32); lamc = S.sb("lamc", [128, 2], F32)
            wsT = S.sb("wsT", [128, 4, 128], BF16)
            sguB = S.sb("sguB", [128, 256], F32); bsT = S.sb("bsT", [128, 4], F32)
            gpre = S.sb("gpre", [128, DM], F32); gpost = S.sb("gpost", [128, DM], F32)
            hin = S.sb("hin", [128, 2, DM], F32)[:, 0:1, :] if False else S.sb("hin", [128, 1, DM], F32); b_hin = [Buf("hin0"), Buf("hin0b")]; b_hin[1] = b_hin[0]
            xT = S.sb("xT", [128, 8, 512], BF16); b_xT = Buf("xT")
            xn = S.sb("xn", [128, DM], BF16); b_xn = Buf("xn")
            junk = xn; b_junk = b_xn
            small = S.sb("small", [128, 8], F32); b_small = Buf("small")
            ftmp = S.sb("ftmp", [128, 512], F32); b_ftmp = Buf("ftmp")
            qz = S.sb("qz", [128, 8, 512], BF16); b_qT = Buf("qz")
            rz = S.sb("rz", [128, 4, 528], BF16); b_roll = Buf("roll")
            xbT = S.sb("xbT", [128, 2, 516], F32); b_xb = Buf("xbT")
            gateT = S.sb("gateT", [128, 2, 512], BF16); b_gate = Buf("gateT")
            carry = S.sb("carry", [128, 2], F32); b_carry = Buf("carry")
            sg = S.sb("sg", [128, 4, 24], F32); b_sg = Buf("sg")
            uv = S.sb("uv", [128, 512], F32); b_uv = Buf("uv")
            avn = S.sb("avn", [128, 256], BF16); b_avn = Buf("avn")
            ytok = S.sb("ytok", [128, 4, 256], BF16); b_ytok = [Buf("ytok%d" % i) for i in range(4)]
            yT = S.sb("yT", [128, 8, 512], BF16); b_yT = Buf("yT")
            ycomb = S.sb("ycomb", [128, 4, 512], F32); b_ycs = [Buf("ycomb%d" % i) for i in range(4)]
            impS = S.sb("imp", [128, 4, 64], F32); b_imp = Buf("imp")
            lg = [S.sb("lg%d" % i, [128, 512], F32) for i in range(2)]; b_lg = [Buf("lg0"), Buf("lg1")]
            PT = [S.sb("PT%d" % i, [128, 512], BF16) for i in range(3)]; b_PT = [Buf("PT%d" % i) for i in range(3)]
            bct = [S.sb("bct0", [128, 512], F32)] * 2; b_bct = [Buf("bct0")] * 2
            nmz = [S.sb("nmz%d" % i, [128, 512], BF16) for i in range(2)]; b_nmT = Buf("nmT")
            nmp = [S.sb("nmp%d" % i, [128, 128], BF16) for i in range(2)]
            fw = [ycomb[:, i, :] for i in range(4)]; b_fw = b_ycs
            xcb = S.sb("xcb", [128, 512], BF16); b_xcb = Buf("xcb")
            hid = S.sb("hid", [128, 4, 64], BF16); b_hid = Buf("hid")
            hidv = S.sb("hidv", [128, 2, 128], BF16); b_hidv = Buf("hidv")
            onesp = S.sb("onesp", [1, 4, 128], BF16); b2v_row = S.sb("b2vr", [1, 64], BF16)
            sc = S.sb("sc", [128, 64], F32); sc2 = S.sb("sc2", [128, 64], F32); m8 = S.sb("m8", [128, 16], F32)
            nm = S.sb("nm", [128, 64], BF16); b_sc = Buf("sc")
            z4 = S.sb("z4", [128, 8], F32); b_z4 = Buf("z4")

            b_win, b_wout, b_cmpw = Buf("win"), Buf("wout"), Buf("cmpw")

            def cdma(q, out, in_, buf=None, **kw):
                S.dma(q, out, in_, writes=[buf if buf is not None else cb], **kw)
            W = D["w_in"][L]
            for c in range(8):
                rows = slice(c * 128, (c + 1) * 128)
                cdma("pool", win_sb[:, c, 0:1024], W[rows, 0:1024], buf=b_win)
                for r in range(4):
                    cdma("pool", win_sb[:, c, 1024 + r * 128:1024 + r * 128 + 64], W[rows, 1024 + r * 64:1024 + r * 64 + 64], buf=b_win)
                    cdma("pool", win_sb[:, c, 1024 + r * 128 + 64:1024 + r * 128 + 128], W[rows, 1024 + (4 + r) * 64:1024 + (4 + r) * 64 + 64], buf=b_win)
                cdma("pool", win_sb[:, c, 1536:NIN], W[rows, 1536:NIN], buf=b_win)
                cdma("pool", wout_sb[:, c, :], D["w_out"][L, rows, :], buf=b_wout)
            for kv in range(2):
                src_w1 = D["cmp_w1"][L, kv].rearrange("(l d) j -> d l j", d=64)
                for half in range(2):
                    for lq in range(4):
                        cdma("pool", w1_sb[half * 64:(half + 1) * 64, kv * 32 + lq * 8:kv * 32 + lq * 8 + 8, :], src_w1[:, lq * 8:(lq + 1) * 8, :], buf=b_cmpw)
            S.op("dve", lambda e: e.memset(w2k_pad[:], 0.0), writes=[b_cmpw])
            for g in range(2):
                cdma("pool", w2k_pad[:, g, g * 64:(g + 1) * 64], D["cmp_w2"][L, 0], buf=b_cmpw)
            cdma("pool", w2v_sb[:], D["cmp_w2"][L, 1], buf=b_cmpw)
            for half in range(2):
                cdma("pool", posT_sb[half * 64:(half + 1) * 64, :, :], D["posT"][L], buf=b_cmpw)
            cdma("sp", b1c[:], D["b1T"][L]); cdma("sp", b2k[:], D["b2kk"][L]); cdma("sp", b2vB[:], D["b2vB"][L], buf=b_cmpw)
            cdma("sp", identf[:], D["ident"])
            cdma("sp", m512_sb[:], D["m512"])
            for q in range(4):
                S.dma("pool", ksE[0][64:128, q * 1024:(q + 1) * 1024], D["eall"][0:64, q * 1024:(q + 1) * 1024], writes=[b_ksT])
                S.dma("pool", ksE[1][0:64, q * 1024:(q + 1) * 1024], D["eall"][0:64, q * 1024:(q + 1) * 1024], writes=[b_ksT])
            cdma("sp", c31_sb[:], D["c31"]); cdma("sp", tmax_sb[:], D["tmax"]); cdma("sp", tmin_sb[:], D["tmin"])
            cdma("sp", cw_sb[:], D["conv_wT"][L])
            cdma("sp", cbias[:], D["conv_b"][L])
            cdma("sp", ba_sb[:], D["lru_ba"][L])
            cdma("sp", bx_sb[:], D["lru_bx"][L])
            cdma("sp", lamc[:], D["lru_lam"][L])
            S.op("dve", lambda e: e.memset(bda[:], 0.0), writes=[cb])
            S.op("dve", lambda e: e.memset(bdx[:], 0.0), writes=[cb])
            for gi in range(4):
                i, a = gi // 2, gi % 2
                cdma("pool", bda[a * 64:(a + 1) * 64, i, a * 64:(a + 1) * 64], D["lru_wa"][L, gi])
                cdma("pool", bdx[a * 64:(a + 1) * 64, i, a * 64:(a + 1) * 64], D["lru_wx"][L, gi])
            cdma("sp", sguB[:], D["sguB"][L]); cdma("sp", bsT[:], D["sgu_bT"][L])
            cdma("sp", gpre[:], D["ngB"][L, 2]); cdma("sp", gpost[:], D["ngB"][L, 3])
            S.op("act", lambda e: e.activation(lamc[:], lamc[:], AF.Exp, scale=-1.0), reads=[cb], writes=[cb])
            S.op("act", lambda e: e.activation(lamc[:], lamc[:], AF.Ln, bias=ones1[:, 0:1]), reads=[cb, b_const], writes=[cb])
            S.op("dve", lambda e: e.tensor_scalar(lamc[:], lamc[:], -8.0, None, ALU.mult), reads=[cb], writes=[cb])
            trl = fw[0]
            S.dma("sp", trl[:, 0:128], D["tril"], writes=[b_fw[0]])
            for g in range(4):
                S.dma("sp", fw[1][:, 0:128], D["sgu_w"][L, g], writes=[b_fw[1]])
                S.op("dve", lambda e: e.tensor_tensor(xn[:, 0:128], fw[1][:, 0:128], trl[:, 0:128], ALU.mult), reads=[b_fw[0], b_fw[1]], writes=[b_xn])
                S.op("pe", lambda e: e.transpose(pt[:, 0:128], xn[:, 0:128], identb[:]), reads=[b_xn, b_const], writes=[b_pt])
                S.op("act", lambda e, g=g: e.activation(wsT[:, g, :], pt[:, 0:128], AF.Identity), reads=[b_pt], writes=[cb])
            pbk = PSB[0]
            for kv in range(2):
                for l in range(32):
                    S.op("pe", lambda e, kv=kv, l=l: e.matmul(pbk[0][:, kv:kv + 1], w1_sb[0:64, kv * 32 + l, :], posT_sb[0:64, kv, l:l + 1],
                                                              start=(kv == 0 and l == 0), stop=(kv == 1 and l == 31), skip_group_check=True),
                         reads=[b_cmpw], writes=[pbk[1]])
            S.op("dve", lambda e: e.tensor_tensor(b1c[:], b1c[:], pbk[0][:, 0:2], ALU.add), reads=[b_cmpw, pbk[1]], writes=[b_cmpw])
            S.op("dve", lambda e: e.memset(vs_aug[:], 1.0), writes=[b_vs])
            S.op("dve", lambda e: e.memset(vw_aug[:], 1.0), writes=[b_vw])
            S.op("dve", lambda e: e.memset(kcmpT[:], 0.0), writes=[b_kcmp])
            S.op("dve", lambda e: e.memset(cv_aug[:], 0.0), writes=[b_cv])
            S.op("dve", lambda e: e.memset(cv_aug[:, :, :, 128:129], 1.0), writes=[b_cv])
            for stt in range(2):
                for g in range(2):
                    S.dma("pool", cv_aug[:, stt, g, 0:64], D["ovl"][stt * 128:(stt + 1) * 128, :], writes=[b_cv])
            S.op("dve", lambda e: e.memset(rz[:], 0.0), writes=[b_roll])
            S.op("dve", lambda e: e.memset(qz[:], 0.0), writes=[b_qT])
            for i_ in range(2):
                S.op("dve", lambda e, i_=i_: e.memset(nmp[i_][:], 0.0), writes=[b_sc])
            S.op("dve", lambda e: e.memset(xbT[:], 0.0), writes=[b_xb])
            S.op("dve", lambda e: e.memset(carry[:], 0.0), writes=[b_carry])
            S.op("dve", lambda e: e.memset(hid[:], 0.0), writes=[b_hid])
            S.op("dve", lambda e: e.memset(onesp[:], 0.0), writes=[cb])
            for a4_ in range(4):
                S.op("dve", lambda e, a4_=a4_: e.memset(onesp[0:1, a4_, 32 * a4_:32 * a4_ + 32], 1.0), reads=[cb], writes=[cb])
            S.dma("pool", b2v_row[:], D["b2vB"][L, 0:1, :], writes=[cb])

            rot = [0]

            def nextbank():
                b = PSB[rot[0] % 3]
                rot[0] += 1
                return b

            ptc = [0]
            lgc = [0]
            gsr = [0]

            for T in range(8):
                t0 = T * 512
                for s in range(4):
                    tok = T * 4 + s
                    prenorm_xT(src, tok, hin[:, 0, :], b_hin[0], gpre[:], cb, junk, b_junk, small, b_small, xn, b_xn, xT, b_xT, s)

                def fm_proj(c0, ncol, evac):
                    pb = nextbank()
                    for kc in range(8):
                        S.op("pe", lambda e, pb=pb, kc=kc: e.matmul(pb[0][0:ncol, 0:512], win_sb[:, kc, c0:c0 + ncol], xT[:, kc, :], start=(kc == 0), stop=(kc == 7)),
                             reads=[b_win, b_xT], writes=[pb[1]])
                    evac(pb)
                for i in range(2):
                    fm_proj(512 + i * 128, 128, lambda pb, i=i: S.op("act", lambda e: e.activation(xbT[:, i, 3:515], pb[0][:, 0:512], AF.Identity), reads=[pb[1]], writes=[b_xb]))
                    fm_proj(768 + i * 128, 128, lambda pb, i=i: S.op("act", lambda e: e.activation(gateT[:, i, :], pb[0][:, 0:512], AF.Gelu_apprx_tanh), reads=[pb[1]], writes=[b_gate]))
                S.op("pool", lambda e: e.memset(qz[64:128, 0:4, :], 0.0), reads=[b_qT], writes=[b_qT])
                S.op("pool", lambda e: e.memset(qz[0:64, 4:8, :], 0.0), reads=[b_qT], writes=[b_qT])
                for r in range(4):
                    def _qev(pb, r=r):
                        S.op("act", lambda e: e.activation(qz[0:64, r, :], pb[0][0:64, 0:512], AF.Identity), reads=[pb[1]], writes=[b_qT])
                        S.op("dve", lambda e: e.tensor_copy(qz[64:128, 4 + r, :], pb[0][64:128, 0:512]), reads=[pb[1]], writes=[b_qT])
                    fm_proj(1024 + r * 128, 128, _qev)
                for kv_ in range(2):
                    def _rev(pb, kv_=kv_):
                        S.op("act", lambda e: e.activation(rz[0:64, kv_ * 2, 16:528], pb[0][0:64, 0:512], AF.Identity), reads=[pb[1]], writes=[b_roll])
                        S.op("dve", lambda e: e.tensor_copy(rz[64:128, kv_ * 2 + 1, 16:528], pb[0][64:128, 0:512]), reads=[pb[1]], writes=[b_roll])
                    fm_proj(1536 + 128 * kv_, 128, _rev)
                def _kev(pb, t0=t0):
                    S.op("dve", lambda e: e.tensor_copy(ksE[0][0:64, t0:t0 + 512], pb[0][0:64, 0:512]), reads=[pb[1]], writes=[b_ksT])
                    S.op("act", lambda e: e.activation(ksE[1][64:128, t0:t0 + 512], pb[0][64:128, 0:512], AF.Identity), reads=[pb[1]], writes=[b_ksT])
                fm_proj(1792, 128, _kev)
                fm_proj(2048, 128, lambda pb, t0=t0: S.op("act", lambda e: e.activation(kwT[:, t0:t0 + 512], pb[0][:, 0:512], AF.Identity), reads=[pb[1]], writes=[b_kwT]))

                S.skipping = 'a' in SKIP
                for s in range(4):
                    tok = T * 4 + s
                    ts_ = slice(s * 128, (s + 1) * 128)
                    pa = nextbank()
                    for kc in range(8):
                        S.op("pe", lambda e, pa=pa, kc=kc, ts_=ts_: e.matmul(pa[0][:, 0:512], xT[:, kc, ts_], win_sb[:, kc, 0:512], start=(kc == 0), stop=(kc == 7)),
                             reads=[b_win, b_xT], writes=[pa[1]])
                    S.op("act", lambda e, pa=pa: e.activation(uv[:], pa[0][:, 0:512], AF.Gelu_apprx_tanh), reads=[pa[1]], writes=[b_uv])
                    S.op("act", lambda e: e.activation(junk[:, 0:256], uv[:, 256:512], AF.Square, accum_out=small[:, 6:7]), reads=[b_uv], writes=[b_junk, b_small])
                    rstd_from_ss(small[:, 6:7], small[:, 7:8], 256.0, [b_small], [b_small], None)
                    S.op("dve", lambda e: e.scalar_tensor_tensor(out=avn[:], in0=uv[:, 256:512], scalar=small[:, 7:8], in1=sguB[:], op0=ALU.mult, op1=ALU.mult),
                         reads=[b_uv, b_small, cb], writes=[b_avn])
                    pm = nextbank()
                    for g in range(4):
                        S.op("pe", lambda e, pm=pm, g=g: e.matmul(pm[0][:, g * 64:(g + 1) * 64], wsT[:, g, :], avn[:, g * 64:(g + 1) * 64], start=True, stop=True,
                                                                  skip_group_check=True),
                             reads=[cb, b_avn], writes=[pm[1]])
                    for g in range(4):
                        S.op("dve", lambda e, pm=pm, g=g, s=s: e.scalar_tensor_tensor(out=ytok[:, s, g * 64:(g + 1) * 64], in0=pm[0][:, g * 64:(g + 1) * 64],
                                                                                      scalar=bsT[:, g:g + 1], in1=uv[:, g * 64:(g + 1) * 64], op0=ALU.add, op1=ALU.mult),
                             reads=[pm[1], cb, b_uv], writes=[b_ytok[s]])
                    pv = nextbank()
                    for kc in range(8):
                        S.op("pe", lambda e, pv=pv, kc=kc, ts_=ts_: e.matmul(pv[0][:, 0:128], xT[:, kc, ts_], win_sb[:, kc, 1920:2048], start=(kc == 0), stop=(kc == 7),
                                                                            skip_group_check=True),
                             reads=[b_win, b_xT], writes=[pv[1]])
                    for kc in range(8):
                        S.op("pe", lambda e, pv=pv, kc=kc, ts_=ts_: e.matmul(pv[0][:, 128:280], xT[:, kc, ts_], win_sb[:, kc, 2176:2328], start=False, stop=(kc == 7),
                                                                            skip_group_check=True),
                             reads=[b_win, b_xT], writes=[pv[1]])
                    for g_ in range(2):
                        S.op("act", lambda e, pv=pv, tok=tok, g_=g_: e.activation(vs_aug[:, tok, g_, 0:64], pv[0][:, g_ * 64:(g_ + 1) * 64], AF.Identity),
                             reads=[pv[1]], writes=[b_vs])
                        S.op("act", lambda e, pv=pv, tok=tok, g_=g_: e.activation(vw_aug[:, tok, g_, 0:64], pv[0][:, 128 + g_ * 64:128 + (g_ + 1) * 64], AF.Identity),
                             reads=[pv[1]], writes=[b_vw])
                    S.op("act", lambda e, pv=pv, s=s: e.activation(sg[:, s, :], pv[0][:, 256:280], AF.Sigmoid), reads=[pv[1]], writes=[b_sg])

                S.skipping = 'b' in SKIP
                for i in range(2):
                    xc, ig, aa, bb = fw[0], fw[1], fw[2], fw[3]
                    S.op("dve", lambda e, i=i: e.tensor_scalar(xc[:], xbT[:, i, 0:512], cw_sb[:, i, 0:1], cbias[:, i:i + 1], ALU.mult, ALU.add),
                         reads=[b_xb, cb], writes=[b_fw[0]])
                    for k in range(1, 4):
                        S.op("dve", lambda e, i=i, k=k: e.scalar_tensor_tensor(out=xc[:], in0=xbT[:, i, k:k + 512], scalar=cw_sb[:, i, k:k + 1], in1=xc[:],
                                                                               op0=ALU.mult, op1=ALU.add), reads=[b_xb, cb, b_fw[0]], writes=[b_fw[0]])
                    S.op("dve", lambda e, i=i: e.tensor_copy(xbT[:, i, 0:3], xbT[:, i, 512:515]), reads=[b_xb], writes=[b_xb])
                    S.op("act", lambda e: e.activation(xcb[:], xc[:], AF.Identity), reads=[b_fw[0]], writes=[b_xcb])
                    pr = nextbank()
                    S.op("pe", lambda e, pr=pr, i=i: e.matmul(pr[0][:, 0:512], bda[:, i, :], xcb[:], start=True, stop=True), reads=[cb, b_xcb], writes=[pr[1]])
                    pi = nextbank()
                    S.op("pe", lambda e, pi=pi, i=i: e.matmul(pi[0][:, 0:512], bdx[:, i, :], xcb[:], start=True, stop=True), reads=[cb, b_xcb], writes=[pi[1]])
                    S.op("act", lambda e, pr=pr, i=i: e.activation(aa[:], pr[0][:, 0:512], AF.Sigmoid, bias=ba_sb[:, i:i + 1]), reads=[pr[1], cb], writes=[b_fw[2]])
                    S.op("act", lambda e, pi=pi, i=i: e.activation(ig[:], pi[0][:, 0:512], AF.Sigmoid, bias=bx_sb[:, i:i + 1]), reads=[pi[1], cb], writes=[b_fw[1]])
                    S.op("act", lambda e, i=i: e.activation(aa[:], aa[:], AF.Exp, scale=lamc[:, i:i + 1]), reads=[b_fw[2], cb], writes=[b_fw[2]])
                    S.op("dve", lambda e: e.tensor_tensor(bb[:], aa[:], aa[:], ALU.mult), reads=[b_fw[2]], writes=[b_fw[3]])
                    S.op("dve", lambda e: e.tensor_scalar(bb[:], bb[:], -1.0, 1.0, ALU.mult, ALU.add), reads=[b_fw[3]], writes=[b_fw[3]])
                    S.op("act", lambda e: e.activation(bb[:], bb[:], AF.Sqrt), reads=[b_fw[3]], writes=[b_fw[3]])
                    S.op("dve", lambda e: e.tensor_tensor(ig[:], ig[:], xc[:], ALU.mult), reads=[b_fw[1], b_fw[0]], writes=[b_fw[1]])
                    S.op("dve", lambda e: e.tensor_tensor(bb[:], bb[:], ig[:], ALU.mult), reads=[b_fw[3], b_fw[1]], writes=[b_fw[3]])
                    S.op("dve", lambda e, i=i: e.tensor_tensor_scan(xc[:], aa[:], bb[:], carry[:, i:i + 1], ALU.mult, ALU.add),
                         reads=[b_fw[2], b_fw[3], b_carry], writes=[b_fw[0]])
                    S.op("dve", lambda e, i=i: e.tensor_copy(carry[:, i:i + 1], xc[:, 511:512]), reads=[b_fw[0]], writes=[b_carry])
                    S.op("dve", lambda e, i=i: e.tensor_tensor(yT[:, 2 + i, :], xc[:], gateT[:, i, :], ALU.mult), reads=[b_fw[0], b_gate], writes=[b_yT])

                S.skipping = 'c' in SKIP
                phs = [nextbank(), nextbank()]
                for g in range(2):
                    ph = phs[g]
                    for kv, roll in ((0, None), (1, None)):
                        col = kv * 32
                        for l in range(32):
                            S.op("pe", lambda e, ph=ph, kv=kv, g=g, l=l, roll=roll, col=col: e.matmul(
                                ph[0][:, col:col + 32], w1_sb[:, kv * 32 + l, :], rz[:, kv * 2 + g, l:l + 16 * 31 + 1:16],
                                start=(kv == 0 and l == 0), stop=(kv == 1 and l == 31), skip_group_check=True), reads=[b_cmpw, b_roll], writes=[ph[1]])
                    for kv in range(2):
                        S.op("act", lambda e, ph=ph, kv=kv, g=g: e.activation(hid[:, kv * 2 + g, 32:64], ph[0][:, kv * 32:kv * 32 + 32],
                                                                              AF.Gelu_apprx_tanh, bias=b1c[:, kv:kv + 1]), reads=[ph[1], b_cmpw], writes=[b_hid])
                S.op("dve", lambda e: e.tensor_copy(rz[:, :, 0:16], rz[:, :, 512:528]), reads=[b_roll], writes=[b_roll])
                pk = nextbank()
                for g in range(2):
                    S.op("pe", lambda e, pk=pk, g=g: e.matmul(pk[0][:, 0:32], w2k_pad[:, g, :], hid[:, g, 32:64], start=(g == 0), stop=(g == 1)),
                         reads=[b_cmpw, b_hid], writes=[pk[1]])
                S.op("act", lambda e, pk=pk, T=T: e.activation(kcmpT[:, 32 * T:32 * T + 32], pk[0][:, 0:32], AF.Identity, bias=b2k[:, 0:1]),
                     reads=[pk[1], b_cmpw], writes=[b_kcmp])
                a4 = T % 4
                stt = T // 4
                S.op("dve", lambda e: e.memset(hidv[:], 0.0), reads=[b_hidv], writes=[b_hidv])
                S.op("dve", lambda e, a4=a4: e.tensor_copy(hidv[:, :, 32 * a4:32 * a4 + 32], hid[:, 2:4, 32:64]), reads=[b_hid, b_hidv], writes=[b_hidv])
                pvv = nextbank()
                for g in range(2):
                    S.op("pe", lambda e, pvv=pvv, g=g: e.matmul(pvv[0][:, g * 64:(g + 1) * 64], hidv[:, g, :], w2v_sb[:], start=(g == 0), stop=False,
                                                                  skip_group_check=True), reads=[b_cmpw, b_hidv], writes=[pvv[1]])
                    S.op("pe", lambda e, pvv=pvv, g=g, a4=a4: e.matmul(pvv[0][:, g * 64:(g + 1) * 64], onesp[0:1, a4, :], b2v_row[0:1, :], start=False, stop=(g == 1),
                                                                      skip_group_check=True), reads=[cb], writes=[pvv[1]])
                for g in range(2):
                    S.op("dve", lambda e, pvv=pvv, g=g, stt=stt: e.tensor_tensor(cv_aug[:, stt, g, 64:128], cv_aug[:, stt, g, 64:128], pvv[0][:, g * 64:(g + 1) * 64], ALU.add),
                         reads=[pvv[1], b_cv], writes=[b_cv])

                S.skipping = 'n' in SKIP
                nslot = 32 * (T + 1)
                stiles = [(0, min(nslot, 128))] + ([(1, nslot - 128)] if nslot > 128 else [])
                for g in range(2):
                    base = 64 * g
                    bs_ = slice(base, base + 64)
                    for r in range(4):
                        h = 4 * g + r
                        ets = []
                        for (stt_, M) in stiles:
                            pb = nextbank()
                            S.op("pe", lambda e, pb=pb, stt_=stt_, M=M, r=r, g=g: e.matmul(pb[0][0:M, 0:512], kcmpT[:, stt_ * 128:stt_ * 128 + M], qz[:, 4 * g + r, :],
                                                                                             start=True, stop=True), reads=[b_kcmp, b_qT], writes=[pb[1]])
                            bi = lgc[0] % 2; lgc[0] += 1
                            S.dma("sp", bct[bi][0:M, :], D["bc"][h, stt_ * 128:stt_ * 128 + M, t0:t0 + 512], writes=[b_bct[bi]])
                            S.op("dve", lambda e, pb=pb, bi=bi, M=M: e.scalar_tensor_tensor(out=lg[bi][0:M, :], in0=pb[0][0:M, 0:512], scalar=0.125, in1=bct[bi][0:M, :],
                                                                                            op0=ALU.mult, op1=ALU.add), reads=[pb[1], b_bct[bi]], writes=[b_lg[bi]])
                            pi_ = ptc[0] % 3; ptc[0] += 1
                            S.op("act", lambda e, bi=bi, pi_=pi_, M=M: e.activation(PT[pi_][0:M, :], lg[bi][0:M, :], AF.Exp), reads=[b_lg[bi]], writes=[b_PT[pi_]])
                            ets.append((pi_, stt_, M))
                        cbk = [PSB[5], PSB[6]]
                        for s in range(4):
                            bk = cbk[s // 2]
                            co = (s % 2) * 129
                            for j, (pi_, stt_, M) in enumerate(ets):
                                S.op("pe", lambda e, bk=bk, co=co, pi_=pi_, stt_=stt_, M=M, s=s, j=j, g=g, ets=ets: e.matmul(
                                    bk[0][:, co:co + 129], PT[pi_][0:M, s * 128:(s + 1) * 128], cv_aug[0:M, stt_, g, :],
                                    start=(s % 2 == 0 and j == 0), stop=(s % 2 == 1 and j == len(ets) - 1), skip_group_check=True),
                                    reads=[b_PT[pi_], b_cv], writes=[bk[1]])
                        for kb in range(2):
                            bk = cbk[kb]
                            zb = bk[0][:, 0:258].rearrange("p (s c) -> p s c", s=2)[:, :, 128:129]
                            zo = z4[:, 2 * kb:2 * kb + 2].rearrange("p (s c) -> p s c", c=1)
                            S.op("dve", lambda e, zb=zb, zo=zo: e.tensor_scalar(zo, zb, 1e-30, None, ALU.max), reads=[bk[1]], writes=[b_z4])
                        S.op("dve", lambda e: e.reciprocal(z4[:, 0:4], z4[:, 0:4]), reads=[b_z4], writes=[b_z4])
                        for s in range(4):
                            bk = cbk[s // 2]
                            co = (s % 2) * 129
                            if r == 0:
                                S.op("dve", lambda e, bk=bk, co=co, s=s: e.tensor_scalar(impS[:, s, :], bk[0][:, co:co + 64], z4[:, s:s + 1], None, ALU.mult),
                                     reads=[bk[1], b_z4], writes=[b_imp])
                            else:
                                S.op("dve", lambda e, bk=bk, co=co, s=s: e.scalar_tensor_tensor(out=impS[:, s, :], in0=bk[0][:, co:co + 64], scalar=z4[:, s:s + 1], in1=impS[:, s, :],
                                                                                                op0=ALU.mult, op1=ALU.add), reads=[bk[1], b_z4, b_imp], writes=[b_imp])
                            S.op("dve", lambda e, bk=bk, co=co, s=s, h=h: e.tensor_scalar(ycomb[:, s, h * 64:(h + 1) * 64], bk[0][:, co + 64:co + 128], z4[:, s:s + 1], sg[:, s, h:h + 1],
                                                                                          ALU.mult, ALU.mult), reads=[bk[1], b_z4, b_sg], writes=[b_ycs[s]])
                    for s in range(4):
                        itile = T * 4 + s
                        off = 63 - 2 * itile
                        S.op("dve", lambda e, s=s, off=off: e.tensor_tensor(sc[:], impS[:, s, :], tmax_sb[:, off:off + 64], ALU.max), reads=[b_imp, cb], writes=[b_sc])
                        S.op("dve", lambda e, off=off: e.tensor_tensor(sc[:], sc[:], tmin_sb[:, off:off + 64], ALU.min), reads=[b_sc, cb], writes=[b_sc])
                        S.op("dve", lambda e: e.memset(sc[:, 0:1], 1e4), reads=[b_sc], writes=[b_sc])
                        S.op("dve", lambda e: e.max(out=m8[:, 0:8], in_=sc[:]), reads=[b_sc], writes=[b_sc])
                        S.op("dve", lambda e: e.match_replace(out=sc2[:], in_to_replace=m8[:, 0:8], in_values=sc[:], imm_value=-3.0e38), reads=[b_sc], writes=[b_sc])
                        S.op("dve", lambda e: e.max(out=m8[:, 8:16], in_=sc2[:]), reads=[b_sc], writes=[b_sc])
                        S.op("dve", lambda e, g=g: e.tensor_scalar(nmp[g][:, 64 * (1 - g):64 * (1 - g) + 64], sc[:], m8[:, 15:16], NEGM, ALU.is_lt, ALU.mult), reads=[b_sc], writes=[b_sc])
                        S.op("pe", lambda e, g=g: e.transpose(pt[:, 0:128], nmp[g][:], identb[:]), reads=[b_sc, b_const], writes=[b_pt])
                        S.op("act", lambda e, g=g, s=s: e.activation(nmz[g][:, s * 128:(s + 1) * 128], pt[:, 0:128], AF.Identity), reads=[b_pt], writes=[b_nmT])
                    wjobs, sjobs = [], []

                    def mk_hook(r_, g=g):
                        def mask_hook():
                            oh = slice(64, 128) if g == 0 else slice(0, 64)
                            S.op("pool", lambda e: e.tensor_copy(qz[oh, 4 * g + r_, :], nmz[g][oh, :]), reads=[b_nmT, b_qT], writes=[b_qT])
                        return mask_hook
                    for r in range(4):
                        h = 4 * g + r
                        selb, winb = PSB[3], PSB[4]
                        st_ = {}

                        def mk_sel(kt, gsc, b_gsc, h=h, g=g, selb=selb, first=False, load=False, hook=None, post=None):
                            Dd = t0 - 128 * kt
                            f0 = max(0, -Dd)
                            J = {}

                            def A():
                                if hook is not None:
                                    hook()
                                if load:
                                    S.dma("pool", gsc[:], D["gs"][h], writes=[b_gsc])
                                pb = nextbank()
                                J["pb"] = pb
                                S.op("pe", lambda e: e.matmul(pb[0][:, f0:512], ksE[g][:, kt * 128:(kt + 1) * 128], qz[:, h, f0:512], start=True, stop=True),
                                     reads=[b_ksT, b_qT], writes=[pb[1]])

                            def B():
                                pb = J["pb"]
                                pi_ = ptc[0] % 3; ptc[0] += 1
                                J["pi"] = pi_
                                if Dd <= 896:
                                    bi = lgc[0] % 2; lgc[0] += 1
                                    x0 = Dd + 384
                                    S.op("dve", lambda e: e.scalar_tensor_tensor(out=lg[bi][:, f0:512], in0=pb[0][:, f0:512], scalar=0.125,
                                                                                 in1=gsc[:, x0 + f0:x0 + 512], op0=ALU.mult, op1=ALU.add),
                                         reads=[pb[1], b_gsc], writes=[b_lg[bi]])
                                    S.op("act", lambda e: e.activation(PT[pi_][:, f0:512], lg[bi][:, f0:512], AF.Exp), reads=[b_lg[bi]], writes=[b_PT[pi_]])
                                else:
                                    S.op("act", lambda e: e.activation(PT[pi_][:, 0:512], pb[0][:, 0:512], AF.Exp, bias=c31_sb[:, h:h + 1], scale=0.125),
                                         reads=[pb[1], cb], writes=[b_PT[pi_]])

                            def C():
                                pi_ = J["pi"]
                                for s in range(f0 // 128, 4):
                                    S.op("pe", lambda e, s=s: e.matmul(selb[0][:, s * 65:(s + 1) * 65], PT[pi_][:, s * 128:(s + 1) * 128], vs_aug[:, kt, g, :],
                                                                        start=(first and s == 0), stop=False, skip_group_check=True),
                                         reads=[b_PT[pi_], b_vs], writes=[selb[1]])
                            return (A, B, C, post)

                        def mk_win(kt, gsc, b_gsc, h=h, g=g, winb=winb, first=False, post=None, load=False):
                            Dd = t0 - 128 * kt
                            lo = max(0, -Dd)
                            hi = min(512, 639 - Dd)
                            J = {}

                            def A():
                                if load:
                                    S.dma("pool", gsc[:], D["gs"][h], writes=[b_gsc])
                                pb = nextbank()
                                J["pb"] = pb
                                S.op("pe", lambda e: e.matmul(pb[0][:, lo:hi], kwT[:, kt * 128:(kt + 1) * 128], qz[:, h, lo:hi], start=True, stop=True),
                                     reads=[b_kwT, b_qT], writes=[pb[1]])

                            def B():
                                pb = J["pb"]
                                bi = lgc[0] % 2; lgc[0] += 1
                                x0 = Dd + 384
                                S.op("dve", lambda e: e.scalar_tensor_tensor(out=lg[bi][:, lo:hi], in0=pb[0][:, lo:hi], scalar=0.125,
                                                                             in1=gsc[:, x0 + lo:x0 + hi], op0=ALU.mult, op1=ALU.add),
                                     reads=[pb[1], b_gsc], writes=[b_lg[bi]])
                                if Dd >= 128:
                                    S.op("dve", lambda e: e.tensor_tensor(lg[bi][:, lo:hi], lg[bi][:, lo:hi], m512_sb[:, Dd - 128 + lo:Dd - 128 + hi], ALU.add),
                                         reads=[b_lg[bi], cb], writes=[b_lg[bi]])
                                pi_ = ptc[0] % 3; ptc[0] += 1
                                J["pi"] = pi_
                                S.op("act", lambda e: e.activation(PT[pi_][:, lo:hi], lg[bi][:, lo:hi], AF.Exp), reads=[b_lg[bi]], writes=[b_PT[pi_]])

                            def C():
                                pi_ = J["pi"]
                                fw_ = first
                                for s in range(4):
                                    a_ = max(lo, 128 * s)
                                    b__ = min(hi, 128 * s + 128)
                                    if b__ <= a_:
                                        continue
                                    S.op("pe", lambda e, s=s, a_=a_, b__=b__, fw_=fw_: e.matmul(winb[0][a_ - 128 * s:b__ - 128 * s, s * 65:(s + 1) * 65], PT[pi_][:, a_:b__],
                                                                                               vw_aug[:, kt, g, :], start=fw_, stop=False, skip_group_check=True),
                                         reads=[b_PT[pi_], b_vw], writes=[winb[1]])
                                    fw_ = False
                            return (A, B, C, post)

                        def mk_post(which_, h=h, selb=selb, winb=winb):
                            def post():
                                if dbg and T < 2:
                                    for bi_, bk in ((0, selb), (1, winb)) if False else [((0, selb), (1, winb))[which_]]:
                                        S.op("act", lambda e, bk=bk: e.activation(lg[0][:, 0:260], bk[0][:, 0:260], AF.Identity), reads=[bk[1]], writes=[b_lg[0]])
                                        S.dma("sp", ydR[T, h, bi_], lg[0][:, 0:260], reads=[b_lg[0]])
                                for (bk, goff) in [((selb, 8), (winb, 16))[which_]]:
                                    zv = bk[0][:, 0:260].rearrange("p (s c) -> p s c", s=4)[:, :, 64:65]
                                    z3 = z4[:, 0:4].rearrange("p (s c) -> p s c", c=1)
                                    f3 = z4[:, 4:8].rearrange("p (s c) -> p s c", c=1)
                                    S.op("dve", lambda e, zv=zv, z3=z3: e.reciprocal(z3, zv), reads=[bk[1]], writes=[b_z4])
                                    S.op("dve", lambda e, z3=z3, f3=f3, goff=goff: e.tensor_tensor(f3, z3, sg[:, :, goff + h:goff + h + 1], ALU.mult), reads=[b_z4, b_sg], writes=[b_z4])
                                    for s in range(4):
                                        S.op("dve", lambda e, bk=bk, s=s: e.scalar_tensor_tensor(out=ycomb[:, s, h * 64:(h + 1) * 64], in0=bk[0][:, s * 65:s * 65 + 64], scalar=z4[:, 4 + s:5 + s],
                                                                                               in1=ycomb[:, s, h * 64:(h + 1) * 64], op0=ALU.mult, op1=ALU.add),
                                             reads=[bk[1], b_z4, b_ycs[s]], writes=[b_ycs[s]])
                            return post

                        kts = [4 * T] + [k for k in range(max(0, 4 * T - 4), 4 * T + 4) if k != 4 * T]
                        gi_ = gsr[0] % 2; gsr[0] += 1
                        for j_, kt in enumerate(kts):
                            wjobs.append(mk_win(kt, gs_cur[gi_], b_gs[gi_], first=(j_ == 0), load=(j_ == 0), post=(mk_post(1) if j_ == len(kts) - 1 else None)))
                        for kt in range(0, 4 * T + 4):
                            wjobs.append(mk_sel(kt, gs_cur[gi_], b_gs[gi_], first=(kt == 0), load=False,
                                                hook=(mk_hook(r) if kt == 0 else None), post=(mk_post(0) if kt == 4 * T + 3 else None)))
                    jobs = wjobs
                    nj = len(jobs)
                    for i_ in range(nj + 2):
                        if i_ < nj:
                            jobs[i_][0]()
                        if 0 <= i_ - 1 < nj:
                            jobs[i_ - 1][1]()
                        if 0 <= i_ - 2 < nj:
                            jobs[i_ - 2][2]()
                            if jobs[i_ - 2][3] is not None:
                                jobs[i_ - 2][3]()

                S.skipping = False
                if dbg and L == stop_after[0]:
                    for s in range(4):
                        tok = T * 4 + s
                        S.dma("pool", ydA[tok * 128:(tok + 1) * 128, :], ytok[:, s, :], reads=[b_ytok[s]])
                        S.dma("sp", ydC[tok * 128:(tok + 1) * 128, :], ycomb[:, s, :], reads=[b_ycs[s]])
                    for i in range(2):
                        S.dma("pool", ydB[i * 128:(i + 1) * 128, t0:t0 + 512], yT[:, 2 + i, :], reads=[b_yT])
                for s in range(4):
                    for c in (0, 1):
                        S.op("pe", lambda e, s=s, c=c: e.transpose(pt[:, c * 128:(c + 1) * 128], ytok[:, s, c * 128:(c + 1) * 128], identb[:]), reads=[b_ytok[s], b_const], writes=[b_pt])
                    S.op("act", lambda e, s=s: e.activation(yT[:, 0:2, s * 128:(s + 1) * 128], pt[:, 0:256].rearrange("p (c n) -> p c n", c=2), AF.Identity), reads=[b_pt], writes=[b_yT])
                    pf = nextbank()
                    for c in range(4):
                        S.op("pe", lambda e, s=s, c=c, pf=pf: e.transpose(pf[0][:, c * 128:(c + 1) * 128], ycomb[:, s, c * 128:(c + 1) * 128], identf[:]), reads=[b_ycs[s], cb], writes=[pf[1]])
                    S.op("act", lambda e, s=s, pf=pf: e.activation(yT[:, 4:8, s * 128:(s + 1) * 128], pf[0][:, 0:512].rearrange("p (c n) -> p c n", c=4), AF.Identity), reads=[pf[1]], writes=[b_yT])
                for s in range(4):
                    tok = T * 4 + s
                    banks = [PSB[5], PSB[6]]
                    for hf in range(2):
                        for c in range(8):
                            S.op("pe", lambda e, hf=hf, c=c, s=s: e.matmul(banks[hf][0][:, 0:512], yT[:, c, s * 128:(s + 1) * 128], wout_sb[:, c, hf * 512:(hf + 1) * 512],
                                                                           start=(c == 0), stop=(c == 7)), reads=[b_yT, b_wout], writes=[banks[hf][1]])
                    postnorm_residual(banks, src, dst, tok, hin[:, 0, :], b_hin[0], gpost, cb, junk, b_junk, small, b_small, ftmp, b_ftmp, True)

        seq = []
        cur = "x"
        for L in range(2):
            for ph in ("ffn1", "mix", "ffn2", "ple"):
                seq.append((L, ph))
        if stop_after is not None:
            seq = seq[:seq.index(stop_after) + 1]
        for i, (L, ph) in enumerate(seq):
            last = (i == len(seq) - 1)
            dst = "y" if last else ("hA" if cur != "hA" else "hB")
            if ph == "ffn1":
                ffn_phase(L, 0, cur, dst)
            elif ph == "ffn2":
                ffn_phase(L, 1, cur, dst)
            elif ph == "mix":
                mixer_phase(L, cur, dst)
            else:
                ple_phase(L, cur, dst)
            cur = dst
        S.finish()
        S.emit()
    return nc


_PROG = {}
LAST_RES = None


def prep_inputs(inputs, b):
    f = lambda a: np.ascontiguousarray(np.asarray(a, dtype=np.float32))
    c = host_constants()
    gs, bc, c31 = host_bias_tables(inputs["rel_bias"])
    m = {}
    m["x"] = f(inputs["x"][b])
    m["pT"] = f(np.transpose(np.asarray(inputs["p"])[:, b], (0, 2, 1)))
    m["ngB"] = f(np.broadcast_to(np.asarray(inputs["norm_g"])[:, :, None, :], (2, 8, 128, DM)))
    m["wg"] = f(inputs["ffn_w_gate"]); m["wu"] = f(inputs["ffn_w_up"]); m["wd"] = f(inputs["ffn_w_down"])
    m["w_in"] = f(inputs["w_in"]); m["w_out"] = f(inputs["w_out"])
    m["sguB"] = f(np.broadcast_to(np.asarray(inputs["sgu_norm_g"])[:, None, :], (2, 128, 256)))
    m["sgu_w"] = f(inputs["sgu_w"])
    m["sgu_bT"] = f(np.transpose(np.asarray(inputs["sgu_b"]), (0, 2, 1)))
    v2 = lambda a: f(np.transpose(np.asarray(a, np.float32).reshape(2, 2, 128), (0, 2, 1)))
    m["conv_wT"] = f(np.transpose(np.asarray(inputs["conv_w"], np.float32).reshape(2, 4, 2, 128), (0, 3, 2, 1)))
    m["conv_b"] = v2(inputs["conv_b"]); m["lru_wa"] = f(inputs["lru_wa"]); m["lru_ba"] = v2(inputs["lru_ba"])
    m["lru_wx"] = f(inputs["lru_wx"]); m["lru_bx"] = v2(inputs["lru_bx"]); m["lru_lam"] = v2(inputs["lru_lambda"])
    m["posT"] = f(np.transpose(np.asarray(inputs["cmp_pos"]), (0, 3, 1, 2)))
    m["cmp_w1"] = f(inputs["cmp_w1"])
    m["b1T"] = f(np.transpose(np.asarray(inputs["cmp_b1"]), (0, 2, 1)))
    m["cmp_w2"] = f(inputs["cmp_w2"])
    b2 = np.asarray(inputs["cmp_b2"], np.float32)
    m["b2kk"] = f(np.concatenate([b2[:, 0], b2[:, 0]], axis=1)[:, :, None])
    m["b2vB"] = f(np.broadcast_to(b2[:, 1][:, None, :], (2, 128, 64)))
    m["wpg"] = f(inputs["ple_w_gate"]); m["wpp"] = f(inputs["ple_w_proj"])
    for k in ("ident", "tril", "eall", "ovl", "m512", "tmax", "tmin"):
        m[k] = c[k]
    m["gs"] = gs; m["bc"] = bc; m["c31"] = c31
    return m


def kernel(**inputs):
    key = STOP_AFTER
    if key not in _PROG:
        _PROG[key] = build(STOP_AFTER)
    nc = _PROG[key]
    shared = prep_inputs(inputs, 0)
    in_maps = []
    for b in range(8):
        m = dict(shared)
        m["x"] = np.ascontiguousarray(np.asarray(inputs["x"][b], dtype=np.float32))
        m["pT"] = np.ascontiguousarray(np.transpose(np.asarray(inputs["p"])[:, b], (0, 2, 1)).astype(np.float32))
        in_maps.append(m)
    ncore = int(os.environ.get("KCORES", "8"))
    res = run_bass_kernel_spmd(nc, in_maps[:ncore], core_ids=list(range(ncore)))
    global LAST_RES
    LAST_RES = res.results
    outs = [np.asarray(r["y"], dtype=np.float32) for r in res.results]
    while len(outs) < 8:
        outs.append(np.zeros_like(outs[0]))
    return np.stack(outs, axis=0)
```

```python
import contextlib
import math
import os
import numpy as np
import concourse.bass as bass
import concourse.mybir as mybir
from concourse.bass_utils import run_bass_kernel_spmd

F32 = mybir.dt.float32
BF16 = mybir.dt.bfloat16
AF = mybir.ActivationFunctionType
ALU = mybir.AluOpType

S_LEN = 4096
DM = 1024
DFF = 2816
NIN = 2328
EPS = 1e-6
NEGM = -30000.0
GSW = 1792
SKIP = os.environ.get('MIXSKIP', '')
STOP_AFTER = None


class Buf:
    __slots__ = ("name", "w", "r")

    def __init__(self, name):
        self.name = name
        self.w = {}
        self.r = {}


class Sched:
    def __init__(self, nc, stack, n_dma_sems=48):
        self.nc = nc
        self.ekeys = ["pe", "dve", "act", "pool", "sp"]
        self.prog = {k: [] for k in self.ekeys}
        self.sems = {}
        for k in self.ekeys:
            self.sems[k] = stack.enter_context(nc.semaphore("s_" + k))
        self.cnt = {k: 0 for k in self.ekeys}
        self.dsem_keys = []
        for i in range(n_dma_sems):
            k = "d%d" % i
            self.sems[k] = stack.enter_context(nc.semaphore("s_" + k))
            self.cnt[k] = 0
            self.dsem_keys.append(k)
        self.drr = 0
        self.qrr = {}
        self.waited = {k: {} for k in self.ekeys}
        self.sb_off = 16512
        self.uid = 0
        self.skipping = False

    def sb(self, name, shape, dtype, align=64):
        nbytes = int(np.prod(shape[1:])) * mybir.dt.size(dtype)
        off = (self.sb_off + align - 1) // align * align
        self.uid += 1
        t = self.nc.alloc_sbuf_tensor_at("%s_%d" % (name, self.uid), list(shape), dtype, offset=off)
        self.sb_off = off + nbytes
        assert self.sb_off <= 229376, (name, self.sb_off)
        return t

    def mark(self):
        return self.sb_off

    def release(self, m):
        self.sb_off = m

    def _wait(self, ek, ev):
        semkey, val = ev
        if semkey == ek and ek == "pe":
            return
        if self.waited[ek].get(semkey, 0) >= val:
            return
        self.waited[ek][semkey] = val
        sem = self.sems[semkey]
        self.prog[ek].append(lambda e, sem=sem, val=val: e.wait_ge(sem, val))

    def _deps(self, ek, reads, writes):
        deps = {}
        for b in reads:
            for k, v in b.w.items():
                deps[k] = max(deps.get(k, 0), v)
        for b in writes:
            for k, v in b.w.items():
                deps[k] = max(deps.get(k, 0), v)
            for k, v in b.r.items():
                deps[k] = max(deps.get(k, 0), v)
        for k, v in deps.items():
            self._wait(ek, (k, v))

    def _record(self, ev, reads, writes):
        for b in reads:
            b.r[ev[0]] = max(b.r.get(ev[0], 0), ev[1])
        for b in writes:
            b.w[ev[0]] = max(b.w.get(ev[0], 0), ev[1])
            b.r = {}

    def op(self, ek, fn, reads=(), writes=()):
        if self.skipping:
            return
        self._deps(ek, reads, writes)
        self.cnt[ek] += 1
        sem = self.sems[ek]
        self.prog[ek].append(lambda e, fn=fn, sem=sem: fn(e).then_inc(sem, 1))
        self._record((ek, self.cnt[ek]), reads, writes)

    def dma(self, qk, out, in_, reads=(), writes=(), **kw):
        if self.skipping:
            return
        self._deps(qk, reads, writes)
        half = len(self.dsem_keys) // 2
        self.qrr[qk] = self.qrr.get(qk, 0) + 1
        base = half if qk == "pool" else 0
        sk = self.dsem_keys[base + self.qrr[qk] % half]
        if self.cnt[sk] > 0:
            self._wait(qk, (sk, self.cnt[sk]))
        self.cnt[sk] += 16
        sem = self.sems[sk]
        self.prog[qk].append(
            lambda e, out=out, in_=in_, sem=sem, kw=kw: e.dma_start(out=out, in_=in_, **kw).then_inc(sem, 16))
        self._record((sk, self.cnt[sk]), reads, writes)

    def barrier(self):
        for ek in self.ekeys:
            for k, v in self.cnt.items():
                if v > 0 and k != ek:
                    self._wait(ek, (k, v))

    def finish(self):
        for k, v in self.cnt.items():
            if v > 0 and k != "sp":
                self._wait("sp", (k, v))

    def emit(self):
        with self.nc.Block() as block:
            @block.tensor
            def _(e):
                for f in self.prog["pe"]:
                    f(e)

            @block.vector
            def _(e):
                for f in self.prog["dve"]:
                    f(e)

            @block.scalar
            def _(e):
                for f in self.prog["act"]:
                    f(e)

            @block.gpsimd
            def _(e):
                for f in self.prog["pool"]:
                    f(e)

            @block.sync
            def _(e):
                for f in self.prog["sp"]:
                    f(e)


def t5_bucket_np(d):
    n = np.maximum(d, 0)
    nf = np.maximum(n, 16).astype(np.float32)
    large = 16 + (np.log(nf / np.float32(16)) / np.float32(math.log(1024 / 16)) * np.float32(16)).astype(np.int32)
    large = np.minimum(large, 31)
    return np.where(n < 16, n, large)


def t5_bucket_jax_exact(d):
    import jax
    import jax.numpy as jnp
    with jax.default_device(jax.devices("cpu")[0]):
        n = jnp.maximum(jnp.asarray(d, jnp.int32), 0)
        nf = jnp.maximum(n, 16).astype(jnp.float32)
        large = 16 + (jnp.log(nf / 16) / math.log(1024 / 16) * 16).astype(jnp.int32)
        large = jnp.minimum(large, 31)
        return np.asarray(jnp.where(n < 16, n, large))


_CONST = {}


def host_constants():
    if _CONST:
        return _CONST
    c = _CONST
    c["ident"] = np.eye(128, dtype=np.float32)
    c["tril"] = np.tril(np.ones((128, 128), np.float32))
    e = np.zeros((128, 4096), np.float32)
    for k in range(4096):
        e[k // 64, k] = 1.0
        e[64 + k // 64, k] = 1.0
    c["eall"] = e
    ov = np.zeros((256, 64), np.float32)
    for s in range(1, 256):
        n = s - 1
        cs = 16 * n
        for j in range(64):
            o = min(cs + 32, 64 * j + 64) - max(cs, 64 * j)
            if o > 0:
                ov[s, j] = o / 32.0
    c["ovl"] = ov
    try:
        bk = t5_bucket_jax_exact(np.arange(0, 4200))
    except Exception:
        bk = t5_bucket_np(np.arange(0, 4200))
    c["bucket"] = bk
    tmax = np.zeros((128, 127), np.float32)
    tmin = np.full((128, 127), 1e5, np.float32)
    for p in range(128):
        hi = 1 if p >= 64 else 0
        for xx in range(127):
            rel = xx - 63 - hi
            if rel in (0, -1):
                tmax[p, xx] = 1e4
            if rel > 0:
                tmin[p, xx] = -1.0
    c["tmax"] = tmax
    c["tmin"] = tmin
    m = np.zeros((128, 896), np.float32)
    for p in range(128):
        xx = np.arange(896)
        m[p, (xx + 128 - p) >= 512] = NEGM
    c["m512"] = m
    return c


def host_bias_tables(rel_bias):
    c = host_constants()
    bk = c["bucket"]
    rb = np.asarray(rel_bias, np.float32)
    p = np.arange(128)[:, None]
    xx = np.arange(GSW)[None, :]
    d = xx - 384 - p
    ok = d >= 0
    g = rb[bk[np.maximum(d, 0)]]
    g = np.where(ok[:, :, None], g, np.float32(NEGM))
    gs = np.ascontiguousarray(np.transpose(g, (2, 0, 1))).astype(np.float32)
    s = np.arange(256)[:, None]
    t = np.arange(4096)[None, :]
    dc = t - (16 * (s - 1) + 31)
    okc = (dc >= 0) & (s >= 1)
    b = rb[bk[np.maximum(dc, 0)]]
    b = np.where(okc[:, :, None], b, np.float32(NEGM))
    bc = np.ascontiguousarray(np.transpose(b, (2, 0, 1))).astype(np.float32)
    c31 = np.ascontiguousarray(np.tile(rb[31][None, :], (128, 1))).astype(np.float32)
    return gs, bc, c31


def build(stop_after=None):
    nc = bass.Bass("TRN2", target_bir_lowering=False)
    D = {}

    def din(name, shape):
        D[name] = nc.dram_tensor(name, list(shape), F32, kind="ExternalInput").ap()
        return D[name]

    din("x", [S_LEN, DM]); din("pT", [2, 256, S_LEN]); din("ngB", [2, 8, 128, DM])
    din("wg", [2, 2, DM, DFF]); din("wu", [2, 2, DM, DFF]); din("wd", [2, 2, DFF, DM])
    din("w_in", [2, DM, NIN]); din("w_out", [2, DM, DM])
    din("sguB", [2, 128, 256]); din("sgu_w", [2, 4, 128, 128]); din("sgu_bT", [2, 128, 4])
    din("conv_wT", [2, 128, 2, 4]); din("conv_b", [2, 128, 2]); din("lru_wa", [2, 4, 64, 64]); din("lru_ba", [2, 128, 2])
    din("lru_wx", [2, 4, 64, 64]); din("lru_bx", [2, 128, 2]); din("lru_lam", [2, 128, 2])
    din("posT", [2, 64, 2, 32]); din("cmp_w1", [2, 2, 2048, 128]); din("b1T", [2, 128, 2])
    din("cmp_w2", [2, 2, 128, 64]); din("b2kk", [2, 128, 1]); din("b2vB", [2, 128, 64])
    din("wpg", [2, DM, DM]); din("wpp", [2, 256, DM])
    din("ident", [128, 128]); din("tril", [128, 128]); din("eall", [128, 4096]); din("ovl", [256, 64])
    din("gs", [8, 128, GSW]); din("m512", [128, 896]); din("bc", [8, 256, S_LEN])
    din("tmax", [128, 127]); din("tmin", [128, 127]); din("c31", [128, 8])
    y = nc.dram_tensor("y", [S_LEN, DM], F32, kind="ExternalOutput").ap()
    dbg = stop_after is not None and stop_after[1] == "mix"
    if dbg:
        ydA = nc.dram_tensor("ydA", [S_LEN, 256], F32, kind="ExternalOutput").ap()
        ydB = nc.dram_tensor("ydB", [256, S_LEN], F32, kind="ExternalOutput").ap()
        ydC = nc.dram_tensor("ydC", [S_LEN, 512], F32, kind="ExternalOutput").ap()
        ydR = nc.dram_tensor("ydR", [2, 8, 2, 128, 260], F32, kind="ExternalOutput").ap()
    hA = nc.dram_tensor("hA", [S_LEN, DM], F32).ap()
    hB = nc.dram_tensor("hB", [S_LEN, DM], F32).ap()

    with contextlib.ExitStack() as st:
        S = Sched(nc, st)
        PSB = []
        for i in range(7):
            PSB.append((st.enter_context(nc.psum_tensor("psb%d" % i, [128, 512], F32)), Buf("psb%d" % i)))
        pt, b_pt = st.enter_context(nc.psum_tensor("pst", [128, 1024], BF16)), Buf("pst")

        identb = S.sb("identb", [128, 128], BF16); b_const = Buf("const")
        S.dma("pool", identb[:], D["ident"], writes=[b_const])
        ones1 = S.sb("ones1", [128, 1], F32)
        S.op("dve", lambda e: e.memset(ones1[:], 1.0), writes=[b_const])
        epsb = S.sb("epsb", [128, 1], F32)
        S.op("dve", lambda e: e.memset(epsb[:], EPS), writes=[b_const])
        glob_mark = S.mark()

        def hbufs(name):
            return [Buf("%s%d" % (name, i)) for i in range(32)]

        HB = {"x": hbufs("x"), "hA": hbufs("hA"), "hB": hbufs("hB"), "y": hbufs("y")}
        HAP = {"x": D["x"], "hA": hA, "hB": hB, "y": y}

        def rstd_from_ss(ss_ap, out_ap, n, reads, writes, tmpbuf):
            S.op("act", lambda e: e.activation(out_ap, ss_ap, AF.Sqrt, bias=epsb[:, 0:1], scale=1.0 / n), reads=list(reads) + [b_const], writes=writes)
            S.op("dve", lambda e: e.reciprocal(out_ap, out_ap), reads=writes, writes=writes)

        def prenorm_xT(src, tok, hin_t, b_hin, gB, b_g, junk, b_junk, small, b_small, xn, b_xn, xT, b_xT, s):
            S.dma("sp", hin_t, HAP[src][tok * 128:(tok + 1) * 128, :], reads=[HB[src][tok]], writes=[b_hin])
            S.op("act", lambda e: e.activation(junk[:], hin_t, AF.Square, accum_out=small[:, 0:1]), reads=[b_hin], writes=[b_junk, b_small])
            rstd_from_ss(small[:, 0:1], small[:, 1:2], float(DM), [b_small], [b_small], None)
            S.op("dve", lambda e: e.scalar_tensor_tensor(out=xn[:], in0=hin_t, scalar=small[:, 1:2], in1=gB, op0=ALU.mult, op1=ALU.mult),
                 reads=[b_hin, b_small, b_g], writes=[b_xn])
            for c in range(8):
                S.op("pe", lambda e, c=c: e.transpose(pt[:, c * 128:(c + 1) * 128], xn[:, c * 128:(c + 1) * 128], identb[:]),
                     reads=[b_xn, b_const], writes=[b_pt])
            S.op("act", lambda e: e.activation(xT[:, :, s * 128:(s + 1) * 128], pt[:].rearrange("p (c n) -> p c n", c=8), AF.Identity),
                 reads=[b_pt], writes=[b_xT])

        def postnorm_residual(banks, src, dst, tok, hin_t, b_hin, gpostB, b_g, junk, b_junk, small, b_small, ftmp, b_ftmp, reload):
            if reload:
                S.dma("sp", hin_t, HAP[src][tok * 128:(tok + 1) * 128, :], reads=[HB[src][tok]], writes=[b_hin])
            for hf in range(2):
                S.op("act", lambda e, hf=hf: e.activation(junk[:, 0:512], banks[hf][0][:, 0:512], AF.Square, accum_out=small[:, 2 + hf:3 + hf]),
                     reads=[banks[hf][1]], writes=[b_junk, b_small])
            S.op("dve", lambda e: e.tensor_tensor(small[:, 4:5], small[:, 2:3], small[:, 3:4], ALU.add), reads=[b_small], writes=[b_small])
            rstd_from_ss(small[:, 4:5], small[:, 5:6], float(DM), [b_small], [b_small], None)
            for hf in range(2):
                S.op("dve", lambda e, hf=hf: e.scalar_tensor_tensor(out=ftmp[:, 0:512], in0=banks[hf][0][:, 0:512], scalar=small[:, 5:6],
                                                                     in1=gpostB[:, hf * 512:(hf + 1) * 512], op0=ALU.mult, op1=ALU.mult),
                     reads=[banks[hf][1], b_small, b_g], writes=[b_ftmp])
                S.op("pool", lambda e, hf=hf: e.tensor_tensor(hin_t[:, hf * 512:(hf + 1) * 512], hin_t[:, hf * 512:(hf + 1) * 512], ftmp[:, 0:512], ALU.add),
                     reads=[b_hin, b_ftmp], writes=[b_hin])
            S.dma("sp", HAP[dst][tok * 128:(tok + 1) * 128, :], hin_t, reads=[b_hin], writes=[HB[dst][tok]])

        def ffn_phase(L, which, src, dst):
            S.barrier()
            S.release(glob_mark)
            wg_sb = S.sb("wg", [128, 8, DFF], BF16); wu_sb = S.sb("wu", [128, 8, DFF], BF16)
            wd_sb = S.sb("wd", [128, 22, DM], BF16)
            b_wg, b_wu, b_wd = Buf("wg"), Buf("wu"), Buf("wd")
            gpre = S.sb("gpre", [128, DM], F32); gpost = S.sb("gpost", [128, DM], F32); b_g = Buf("g")
            hin = S.sb("hin", [128, 4, DM], F32); b_hin = [Buf("hin%d" % i) for i in range(4)]
            xT = S.sb("xT", [128, 8, 512], BF16); b_xT = Buf("xT")
            hT = S.sb("hT", [128, 22, 512], BF16); b_hT = Buf("hT")
            xn = S.sb("xn", [128, DM], BF16); b_xn = Buf("xn")
            junk = S.sb("junk", [128, DM], BF16); b_junk = Buf("junk")
            small = S.sb("small", [128, 8], F32); b_small = Buf("small")
            ftmp = S.sb("ftmp", [128, DM], F32); b_ftmp = Buf("ftmp")
            sgt = [S.sb("sgt%d" % i, [128, 512], BF16) for i in range(2)]; b_sgt = [Buf("sgt0"), Buf("sgt1")]
            ni = 4 * which
            S.dma("sp", gpre[:], D["ngB"][L, ni], writes=[b_g])
            S.dma("sp", gpost[:], D["ngB"][L, ni + 1], writes=[b_g])
            S.op("dve", lambda e: e.tensor_scalar(gpost[:], gpost[:], 0.5, None, ALU.mult), reads=[b_g], writes=[b_g])
            NCB = 4
            CW = DFF // NCB
            cbs = [(0, 768), (768, 1536), (1536, 2304), (2304, 2816)]
            b_wgc = [Buf("wgc%d" % i) for i in range(len(cbs))]
            b_wuc = [Buf("wuc%d" % i) for i in range(len(cbs))]
            b_wdc = [Buf("wdc%d" % i) for i in range(22)]
            for bi_, (c0, c1) in enumerate(cbs):
                for c in range(8):
                    S.dma("pool", wg_sb[:, c, c0:c1], D["wg"][L, which, c * 128:(c + 1) * 128, c0:c1], writes=[b_wgc[bi_]])
                    S.dma("pool", wu_sb[:, c, c0:c1], D["wu"][L, which, c * 128:(c + 1) * 128, c0:c1], writes=[b_wuc[bi_]])
            for c in range(22):
                S.dma("pool", wd_sb[:, c, :], D["wd"][L, which, c * 128:(c + 1) * 128, :], writes=[b_wdc[c]], max_dma_last_dim=4096)

            def cb_of(fc):
                for bi_, (c0, c1) in enumerate(cbs):
                    if c0 <= fc * 128 < c1:
                        return bi_
            pair = 0
            dbank = 0
            for T in range(8):
                for s in range(4):
                    prenorm_xT(src, T * 4 + s, hin[:, s, :], b_hin[s], gpre[:], b_g, junk, b_junk, small, b_small, xn, b_xn, xT, b_xT, s)
                for fc in range(22):
                    pg, bg = PSB[(pair % 2) * 2]
                    pu, bu = PSB[(pair % 2) * 2 + 1]
                    pair += 1
                    for (wsb, bw, ps_, bps) in ((wg_sb, b_wgc[cb_of(fc)], pg, bg), (wu_sb, b_wuc[cb_of(fc)], pu, bu)):
                        for kc in range(8):
                            S.op("pe", lambda e, wsb=wsb, ps_=ps_, kc=kc, fc=fc: e.matmul(ps_[:, 0:512], wsb[:, kc, fc * 128:(fc + 1) * 128], xT[:, kc, :],
                                                                                          start=(kc == 0), stop=(kc == 7)),
                                 reads=[bw, b_xT], writes=[bps])
                    sg_, bsg = sgt[fc % 2], b_sgt[fc % 2]
                    S.op("act", lambda e, sg_=sg_, pg=pg: e.activation(sg_[:], pg[:, 0:512], AF.Silu), reads=[bg], writes=[bsg])
                    S.op("dve", lambda e, sg_=sg_, pu=pu, fc=fc: e.tensor_tensor(hT[:, fc, :], sg_[:], pu[:, 0:512], ALU.mult),
                         reads=[bsg, bu], writes=[b_hT])
                for s in range(4):
                    banks = []
                    for hf in range(2):
                        pb = PSB[4 + dbank % 3]; dbank += 1
                        banks.append(pb)
                        for fc in range(22):
                            S.op("pe", lambda e, pb=pb, fc=fc, s=s, hf=hf: e.matmul(pb[0][:, 0:512], hT[:, fc, s * 128:(s + 1) * 128],
                                                                                    wd_sb[:, fc, hf * 512:(hf + 1) * 512], start=(fc == 0), stop=(fc == 21)),
                                 reads=[b_hT, b_wdc[fc]], writes=[pb[1]])
                    postnorm_residual(banks, src, dst, T * 4 + s, hin[:, s, :], b_hin[s], gpost, b_g, junk, b_junk, small, b_small, ftmp, b_ftmp, False)

        def ple_phase(L, src, dst):
            S.barrier()
            S.release(glob_mark)
            wpg_sb = S.sb("wpg", [128, 8, DM], BF16); wpp_sb = S.sb("wpp", [128, 2, DM], BF16)
            pT_sb = S.sb("pTs", [128, 2, S_LEN], BF16)
            b_w = Buf("plew")
            gpre = S.sb("gpre", [128, DM], F32); gpost = S.sb("gpost", [128, DM], F32); b_g = Buf("g")
            hin = S.sb("hin", [128, 2, DM], F32); b_hin = [Buf("hin0"), Buf("hin1")]
            def two(name, shape, dt):
                return [S.sb(name + str(i), shape, dt) for i in range(2)], [Buf(name + str(i)) for i in range(2)]
            xT2, b_xT2 = two("xT", [128, 8, 128], BF16)
            xn2, b_xn2 = two("xn", [128, DM], BF16)
            junk2, b_junk2 = two("junk", [128, DM], BF16)
            small2, b_small2 = two("small", [128, 8], F32)
            ftmp2, b_ftmp2 = two("ftmp", [128, DM], F32)
            sgf2, b_sgf2 = two("sgf", [128, DM], F32)
            u2, b_u2 = two("u", [128, DM], F32)
            S.dma("sp", gpre[:], D["ngB"][L, 6], writes=[b_g])
            S.dma("sp", gpost[:], D["ngB"][L, 7], writes=[b_g])
            for c in range(8):
                S.dma("pool", wpg_sb[:, c, :], D["wpg"][L, c * 128:(c + 1) * 128, :], writes=[b_w])
            for c in range(2):
                S.dma("pool", wpp_sb[:, c, :], D["wpp"][L, c * 128:(c + 1) * 128, :], writes=[b_w])
                for q in range(4):
                    S.dma("pool", pT_sb[:, c, q * 1024:(q + 1) * 1024], D["pT"][L, c * 128:(c + 1) * 128, q * 1024:(q + 1) * 1024], writes=[b_w])
            rbl = [0]

            def _tile(tok, xT, b_xT, xn, b_xn, junk, b_junk, small, b_small, ftmp, b_ftmp, sgf, b_sgf, u, b_u, hi_, bh):
                prenorm_xT(src, tok, hi_, bh, gpre[:], b_g, junk, b_junk, small, b_small, xn, b_xn, xT, b_xT, 0)
                gb = []
                pb_ = []
                for hf in range(2):
                    g_ = PSB[rbl[0] % 7]; rbl[0] += 1
                    p_ = PSB[rbl[0] % 7]; rbl[0] += 1
                    for kc in range(8):
                        S.op("pe", lambda e, g_=g_, kc=kc, hf=hf: e.matmul(g_[0][:, 0:512], xT[:, kc, :], wpg_sb[:, kc, hf * 512:(hf + 1) * 512],
                                                                           start=(kc == 0), stop=(kc == 7)), reads=[b_xT, b_w], writes=[g_[1]])
                    for c in range(2):
                        S.op("pe", lambda e, p_=p_, c=c, hf=hf, tok=tok: e.matmul(p_[0][:, 0:512], pT_sb[:, c, tok * 128:(tok + 1) * 128],
                                                                                  wpp_sb[:, c, hf * 512:(hf + 1) * 512], start=(c == 0), stop=(c == 1)),
                             reads=[b_w], writes=[p_[1]])
                    S.op("act", lambda e, g_=g_, hf=hf: e.activation(sgf[:, hf * 512:(hf + 1) * 512], g_[0][:, 0:512], AF.Sigmoid), reads=[g_[1]], writes=[b_sgf])
                    S.op("dve", lambda e, p_=p_, hf=hf: e.tensor_tensor(u[:, hf * 512:(hf + 1) * 512], sgf[:, hf * 512:(hf + 1) * 512], p_[0][:, 0:512], ALU.mult),
                         reads=[b_sgf, p_[1]], writes=[b_u])
                S.op("act", lambda e: e.activation(junk[:], u[:], AF.Square, accum_out=small[:, 4:5]), reads=[b_u], writes=[b_junk, b_small])
                rstd_from_ss(small[:, 4:5], small[:, 5:6], float(DM), [b_small], [b_small], None)
                S.op("dve", lambda e: e.scalar_tensor_tensor(out=ftmp[:], in0=u[:], scalar=small[:, 5:6], in1=gpost[:], op0=ALU.mult, op1=ALU.mult),
                     reads=[b_u, b_small, b_g], writes=[b_ftmp])
                S.op("pool", lambda e, hi_=hi_: e.tensor_tensor(hi_, hi_, ftmp[:], ALU.add), reads=[bh, b_ftmp], writes=[bh])
                S.dma("sp", HAP[dst][tok * 128:(tok + 1) * 128, :], hi_, reads=[bh], writes=[HB[dst][tok]])

            for tok in range(32):
                k_ = 0
                _tile(tok, xT2[k_], b_xT2[k_], xn2[k_], b_xn2[k_], junk2[k_], b_junk2[k_], small2[k_], b_small2[k_], ftmp2[k_], b_ftmp2[k_],
                      sgf2[k_], b_sgf2[k_], u2[k_], b_u2[k_], hin[:, tok % 2, :], b_hin[tok % 2])

        def mixer_phase(L, src, dst):
            S.barrier()
            S.release(glob_mark)
            cb = Buf("mixconst")
            win_sb = S.sb("win", [128, 8, NIN], BF16)
            wout_sb = S.sb("wout", [128, 8, DM], BF16)
            w1_sb = S.sb("w1", [128, 64, 128], BF16)
            w2k_pad = S.sb("w2kp", [128, 2, 128], BF16)
            w2v_sb = S.sb("w2v", [128, 64], BF16)
            posT_sb = S.sb("posT", [128, 2, 32], BF16)
            b1c = S.sb("b1c", [128, 2], F32)
            b2k = S.sb("b2k", [128, 1], F32)
            b2vB = S.sb("b2vB", [128, 64], F32)
            ksE = [S.sb("ksE%d" % i, [128, S_LEN], BF16) for i in range(2)]; b_ksT = Buf("ksT")
            kwT = S.sb("kwT", [128, S_LEN], BF16); b_kwT = Buf("kwT")
            vs_aug = S.sb("vsa", [128, 32, 2, 65], BF16); b_vs = Buf("vsa")
            vw_aug = S.sb("vwa", [128, 32, 2, 65], BF16); b_vw = Buf("vwa")
            kcmpT = S.sb("kcmpT", [128, 256], BF16); b_kcmp = Buf("kcmpT")
            cv_aug = S.sb("cva", [128, 2, 2, 129], BF16); b_cv = Buf("cva")
            gs_cur = [S.sb("gsc%d" % i, [128, GSW], BF16) for i in range(2)]; b_gs = [Buf("gs0"), Buf("gs1")]
            identf = S.sb("identf", [128, 128], F32)
            m512_sb = S.sb("m512", [128, 896], F32)
            c31_sb = S.sb("c31", [128, 8], F32)
            tmax_sb = S.sb("tmax", [128, 127], F32); tmin_sb = S.sb("tmin", [128, 127], F32)
            cw_sb = S.sb("cw", [128, 2, 4], F32); cbias = S.sb("cbias", [128, 2], F32)
            bda = S.sb("bda", [128, 2, 128], BF16); bdx = S.sb("bdx", [128, 2, 128], BF16)
            ba_sb = S.sb("ba", [128, 2], F32); bx_sb = S.sb("bx", [128, 2], F32); lamc = S.sb("lamc", [128, 2], F32)
            wsT = S.sb("wsT", [128, 4, 128], BF16)
            sguB = S.sb("sguB", [128, 256], F32); bsT = S.sb("bsT", [128, 4], F32)
            gpre = S.sb("gpre", [128, DM], F32); gpost = S.sb("gpost", [128, DM], F32)
            hin = S.sb("hin", [128, 2, DM], F32)[:, 0:1, :] if False else S.sb("hin", [128, 1, DM], F32); b_hin = [Buf("hin0"), Buf("hin0b")]; b_hin[1] = b_hin[0]
            xT = S.sb("xT", [128, 8, 512], BF16); b_xT = Buf("xT")
            xn = S.sb("xn", [128, DM], BF16); b_xn = Buf("xn")
            junk = xn; b_junk = b_xn
            small = S.sb("small", [128, 8], F32); b_small = Buf("small")
            ftmp = S.sb("ftmp", [128, 512], F32); b_ftmp = Buf("ftmp")
            qz = S.sb("qz", [128, 8, 512], BF16); b_qT = Buf("qz")
            rz = S.sb("rz", [128, 4, 528], BF16); b_roll = Buf("roll")
            xbT = S.sb("xbT", [128, 2, 516], F32); b_xb = Buf("xbT")
            gateT = S.sb("gateT", [128, 2, 512], BF16); b_gate = Buf("gateT")
            carry = S.sb("carry", [128, 2], F32); b_carry = Buf("carry")
            sg = S.sb("sg", [128, 4, 24], F32); b_sg = Buf("sg")
            uv = S.sb("uv", [128, 512], F32); b_uv = Buf("uv")
            avn = S.sb("avn", [128, 256], BF16); b_avn = Buf("avn")
            ytok = S.sb("ytok", [128, 4, 256], BF16); b_ytok = [Buf("ytok%d" % i) for i in range(4)]
            yT = S.sb("yT", [128, 8, 512], BF16); b_yT = Buf("yT")
            ycomb = S.sb("ycomb", [128, 4, 512], F32); b_ycs = [Buf("ycomb%d" % i) for i in range(4)]
            impS = S.sb("imp", [128, 4, 64], F32); b_imp = Buf("imp")
            lg = [S.sb("lg%d" % i, [128, 512], F32) for i in range(2)]; b_lg = [Buf("lg0"), Buf("lg1")]
            PT = [S.sb("PT%d" % i, [128, 512], BF16) for i in range(3)]; b_PT = [Buf("PT%d" % i) for i in range(3)]
            bct = [S.sb("bct0", [128, 512], F32)] * 2; b_bct = [Buf("bct0")] * 2
            nmz = [S.sb("nmz%d" % i, [128, 512], BF16) for i in range(2)]; b_nmT = Buf("nmT")
            nmp = [S.sb("nmp%d" % i, [128, 128], BF16) for i in range(2)]
            fw = [ycomb[:, i, :] for i in range(4)]; b_fw = b_ycs
            xcb = S.sb("xcb", [128, 512], BF16); b_xcb = Buf("xcb")
            hid = S.sb("hid", [128, 4, 64], BF16); b_hid = Buf("hid")
            hidv = S.sb("hidv", [128, 2, 128], BF16); b_hidv = Buf("hidv")
            onesp = S.sb("onesp", [1, 4, 128], BF16); b2v_row = S.sb("b2vr", [1, 64], BF16)
            sc = S.sb("sc", [128, 64], F32); sc2 = S.sb("sc2", [128, 64], F32); m8 = S.sb("m8", [128, 16], F32)
            nm = S.sb("nm", [128, 64], BF16); b_sc = Buf("sc")
            z4 = S.sb("z4", [128, 8], F32); b_z4 = Buf("z4")

            b_win, b_wout, b_cmpw = Buf("win"), Buf("wout"), Buf("cmpw")

            def cdma(q, out, in_, buf=None, **kw):
                S.dma(q, out, in_, writes=[buf if buf is not None else cb], **kw)
            W = D["w_in"][L]
            for c in range(8):
                rows = slice(c * 128, (c + 1) * 128)
                cdma("pool", win_sb[:, c, 0:1024], W[rows, 0:1024], buf=b_win)
                for r in range(4):
                    cdma("pool", win_sb[:, c, 1024 + r * 128:1024 + r * 128 + 64], W[rows, 1024 + r * 64:1024 + r * 64 + 64], buf=b_win)
                    cdma("pool", win_sb[:, c, 1024 + r * 128 + 64:1024 + r * 128 + 128], W[rows, 1024 + (4 + r) * 64:1024 + (4 + r) * 64 + 64], buf=b_win)
                cdma("pool", win_sb[:, c, 1536:NIN], W[rows, 1536:NIN], buf=b_win)
                cdma("pool", wout_sb[:, c, :], D["w_out"][L, rows, :], buf=b_wout)
            for kv in range(2):
                src_w1 = D["cmp_w1"][L, kv].rearrange("(l d) j -> d l j", d=64)
                for half in range(2):
                    for lq in range(4):
                        cdma("pool", w1_sb[half * 64:(half + 1) * 64, kv * 32 + lq * 8:kv * 32 + lq * 8 + 8, :], src_w1[:, lq * 8:(lq + 1) * 8, :], buf=b_cmpw)
            S.op("dve", lambda e: e.memset(w2k_pad[:], 0.0), writes=[b_cmpw])
            for g in range(2):
                cdma("pool", w2k_pad[:, g, g * 64:(g + 1) * 64], D["cmp_w2"][L, 0], buf=b_cmpw)
            cdma("pool", w2v_sb[:], D["cmp_w2"][L, 1], buf=b_cmpw)
            for half in range(2):
                cdma("pool", posT_sb[half * 64:(half + 1) * 64, :, :], D["posT"][L], buf=b_cmpw)
            cdma("sp", b1c[:], D["b1T"][L]); cdma("sp", b2k[:], D["b2kk"][L]); cdma("sp", b2vB[:], D["b2vB"][L], buf=b_cmpw)
            cdma("sp", identf[:], D["ident"])
            cdma("sp", m512_sb[:], D["m512"])
            for q in range(4):
                S.dma("pool", ksE[0][64:128, q * 1024:(q + 1) * 1024], D["eall"][0:64, q * 1024:(q + 1) * 1024], writes=[b_ksT])
                S.dma("pool", ksE[1][0:64, q * 1024:(q + 1) * 1024], D["eall"][0:64, q * 1024:(q + 1) * 1024], writes=[b_ksT])
            cdma("sp", c31_sb[:], D["c31"]); cdma("sp", tmax_sb[:], D["tmax"]); cdma("sp", tmin_sb[:], D["tmin"])
            cdma("sp", cw_sb[:], D["conv_wT"][L])
            cdma("sp", cbias[:], D["conv_b"][L])
            cdma("sp", ba_sb[:], D["lru_ba"][L])
            cdma("sp", bx_sb[:], D["lru_bx"][L])
            cdma("sp", lamc[:], D["lru_lam"][L])
            S.op("dve", lambda e: e.memset(bda[:], 0.0), writes=[cb])
            S.op("dve", lambda e: e.memset(bdx[:], 0.0), writes=[cb])
            for gi in range(4):
                i, a = gi // 2, gi % 2
                cdma("pool", bda[a * 64:(a + 1) * 64, i, a * 64:(a + 1) * 64], D["lru_wa"][L, gi])
                cdma("pool", bdx[a * 64:(a + 1) * 64, i, a * 64:(a + 1) * 64], D["lru_wx"][L, gi])
            cdma("sp", sguB[:], D["sguB"][L]); cdma("sp", bsT[:], D["sgu_bT"][L])
            cdma("sp", gpre[:], D["ngB"][L, 2]); cdma("sp", gpost[:], D["ngB"][L, 3])
            S.op("act", lambda e: e.activation(lamc[:], lamc[:], AF.Exp, scale=-1.0), reads=[cb], writes=[cb])
            S.op("act", lambda e: e.activation(lamc[:], lamc[:], AF.Ln, bias=ones1[:, 0:1]), reads=[cb, b_const], writes=[cb])
            S.op("dve", lambda e: e.tensor_scalar(lamc[:], lamc[:], -8.0, None, ALU.mult), reads=[cb], writes=[cb])
            trl = fw[0]
            S.dma("sp", trl[:, 0:128], D["tril"], writes=[b_fw[0]])
            for g in range(4):
                S.dma("sp", fw[1][:, 0:128], D["sgu_w"][L, g], writes=[b_fw[1]])
                S.op("dve", lambda e: e.tensor_tensor(xn[:, 0:128], fw[1][:, 0:128], trl[:, 0:128], ALU.mult), reads=[b_fw[0], b_fw[1]], writes=[b_xn])
                S.op("pe", lambda e: e.transpose(pt[:, 0:128], xn[:, 0:128], identb[:]), reads=[b_xn, b_const], writes=[b_pt])
                S.op("act", lambda e, g=g: e.activation(wsT[:, g, :], pt[:, 0:128], AF.Identity), reads=[b_pt], writes=[cb])
            pbk = PSB[0]
            for kv in range(2):
                for l in range(32):
                    S.op("pe", lambda e, kv=kv, l=l: e.matmul(pbk[0][:, kv:kv + 1], w1_sb[0:64, kv * 32 + l, :], posT_sb[0:64, kv, l:l + 1],
                                                              start=(kv == 0 and l == 0), stop=(kv == 1 and l == 31), skip_group_check=True),
                         reads=[b_cmpw], writes=[pbk[1]])
            S.op("dve", lambda e: e.tensor_tensor(b1c[:], b1c[:], pbk[0][:, 0:2], ALU.add), reads=[b_cmpw, pbk[1]], writes=[b_cmpw])
            S.op("dve", lambda e: e.memset(vs_aug[:], 1.0), writes=[b_vs])
            S.op("dve", lambda e: e.memset(vw_aug[:], 1.0), writes=[b_vw])
            S.op("dve", lambda e: e.memset(kcmpT[:], 0.0), writes=[b_kcmp])
            S.op("dve", lambda e: e.memset(cv_aug[:], 0.0), writes=[b_cv])
            S.op("dve", lambda e: e.memset(cv_aug[:, :, :, 128:129], 1.0), writes=[b_cv])
            for stt in range(2):
                for g in range(2):
                    S.dma("pool", cv_aug[:, stt, g, 0:64], D["ovl"][stt * 128:(stt + 1) * 128, :], writes=[b_cv])
            S.op("dve", lambda e: e.memset(rz[:], 0.0), writes=[b_roll])
            S.op("dve", lambda e: e.memset(qz[:], 0.0), writes=[b_qT])
            for i_ in range(2):
                S.op("dve", lambda e, i_=i_: e.memset(nmp[i_][:], 0.0), writes=[b_sc])
            S.op("dve", lambda e: e.memset(xbT[:], 0.0), writes=[b_xb])
            S.op("dve", lambda e: e.memset(carry[:], 0.0), writes=[b_carry])
            S.op("dve", lambda e: e.memset(hid[:], 0.0), writes=[b_hid])
            S.op("dve", lambda e: e.memset(onesp[:], 0.0), writes=[cb])
            for a4_ in range(4):
                S.op("dve", lambda e, a4_=a4_: e.memset(onesp[0:1, a4_, 32 * a4_:32 * a4_ + 32], 1.0), reads=[cb], writes=[cb])
            S.dma("pool", b2v_row[:], D["b2vB"][L, 0:1, :], writes=[cb])

            rot = [0]

            def nextbank():
                b = PSB[rot[0] % 3]
                rot[0] += 1
                return b

            ptc = [0]
            lgc = [0]
            gsr = [0]

            for T in range(8):
                t0 = T * 512
                for s in range(4):
                    tok = T * 4 + s
                    prenorm_xT(src, tok, hin[:, 0, :], b_hin[0], gpre[:], cb, junk, b_junk, small, b_small, xn, b_xn, xT, b_xT, s)

                def fm_proj(c0, ncol, evac):
                    pb = nextbank()
                    for kc in range(8):
                        S.op("pe", lambda e, pb=pb, kc=kc: e.matmul(pb[0][0:ncol, 0:512], win_sb[:, kc, c0:c0 + ncol], xT[:, kc, :], start=(kc == 0), stop=(kc == 7)),
                             reads=[b_win, b_xT], writes=[pb[1]])
                    evac(pb)
                for i in range(2):
                    fm_proj(512 + i * 128, 128, lambda pb, i=i: S.op("act", lambda e: e.activation(xbT[:, i, 3:515], pb[0][:, 0:512], AF.Identity), reads=[pb[1]], writes=[b_xb]))
                    fm_proj(768 + i * 128, 128, lambda pb, i=i: S.op("act", lambda e: e.activation(gateT[:, i, :], pb[0][:, 0:512], AF.Gelu_apprx_tanh), reads=[pb[1]], writes=[b_gate]))
                S.op("pool", lambda e: e.memset(qz[64:128, 0:4, :], 0.0), reads=[b_qT], writes=[b_qT])
                S.op("pool", lambda e: e.memset(qz[0:64, 4:8, :], 0.0), reads=[b_qT], writes=[b_qT])
                for r in range(4):
                    def _qev(pb, r=r):
                        S.op("act", lambda e: e.activation(qz[0:64, r, :], pb[0][0:64, 0:512], AF.Identity), reads=[pb[1]], writes=[b_qT])
                        S.op("dve", lambda e: e.tensor_copy(qz[64:128, 4 + r, :], pb[0][64:128, 0:512]), reads=[pb[1]], writes=[b_qT])
                    fm_proj(1024 + r * 128, 128, _qev)
                for kv_ in range(2):
                    def _rev(pb, kv_=kv_):
                        S.op("act", lambda e: e.activation(rz[0:64, kv_ * 2, 16:528], pb[0][0:64, 0:512], AF.Identity), reads=[pb[1]], writes=[b_roll])
                        S.op("dve", lambda e: e.tensor_copy(rz[64:128, kv_ * 2 + 1, 16:528], pb[0][64:128, 0:512]), reads=[pb[1]], writes=[b_roll])
                    fm_proj(1536 + 128 * kv_, 128, _rev)
                def _kev(pb, t0=t0):
                    S.op("dve", lambda e: e.tensor_copy(ksE[0][0:64, t0:t0 + 512], pb[0][0:64, 0:512]), reads=[pb[1]], writes=[b_ksT])
                    S.op("act", lambda e: e.activation(ksE[1][64:128, t0:t0 + 512], pb[0][64:128, 0:512], AF.Identity), reads=[pb[1]], writes=[b_ksT])
                fm_proj(1792, 128, _kev)
                fm_proj(2048, 128, lambda pb, t0=t0: S.op("act", lambda e: e.activation(kwT[:, t0:t0 + 512], pb[0][:, 0:512], AF.Identity), reads=[pb[1]], writes=[b_kwT]))

                S.skipping = 'a' in SKIP
                for s in range(4):
                    tok = T * 4 + s
                    ts_ = slice(s * 128, (s + 1) * 128)
                    pa = nextbank()
                    for kc in range(8):
                        S.op("pe", lambda e, pa=pa, kc=kc, ts_=ts_: e.matmul(pa[0][:, 0:512], xT[:, kc, ts_], win_sb[:, kc, 0:512], start=(kc == 0), stop=(kc == 7)),
                             reads=[b_win, b_xT], writes=[pa[1]])
                    S.op("act", lambda e, pa=pa: e.activation(uv[:], pa[0][:, 0:512], AF.Gelu_apprx_tanh), reads=[pa[1]], writes=[b_uv])
                    S.op("act", lambda e: e.activation(junk[:, 0:256], uv[:, 256:512], AF.Square, accum_out=small[:, 6:7]), reads=[b_uv], writes=[b_junk, b_small])
                    rstd_from_ss(small[:, 6:7], small[:, 7:8], 256.0, [b_small], [b_small], None)
                    S.op("dve", lambda e: e.scalar_tensor_tensor(out=avn[:], in0=uv[:, 256:512], scalar=small[:, 7:8], in1=sguB[:], op0=ALU.mult, op1=ALU.mult),
                         reads=[b_uv, b_small, cb], writes=[b_avn])
                    pm = nextbank()
                    for g in range(4):
                        S.op("pe", lambda e, pm=pm, g=g: e.matmul(pm[0][:, g * 64:(g + 1) * 64], wsT[:, g, :], avn[:, g * 64:(g + 1) * 64], start=True, stop=True,
                                                                  skip_group_check=True),
                             reads=[cb, b_avn], writes=[pm[1]])
                    for g in range(4):
                        S.op("dve", lambda e, pm=pm, g=g, s=s: e.scalar_tensor_tensor(out=ytok[:, s, g * 64:(g + 1) * 64], in0=pm[0][:, g * 64:(g + 1) * 64],
                                                                                      scalar=bsT[:, g:g + 1], in1=uv[:, g * 64:(g + 1) * 64], op0=ALU.add, op1=ALU.mult),
                             reads=[pm[1], cb, b_uv], writes=[b_ytok[s]])
                    pv = nextbank()
                    for kc in range(8):
                        S.op("pe", lambda e, pv=pv, kc=kc, ts_=ts_: e.matmul(pv[0][:, 0:128], xT[:, kc, ts_], win_sb[:, kc, 1920:2048], start=(kc == 0), stop=(kc == 7),
                                                                            skip_group_check=True),
                             reads=[b_win, b_xT], writes=[pv[1]])
                    for kc in range(8):
                        S.op("pe", lambda e, pv=pv, kc=kc, ts_=ts_: e.matmul(pv[0][:, 128:280], xT[:, kc, ts_], win_sb[:, kc, 2176:2328], start=False, stop=(kc == 7),
                                                                            skip_group_check=True),
                             reads=[b_win, b_xT], writes=[pv[1]])
                    for g_ in range(2):
                        S.op("act", lambda e, pv=pv, tok=tok, g_=g_: e.activation(vs_aug[:, tok, g_, 0:64], pv[0][:, g_ * 64:(g_ + 1) * 64], AF.Identity),
                             reads=[pv[1]], writes=[b_vs])
                        S.op("act", lambda e, pv=pv, tok=tok, g_=g_: e.activation(vw_aug[:, tok, g_, 0:64], pv[0][:, 128 + g_ * 64:128 + (g_ + 1) * 64], AF.Identity),
                             reads=[pv[1]], writes=[b_vw])
                    S.op("act", lambda e, pv=pv, s=s: e.activation(sg[:, s, :], pv[0][:, 256:280], AF.Sigmoid), reads=[pv[1]], writes=[b_sg])

                S.skipping = 'b' in SKIP
                for i in range(2):
                    xc, ig, aa, bb = fw[0], fw[1], fw[2], fw[3]
                    S.op("dve", lambda e, i=i: e.tensor_scalar(xc[:], xbT[:, i, 0:512], cw_sb[:, i, 0:1], cbias[:, i:i + 1], ALU.mult, ALU.add),
                         reads=[b_xb, cb], writes=[b_fw[0]])
                    for k in range(1, 4):
                        S.op("dve", lambda e, i=i, k=k: e.scalar_tensor_tensor(out=xc[:], in0=xbT[:, i, k:k + 512], scalar=cw_sb[:, i, k:k + 1], in1=xc[:],
                                                                               op0=ALU.mult, op1=ALU.add), reads=[b_xb, cb, b_fw[0]], writes=[b_fw[0]])
                    S.op("dve", lambda e, i=i: e.tensor_copy(xbT[:, i, 0:3], xbT[:, i, 512:515]), reads=[b_xb], writes=[b_xb])
                    S.op("act", lambda e: e.activation(xcb[:], xc[:], AF.Identity), reads=[b_fw[0]], writes=[b_xcb])
                    pr = nextbank()
                    S.op("pe", lambda e, pr=pr, i=i: e.matmul(pr[0][:, 0:512], bda[:, i, :], xcb[:], start=True, stop=True), reads=[cb, b_xcb], writes=[pr[1]])
                    pi = nextbank()
                    S.op("pe", lambda e, pi=pi, i=i: e.matmul(pi[0][:, 0:512], bdx[:, i, :], xcb[:], start=True, stop=True), reads=[cb, b_xcb], writes=[pi[1]])
                    S.op("act", lambda e, pr=pr, i=i: e.activation(aa[:], pr[0][:, 0:512], AF.Sigmoid, bias=ba_sb[:, i:i + 1]), reads=[pr[1], cb], writes=[b_fw[2]])
                    S.op("act", lambda e, pi=pi, i=i: e.activation(ig[:], pi[0][:, 0:512], AF.Sigmoid, bias=bx_sb[:, i:i + 1]), reads=[pi[1], cb], writes=[b_fw[1]])
                    S.op("act", lambda e, i=i: e.activation(aa[:], aa[:], AF.Exp, scale=lamc[:, i:i + 1]), reads=[b_fw[2], cb], writes=[b_fw[2]])
                    S.op("dve", lambda e: e.tensor_tensor(bb[:], aa[:], aa[:], ALU.mult), reads=[b_fw[2]], writes=[b_fw[3]])
                    S.op("dve", lambda e: e.tensor_scalar(bb[:], bb[:], -1.0, 1.0, ALU.mult, ALU.add), reads=[b_fw[3]], writes=[b_fw[3]])
                    S.op("act", lambda e: e.activation(bb[:], bb[:], AF.Sqrt), reads=[b_fw[3]], writes=[b_fw[3]])
                    S.op("dve", lambda e: e.tensor_tensor(ig[:], ig[:], xc[:], ALU.mult), reads=[b_fw[1], b_fw[0]], writes=[b_fw[1]])
                    S.op("dve", lambda e: e.tensor_tensor(bb[:], bb[:], ig[:], ALU.mult), reads=[b_fw[3], b_fw[1]], writes=[b_fw[3]])
                    S.op("dve", lambda e, i=i: e.tensor_tensor_scan(xc[:], aa[:], bb[:], carry[:, i:i + 1], ALU.mult, ALU.add),
                         reads=[b_fw[2], b_fw[3], b_carry], writes=[b_fw[0]])
                    S.op("dve", lambda e, i=i: e.tensor_copy(carry[:, i:i + 1], xc[:, 511:512]), reads=[b_fw[0]], writes=[b_carry])
                    S.op("dve", lambda e, i=i: e.tensor_tensor(yT[:, 2 + i, :], xc[:], gateT[:, i, :], ALU.mult), reads=[b_fw[0], b_gate], writes=[b_yT])

                S.skipping = 'c' in SKIP
                phs = [nextbank(), nextbank()]
                for g in range(2):
                    ph = phs[g]
                    for kv, roll in ((0, None), (1, None)):
                        col = kv * 32
                        for l in range(32):
                            S.op("pe", lambda e, ph=ph, kv=kv, g=g, l=l, roll=roll, col=col: e.matmul(
                                ph[0][:, col:col + 32], w1_sb[:, kv * 32 + l, :], rz[:, kv * 2 + g, l:l + 16 * 31 + 1:16],
                                start=(kv == 0 and l == 0), stop=(kv == 1 and l == 31), skip_group_check=True), reads=[b_cmpw, b_roll], writes=[ph[1]])
                    for kv in range(2):
                        S.op("act", lambda e, ph=ph, kv=kv, g=g: e.activation(hid[:, kv * 2 + g, 32:64], ph[0][:, kv * 32:kv * 32 + 32],
                                                                              AF.Gelu_apprx_tanh, bias=b1c[:, kv:kv + 1]), reads=[ph[1], b_cmpw], writes=[b_hid])
                S.op("dve", lambda e: e.tensor_copy(rz[:, :, 0:16], rz[:, :, 512:528]), reads=[b_roll], writes=[b_roll])
                pk = nextbank()
                for g in range(2):
                    S.op("pe", lambda e, pk=pk, g=g: e.matmul(pk[0][:, 0:32], w2k_pad[:, g, :], hid[:, g, 32:64], start=(g == 0), stop=(g == 1)),
                         reads=[b_cmpw, b_hid], writes=[pk[1]])
                S.op("act", lambda e, pk=pk, T=T: e.activation(kcmpT[:, 32 * T:32 * T + 32], pk[0][:, 0:32], AF.Identity, bias=b2k[:, 0:1]),
                     reads=[pk[1], b_cmpw], writes=[b_kcmp])
                a4 = T % 4
                stt = T // 4
                S.op("dve", lambda e: e.memset(hidv[:], 0.0), reads=[b_hidv], writes=[b_hidv])
                S.op("dve", lambda e, a4=a4: e.tensor_copy(hidv[:, :, 32 * a4:32 * a4 + 32], hid[:, 2:4, 32:64]), reads=[b_hid, b_hidv], writes=[b_hidv])
                pvv = nextbank()
                for g in range(2):
                    S.op("pe", lambda e, pvv=pvv, g=g: e.matmul(pvv[0][:, g * 64:(g + 1) * 64], hidv[:, g, :], w2v_sb[:], start=(g == 0), stop=False,
                                                                  skip_group_check=True), reads=[b_cmpw, b_hidv], writes=[pvv[1]])
                    S.op("pe", lambda e, pvv=pvv, g=g, a4=a4: e.matmul(pvv[0][:, g * 64:(g + 1) * 64], onesp[0:1, a4, :], b2v_row[0:1, :], start=False, stop=(g == 1),
                                                                      skip_group_check=True), reads=[cb], writes=[pvv[1]])
                for g in range(2):
                    S.op("dve", lambda e, pvv=pvv, g=g, stt=stt: e.tensor_tensor(cv_aug[:, stt, g, 64:128], cv_aug[:, stt, g, 64:128], pvv[0][:, g * 64:(g + 1) * 64], ALU.add),
                         reads=[pvv[1], b_cv], writes=[b_cv])

                S.skipping = 'n' in SKIP
                nslot = 32 * (T + 1)
                stiles = [(0, min(nslot, 128))] + ([(1, nslot - 128)] if nslot > 128 else [])
                for g in range(2):
                    base = 64 * g
                    bs_ = slice(base, base + 64)
                    for r in range(4):
                        h = 4 * g + r
                        ets = []
                        for (stt_, M) in stiles:
                            pb = nextbank()
                            S.op("pe", lambda e, pb=pb, stt_=stt_, M=M, r=r, g=g: e.matmul(pb[0][0:M, 0:512], kcmpT[:, stt_ * 128:stt_ * 128 + M], qz[:, 4 * g + r, :],
                                                                                             start=True, stop=True), reads=[b_kcmp, b_qT], writes=[pb[1]])
                            bi = lgc[0] % 2; lgc[0] += 1
                            S.dma("sp", bct[bi][0:M, :], D["bc"][h, stt_ * 128:stt_ * 128 + M, t0:t0 + 512], writes=[b_bct[bi]])
                            S.op("dve", lambda e, pb=pb, bi=bi, M=M: e.scalar_tensor_tensor(out=lg[bi][0:M, :], in0=pb[0][0:M, 0:512], scalar=0.125, in1=bct[bi][0:M, :],
                                                                                            op0=ALU.mult, op1=ALU.add), reads=[pb[1], b_bct[bi]], writes=[b_lg[bi]])
                            pi_ = ptc[0] % 3; ptc[0] += 1
                            S.op("act", lambda e, bi=bi, pi_=pi_, M=M: e.activation(PT[pi_][0:M, :], lg[bi][0:M, :], AF.Exp), reads=[b_lg[bi]], writes=[b_PT[pi_]])
                            ets.append((pi_, stt_, M))
                        cbk = [PSB[5], PSB[6]]
                        for s in range(4):
                            bk = cbk[s // 2]
                            co = (s % 2) * 129
                            for j, (pi_, stt_, M) in enumerate(ets):
                                S.op("pe", lambda e, bk=bk, co=co, pi_=pi_, stt_=stt_, M=M, s=s, j=j, g=g, ets=ets: e.matmul(
                                    bk[0][:, co:co + 129], PT[pi_][0:M, s * 128:(s + 1) * 128], cv_aug[0:M, stt_, g, :],
                                    start=(s % 2 == 0 and j == 0), stop=(s % 2 == 1 and j == len(ets) - 1), skip_group_check=True),
                                    reads=[b_PT[pi_], b_cv], writes=[bk[1]])
                        for kb in range(2):
                            bk = cbk[kb]
                            zb = bk[0][:, 0:258].rearrange("p (s c) -> p s c", s=2)[:, :, 128:129]
                            zo = z4[:, 2 * kb:2 * kb + 2].rearrange("p (s c) -> p s c", c=1)
                            S.op("dve", lambda e, zb=zb, zo=zo: e.tensor_scalar(zo, zb, 1e-30, None, ALU.max), reads=[bk[1]], writes=[b_z4])
                        S.op("dve", lambda e: e.reciprocal(z4[:, 0:4], z4[:, 0:4]), reads=[b_z4], writes=[b_z4])
                        for s in range(4):
                            bk = cbk[s // 2]
                            co = (s % 2) * 129
                            if r == 0:
                                S.op("dve", lambda e, bk=bk, co=co, s=s: e.tensor_scalar(impS[:, s, :], bk[0][:, co:co + 64], z4[:, s:s + 1], None, ALU.mult),
                                     reads=[bk[1], b_z4], writes=[b_imp])
                            else:
                                S.op("dve", lambda e, bk=bk, co=co, s=s: e.scalar_tensor_tensor(out=impS[:, s, :], in0=bk[0][:, co:co + 64], scalar=z4[:, s:s + 1], in1=impS[:, s, :],
                                                                                                op0=ALU.mult, op1=ALU.add), reads=[bk[1], b_z4, b_imp], writes=[b_imp])
                            S.op("dve", lambda e, bk=bk, co=co, s=s, h=h: e.tensor_scalar(ycomb[:, s, h * 64:(h + 1) * 64], bk[0][:, co + 64:co + 128], z4[:, s:s + 1], sg[:, s, h:h + 1],
                                                                                          ALU.mult, ALU.mult), reads=[bk[1], b_z4, b_sg], writes=[b_ycs[s]])
                    for s in range(4):
                        itile = T * 4 + s
                        off = 63 - 2 * itile
                        S.op("dve", lambda e, s=s, off=off: e.tensor_tensor(sc[:], impS[:, s, :], tmax_sb[:, off:off + 64], ALU.max), reads=[b_imp, cb], writes=[b_sc])
                        S.op("dve", lambda e, off=off: e.tensor_tensor(sc[:], sc[:], tmin_sb[:, off:off + 64], ALU.min), reads=[b_sc, cb], writes=[b_sc])
                        S.op("dve", lambda e: e.memset(sc[:, 0:1], 1e4), reads=[b_sc], writes=[b_sc])
                        S.op("dve", lambda e: e.max(out=m8[:, 0:8], in_=sc[:]), reads=[b_sc], writes=[b_sc])
                        S.op("dve", lambda e: e.match_replace(out=sc2[:], in_to_replace=m8[:, 0:8], in_values=sc[:], imm_value=-3.0e38), reads=[b_sc], writes=[b_sc])
                        S.op("dve", lambda e: e.max(out=m8[:, 8:16], in_=sc2[:]), reads=[b_sc], writes=[b_sc])
                        S.op("dve", lambda e, g=g: e.tensor_scalar(nmp[g][:, 64 * (1 - g):64 * (1 - g) + 64], sc[:], m8[:, 15:16], NEGM, ALU.is_lt, ALU.mult), reads=[b_sc], writes=[b_sc])
                        S.op("pe", lambda e, g=g: e.transpose(pt[:, 0:128], nmp[g][:], identb[:]), reads=[b_sc, b_const], writes=[b_pt])
                        S.op("act", lambda e, g=g, s=s: e.activation(nmz[g][:, s * 128:(s + 1) * 128], pt[:, 0:128], AF.Identity), reads=[b_pt], writes=[b_nmT])
                    wjobs, sjobs = [], []

                    def mk_hook(r_, g=g):
                        def mask_hook():
                            oh = slice(64, 128) if g == 0 else slice(0, 64)
                            S.op("pool", lambda e: e.tensor_copy(qz[oh, 4 * g + r_, :], nmz[g][oh, :]), reads=[b_nmT, b_qT], writes=[b_qT])
                        return mask_hook
                    for r in range(4):
                        h = 4 * g + r
                        selb, winb = PSB[3], PSB[4]
                        st_ = {}

                        def mk_sel(kt, gsc, b_gsc, h=h, g=g, selb=selb, first=False, load=False, hook=None, post=None):
                            Dd = t0 - 128 * kt
                            f0 = max(0, -Dd)
                            J = {}

                            def A():
                                if hook is not None:
                                    hook()
                                if load:
                                    S.dma("pool", gsc[:], D["gs"][h], writes=[b_gsc])
                                pb = nextbank()
                                J["pb"] = pb
                                S.op("pe", lambda e: e.matmul(pb[0][:, f0:512], ksE[g][:, kt * 128:(kt + 1) * 128], qz[:, h, f0:512], start=True, stop=True),
                                     reads=[b_ksT, b_qT], writes=[pb[1]])

                            def B():
                                pb = J["pb"]
                                pi_ = ptc[0] % 3; ptc[0] += 1
                                J["pi"] = pi_
                                if Dd <= 896:
                                    bi = lgc[0] % 2; lgc[0] += 1
                                    x0 = Dd + 384
                                    S.op("dve", lambda e: e.scalar_tensor_tensor(out=lg[bi][:, f0:512], in0=pb[0][:, f0:512], scalar=0.125,
                                                                                 in1=gsc[:, x0 + f0:x0 + 512], op0=ALU.mult, op1=ALU.add),
                                         reads=[pb[1], b_gsc], writes=[b_lg[bi]])
                                    S.op("act", lambda e: e.activation(PT[pi_][:, f0:512], lg[bi][:, f0:512], AF.Exp), reads=[b_lg[bi]], writes=[b_PT[pi_]])
                                else:
                                    S.op("act", lambda e: e.activation(PT[pi_][:, 0:512], pb[0][:, 0:512], AF.Exp, bias=c31_sb[:, h:h + 1], scale=0.125),
                                         reads=[pb[1], cb], writes=[b_PT[pi_]])

                            def C():
                                pi_ = J["pi"]
                                for s in range(f0 // 128, 4):
                                    S.op("pe", lambda e, s=s: e.matmul(selb[0][:, s * 65:(s + 1) * 65], PT[pi_][:, s * 128:(s + 1) * 128], vs_aug[:, kt, g, :],
                                                                        start=(first and s == 0), stop=False, skip_group_check=True),
                                         reads=[b_PT[pi_], b_vs], writes=[selb[1]])
                            return (A, B, C, post)

                        def mk_win(kt, gsc, b_gsc, h=h, g=g, winb=winb, first=False, post=None, load=False):
                            Dd = t0 - 128 * kt
                            lo = max(0, -Dd)
                            hi = min(512, 639 - Dd)
                            J = {}

                            def A():
                                if load:
                                    S.dma("pool", gsc[:], D["gs"][h], writes=[b_gsc])
                                pb = nextbank()
                                J["pb"] = pb
                                S.op("pe", lambda e: e.matmul(pb[0][:, lo:hi], kwT[:, kt * 128:(kt + 1) * 128], qz[:, h, lo:hi], start=True, stop=True),
                                     reads=[b_kwT, b_qT], writes=[pb[1]])

                            def B():
                                pb = J["pb"]
                                bi = lgc[0] % 2; lgc[0] += 1
                                x0 = Dd + 384
                                S.op("dve", lambda e: e.scalar_tensor_tensor(out=lg[bi][:, lo:hi], in0=pb[0][:, lo:hi], scalar=0.125,
                                                                             in1=gsc[:, x0 + lo:x0 + hi], op0=ALU.mult, op1=ALU.add),
                                     reads=[pb[1], b_gsc], writes=[b_lg[bi]])
                                if Dd >= 128:
                                    S.op("dve", lambda e: e.tensor_tensor(lg[bi][:, lo:hi], lg[bi][:, lo:hi], m512_sb[:, Dd - 128 + lo:Dd - 128 + hi], ALU.add),
                                         reads=[b_lg[bi], cb], writes=[b_lg[bi]])
                                pi_ = ptc[0] % 3; ptc[0] += 1
                                J["pi"] = pi_
                                S.op("act", lambda e: e.activation(PT[pi_][:, lo:hi], lg[bi][:, lo:hi], AF.Exp), reads=[b_lg[bi]], writes=[b_PT[pi_]])

                            def C():
                                pi_ = J["pi"]
                                fw_ = first
                                for s in range(4):
                                    a_ = max(lo, 128 * s)
                                    b__ = min(hi, 128 * s + 128)
                                    if b__ <= a_:
                                        continue
                                    S.op("pe", lambda e, s=s, a_=a_, b__=b__, fw_=fw_: e.matmul(winb[0][a_ - 128 * s:b__ - 128 * s, s * 65:(s + 1) * 65], PT[pi_][:, a_:b__],
                                                                                               vw_aug[:, kt, g, :], start=fw_, stop=False, skip_group_check=True),
                                         reads=[b_PT[pi_], b_vw], writes=[winb[1]])
                                    fw_ = False
                            return (A, B, C, post)

                        def mk_post(which_, h=h, selb=selb, winb=winb):
                            def post():
                                if dbg and T < 2:
                                    for bi_, bk in ((0, selb), (1, winb)) if False else [((0, selb), (1, winb))[which_]]:
                                        S.op("act", lambda e, bk=bk: e.activation(lg[0][:, 0:260], bk[0][:, 0:260], AF.Identity), reads=[bk[1]], writes=[b_lg[0]])
                                        S.dma("sp", ydR[T, h, bi_], lg[0][:, 0:260], reads=[b_lg[0]])
                                for (bk, goff) in [((selb, 8), (winb, 16))[which_]]:
                                    zv = bk[0][:, 0:260].rearrange("p (s c) -> p s c", s=4)[:, :, 64:65]
                                    z3 = z4[:, 0:4].rearrange("p (s c) -> p s c", c=1)
                                    f3 = z4[:, 4:8].rearrange("p (s c) -> p s c", c=1)
                                    S.op("dve", lambda e, zv=zv, z3=z3: e.reciprocal(z3, zv), reads=[bk[1]], writes=[b_z4])
                                    S.op("dve", lambda e, z3=z3, f3=f3, goff=goff: e.tensor_tensor(f3, z3, sg[:, :, goff + h:goff + h + 1], ALU.mult), reads=[b_z4, b_sg], writes=[b_z4])
                                    for s in range(4):
                                        S.op("dve", lambda e, bk=bk, s=s: e.scalar_tensor_tensor(out=ycomb[:, s, h * 64:(h + 1) * 64], in0=bk[0][:, s * 65:s * 65 + 64], scalar=z4[:, 4 + s:5 + s],
                                                                                               in1=ycomb[:, s, h * 64:(h + 1) * 64], op0=ALU.mult, op1=ALU.add),
                                             reads=[bk[1], b_z4, b_ycs[s]], writes=[b_ycs[s]])
                            return post

                        kts = [4 * T] + [k for k in range(max(0, 4 * T - 4), 4 * T + 4) if k != 4 * T]
                        gi_ = gsr[0] % 2; gsr[0] += 1
                        for j_, kt in enumerate(kts):
                            wjobs.append(mk_win(kt, gs_cur[gi_], b_gs[gi_], first=(j_ == 0), load=(j_ == 0), post=(mk_post(1) if j_ == len(kts) - 1 else None)))
                        for kt in range(0, 4 * T + 4):
                            wjobs.append(mk_sel(kt, gs_cur[gi_], b_gs[gi_], first=(kt == 0), load=False,
                                                hook=(mk_hook(r) if kt == 0 else None), post=(mk_post(0) if kt == 4 * T + 3 else None)))
                    jobs = wjobs
                    nj = len(jobs)
                    for i_ in range(nj + 2):
                        if i_ < nj:
                            jobs[i_][0]()
                        if 0 <= i_ - 1 < nj:
                            jobs[i_ - 1][1]()
                        if 0 <= i_ - 2 < nj:
                            jobs[i_ - 2][2]()
                            if jobs[i_ - 2][3] is not None:
                                jobs[i_ - 2][3]()

                S.skipping = False
                if dbg and L == stop_after[0]:
                    for s in range(4):
                        tok = T * 4 + s
                        S.dma("pool", ydA[tok * 128:(tok + 1) * 128, :], ytok[:, s, :], reads=[b_ytok[s]])
                        S.dma("sp", ydC[tok * 128:(tok + 1) * 128, :], ycomb[:, s, :], reads=[b_ycs[s]])
                    for i in range(2):
                        S.dma("pool", ydB[i * 128:(i + 1) * 128, t0:t0 + 512], yT[:, 2 + i, :], reads=[b_yT])
                for s in range(4):
                    for c in (0, 1):
                        S.op("pe", lambda e, s=s, c=c: e.transpose(pt[:, c * 128:(c + 1) * 128], ytok[:, s, c * 128:(c + 1) * 128], identb[:]), reads=[b_ytok[s], b_const], writes=[b_pt])
                    S.op("act", lambda e, s=s: e.activation(yT[:, 0:2, s * 128:(s + 1) * 128], pt[:, 0:256].rearrange("p (c n) -> p c n", c=2), AF.Identity), reads=[b_pt], writes=[b_yT])
                    pf = nextbank()
                    for c in range(4):
                        S.op("pe", lambda e, s=s, c=c, pf=pf: e.transpose(pf[0][:, c * 128:(c + 1) * 128], ycomb[:, s, c * 128:(c + 1) * 128], identf[:]), reads=[b_ycs[s], cb], writes=[pf[1]])
                    S.op("act", lambda e, s=s, pf=pf: e.activation(yT[:, 4:8, s * 128:(s + 1) * 128], pf[0][:, 0:512].rearrange("p (c n) -> p c n", c=4), AF.Identity), reads=[pf[1]], writes=[b_yT])
                for s in range(4):
                    tok = T * 4 + s
                    banks = [PSB[5], PSB[6]]
                    for hf in range(2):
                        for c in range(8):
                            S.op("pe", lambda e, hf=hf, c=c, s=s: e.matmul(banks[hf][0][:, 0:512], yT[:, c, s * 128:(s + 1) * 128], wout_sb[:, c, hf * 512:(hf + 1) * 512],
                                                                           start=(c == 0), stop=(c == 7)), reads=[b_yT, b_wout], writes=[banks[hf][1]])
                    postnorm_residual(banks, src, dst, tok, hin[:, 0, :], b_hin[0], gpost, cb, junk, b_junk, small, b_small, ftmp, b_ftmp, True)

        seq = []
        cur = "x"
        for L in range(2):
            for ph in ("ffn1", "mix", "ffn2", "ple"):
                seq.append((L, ph))
        if stop_after is not None:
            seq = seq[:seq.index(stop_after) + 1]
        for i, (L, ph) in enumerate(seq):
            last = (i == len(seq) - 1)
            dst = "y" if last else ("hA" if cur != "hA" else "hB")
            if ph == "ffn1":
                ffn_phase(L, 0, cur, dst)
            elif ph == "ffn2":
                ffn_phase(L, 1, cur, dst)
            elif ph == "mix":
                mixer_phase(L, cur, dst)
            else:
                ple_phase(L, cur, dst)
            cur = dst
        S.finish()
        S.emit()
    return nc


_PROG = {}
LAST_RES = None


def prep_inputs(inputs, b):
    f = lambda a: np.ascontiguousarray(np.asarray(a, dtype=np.float32))
    c = host_constants()
    gs, bc, c31 = host_bias_tables(inputs["rel_bias"])
    m = {}
    m["x"] = f(inputs["x"][b])
    m["pT"] = f(np.transpose(np.asarray(inputs["p"])[:, b], (0, 2, 1)))
    m["ngB"] = f(np.broadcast_to(np.asarray(inputs["norm_g"])[:, :, None, :], (2, 8, 128, DM)))
    m["wg"] = f(inputs["ffn_w_gate"]); m["wu"] = f(inputs["ffn_w_up"]); m["wd"] = f(inputs["ffn_w_down"])
    m["w_in"] = f(inputs["w_in"]); m["w_out"] = f(inputs["w_out"])
    m["sguB"] = f(np.broadcast_to(np.asarray(inputs["sgu_norm_g"])[:, None, :], (2, 128, 256)))
    m["sgu_w"] = f(inputs["sgu_w"])
    m["sgu_bT"] = f(np.transpose(np.asarray(inputs["sgu_b"]), (0, 2, 1)))
    v2 = lambda a: f(np.transpose(np.asarray(a, np.float32).reshape(2, 2, 128), (0, 2, 1)))
    m["conv_wT"] = f(np.transpose(np.asarray(inputs["conv_w"], np.float32).reshape(2, 4, 2, 128), (0, 3, 2, 1)))
    m["conv_b"] = v2(inputs["conv_b"]); m["lru_wa"] = f(inputs["lru_wa"]); m["lru_ba"] = v2(inputs["lru_ba"])
    m["lru_wx"] = f(inputs["lru_wx"]); m["lru_bx"] = v2(inputs["lru_bx"]); m["lru_lam"] = v2(inputs["lru_lambda"])
    m["posT"] = f(np.transpose(np.asarray(inputs["cmp_pos"]), (0, 3, 1, 2)))
    m["cmp_w1"] = f(inputs["cmp_w1"])
    m["b1T"] = f(np.transpose(np.asarray(inputs["cmp_b1"]), (0, 2, 1)))
    m["cmp_w2"] = f(inputs["cmp_w2"])
    b2 = np.asarray(inputs["cmp_b2"], np.float32)
    m["b2kk"] = f(np.concatenate([b2[:, 0], b2[:, 0]], axis=1)[:, :, None])
    m["b2vB"] = f(np.broadcast_to(b2[:, 1][:, None, :], (2, 128, 64)))
    m["wpg"] = f(inputs["ple_w_gate"]); m["wpp"] = f(inputs["ple_w_proj"])
    for k in ("ident", "tril", "eall", "ovl", "m512", "tmax", "tmin"):
        m[k] = c[k]
    m["gs"] = gs; m["bc"] = bc; m["c31"] = c31
    return m


def kernel(**inputs):
    key = STOP_AFTER
    if key not in _PROG:
        _PROG[key] = build(STOP_AFTER)
    nc = _PROG[key]
    shared = prep_inputs(inputs, 0)
    in_maps = []
    for b in range(8):
        m = dict(shared)
        m["x"] = np.ascontiguousarray(np.asarray(inputs["x"][b], dtype=np.float32))
        m["pT"] = np.ascontiguousarray(np.transpose(np.asarray(inputs["p"])[:, b], (0, 2, 1)).astype(np.float32))
        in_maps.append(m)
    ncore = int(os.environ.get("KCORES", "8"))
    res = run_bass_kernel_spmd(nc, in_maps[:ncore], core_ids=list(range(ncore)))
    global LAST_RES
    LAST_RES = res.results
    outs = [np.asarray(r["y"], dtype=np.float32) for r in res.results]
    while len(outs) < 8:
        outs.append(np.zeros_like(outs[0]))
    return np.stack(outs, axis=0)
```
